# Optimizing a Trainium2 kernel written in Bass

```python
import math
import jax, jax.numpy as jnp
from jax import lax
import numpy as np

D_MODEL = 4096
BATCH = 4
SEQ = 4096
DEPTH = 1
DEC_BATCH = 4
DEC_SEQ = 2048
PAST_LEN = 128

PLE_DIM = 256
HEAD_DIM = 128
N_Q_HEADS = D_MODEL // 256
N_KV_HEADS = N_Q_HEADS // 4
Q_PER_KV = N_Q_HEADS // N_KV_HEADS
ATTN_WIDTH = N_Q_HEADS * HEAD_DIM
KV_WIDTH = N_KV_HEADS * HEAD_DIM
WINDOW = 128
BLOCK = 128
N_BUCKETS = 32
MAX_DISTANCE = 128
HYENA_WIDTH = D_MODEL // 2
FILTER_BANDS = 16
FILTER_EMB = 2 * FILTER_BANDS + 1
FILTER_HIDDEN = 64
SHORT_CONV = 3
D_FF = 11008
EPS = 1e-6
NEG_INF = -1e30
Q_END = ATTN_WIDTH
K_END = Q_END + KV_WIDTH
V_END = K_END + KV_WIDTH
HY_END = V_END + 3 * HYENA_WIDTH
IN_COLS = HY_END + 2 * D_MODEL

kernel_name = 'hybrid_swa_hyena_encoder'


def rmsnorm(x, g):
    x32 = x.astype(jnp.float32)
    y = x32 * lax.rsqrt(jnp.mean(x32 * x32, axis=-1, keepdims=True) + EPS)
    return (y * g.astype(jnp.float32)).astype(x.dtype)


def dwconv3(x, w, b):
    xp = jnp.pad(x, ((0, 0), (1, 1), (0, 0)))
    return xp[:, :-2] * w[0] + xp[:, 1:-1] * w[1] + xp[:, 2:] * w[2] + b


def t5_bucket(rel):
    half = N_BUCKETS // 2
    max_exact = half // 2
    ret = jnp.where(rel > 0, half, 0)
    n = jnp.abs(rel)
    nf = jnp.maximum(n, 1).astype(jnp.float32)
    large = max_exact + (jnp.log(nf / max_exact) / math.log(MAX_DISTANCE / max_exact) * (half - max_exact)).astype(jnp.int32)
    large = jnp.minimum(large, half - 1)
    return ret + jnp.where(n < max_exact, n, large)


def windowed_gqa(q, k, v, rel_bias, sink):
    B, L = q.shape[:2]
    nb = L // BLOCK
    qb = q.reshape(B, nb, BLOCK, N_KV_HEADS, Q_PER_KV, HEAD_DIM)

    def band(t):
        tp = jnp.pad(t, ((0, 0), (BLOCK, BLOCK), (0, 0), (0, 0))).reshape(B, nb + 2, BLOCK, N_KV_HEADS, HEAD_DIM)
        return jnp.concatenate([tp[:, :-2], tp[:, 1:-1], tp[:, 2:]], axis=2)

    kb, vb = band(k), band(v)
    s = jnp.einsum('bnqkgd,bnskd->bnkgqs', qb, kb, preferred_element_type=jnp.float32) * (HEAD_DIM ** -0.5)
    qi = jnp.arange(BLOCK)[:, None]
    sj = jnp.arange(3 * BLOCK)[None, :]
    rel = sj - BLOCK - qi
    bias = jnp.transpose(rel_bias[t5_bucket(rel)], (2, 0, 1)).astype(jnp.float32)
    bias = bias.reshape(N_KV_HEADS, Q_PER_KV, BLOCK, 3 * BLOCK)
    kpos = jnp.arange(nb)[:, None] * BLOCK - BLOCK + jnp.arange(3 * BLOCK)[None, :]
    valid = (jnp.abs(rel) <= WINDOW)[None] & ((kpos >= 0) & (kpos < L))[:, None, :]
    s = jnp.where(valid[None, :, None, None], s + bias, NEG_INF)
    sink_l = sink.astype(jnp.float32).reshape(N_KV_HEADS, Q_PER_KV)[:, :, None, None]
    m = jnp.maximum(jnp.max(s, axis=-1, keepdims=True), sink_l)
    pr = jnp.exp(s - m)
    denom = jnp.sum(pr, axis=-1, keepdims=True) + jnp.exp(sink_l - m)
    o = jnp.einsum('bnkgqs,bnskd->bnqkgd', (pr / denom).astype(v.dtype), vb)
    return o.reshape(B, L, ATTN_WIDTH)


def hyena_filters(L, w1, b1, f1, w2, b2, f2, w3, decay):
    f32 = jnp.float32
    pos = jnp.arange(L, dtype=f32)
    t = pos / (L - 1)
    bands = jnp.linspace(1e-4, FILTER_BANDS - 1, FILTER_BANDS, dtype=f32)
    ang = (2.0 * math.pi / L) * pos[:, None] * bands[None, :]
    z = jnp.concatenate([t[:, None], jnp.cos(ang), -jnp.sin(ang)], axis=-1)
    h = jnp.sin(f1.astype(f32) * (z @ w1.astype(f32) + b1.astype(f32)))
    h = jnp.sin(f2.astype(f32) * (h @ w2.astype(f32) + b2.astype(f32)))
    h = (h @ w3.astype(f32)).reshape(L, 2, HYENA_WIDTH)
    return h * jnp.exp(-t[:, None, None] * decay.astype(f32)[None])


def bidir_long_conv(u, h_fwd, h_bwd):
    L, C = h_fwd.shape
    k = jnp.concatenate([h_fwd, jnp.zeros((1, C), jnp.float32), h_bwd[:0:-1]], axis=0)
    kf = jnp.fft.rfft(k, axis=0)
    uf = jnp.fft.rfft(u.astype(jnp.float32), n=2 * L, axis=1)
    return jnp.fft.irfft(uf * kf[None], n=2 * L, axis=1)[:, :L]


def encoder_layer(x, p, rel_bias, g_mix, w_in, attn_sink, hy_short_w, hy_short_b,
                  hy_filt_w1, hy_filt_b1, hy_filt_f1, hy_filt_w2, hy_filt_b2, hy_filt_f2,
                  hy_filt_w3, hy_decay, hy_skip, w_attn_o, w_hyena_o, w_out,
                  g_ffn, w_up, ffn_conv_w, ffn_conv_b, w_down, g_ple, w_ple_gate, w_ple):
    B, L, _ = x.shape
    h = rmsnorm(x, g_mix)
    z = h @ w_in
    q, k, v, hy, gate_logits = jnp.split(z, [Q_END, K_END, V_END, HY_END], axis=-1)
    attn = windowed_gqa(q.reshape(B, L, N_Q_HEADS, HEAD_DIM),
                        k.reshape(B, L, N_KV_HEADS, HEAD_DIM),
                        v.reshape(B, L, N_KV_HEADS, HEAD_DIM), rel_bias, attn_sink)
    hy = dwconv3(hy, hy_short_w, hy_short_b)
    hv, hx1, hx0 = jnp.split(hy, 3, axis=-1)
    filt = hyena_filters(L, hy_filt_w1, hy_filt_b1, hy_filt_f1, hy_filt_w2, hy_filt_b2,
                         hy_filt_f2, hy_filt_w3, hy_decay)
    u = hv * hx1
    hy_out = (bidir_long_conv(u, filt[:, 0], filt[:, 1])
              + u.astype(jnp.float32) * hy_skip.astype(jnp.float32)).astype(x.dtype) * hx0
    gates = jax.nn.sigmoid(gate_logits)
    g_attn, g_hy = jnp.split(gates, 2, axis=-1)
    mixed = (g_attn * (attn @ w_attn_o) + g_hy * (hy_out @ w_hyena_o)) @ w_out
    x = x + mixed
    h = rmsnorm(x, g_ffn)
    gu, val = jnp.split(h @ w_up, 2, axis=-1)
    gu = dwconv3(gu, ffn_conv_w, ffn_conv_b)
    x = x + (jax.nn.gelu(gu, approximate=False) * val) @ w_down
    h = rmsnorm(x, g_ple)
    x = x + jax.nn.sigmoid(h @ w_ple_gate) * (p @ w_ple)
    return x


def trunk(x, p, rel_bias, layer_weights, g_final):
    for l in range(DEPTH):
        x = encoder_layer(x, p[l], rel_bias, *[w[l] for w in layer_weights])
    return rmsnorm(x, g_final)


def setup_inputs(seed: int = 0) -> dict:
    key = jax.random.key(seed)
    ks = jax.random.split(key, 32)
    f32 = jnp.float32

    def nrm(k, shape, scale):
        return jax.random.normal(k, shape, f32) * scale

    def gain(k, shape):
        return 1.0 + 0.02 * jax.random.normal(k, shape, f32)

    return {
        'x_prompt': nrm(ks[0], (BATCH, SEQ, D_MODEL), 1.0),
        'x_sample': nrm(ks[1], (DEC_BATCH, DEC_SEQ, D_MODEL), 1.0),
        'p_prompt': nrm(ks[2], (DEPTH, BATCH, SEQ, PLE_DIM), 1.0),
        'p_sample': nrm(ks[3], (DEPTH, DEC_BATCH, DEC_SEQ, PLE_DIM), 1.0),
        'rel_bias': nrm(ks[4], (N_BUCKETS, N_Q_HEADS), 0.5),
        'g_mix': gain(ks[5], (DEPTH, D_MODEL)),
        'w_in': nrm(ks[6], (DEPTH, D_MODEL, IN_COLS), D_MODEL ** -0.5),
        'attn_sink': nrm(ks[7], (DEPTH, N_Q_HEADS), 0.5),
        'hy_short_w': nrm(ks[8], (DEPTH, SHORT_CONV, 3 * HYENA_WIDTH), SHORT_CONV ** -0.5),
        'hy_short_b': nrm(ks[9], (DEPTH, 3 * HYENA_WIDTH), 0.02),
        'hy_filt_w1': nrm(ks[10], (DEPTH, FILTER_EMB, FILTER_HIDDEN), FILTER_EMB ** -0.5),
        'hy_filt_b1': nrm(ks[11], (DEPTH, FILTER_HIDDEN), 0.02),
        'hy_filt_f1': gain(ks[12], (DEPTH, FILTER_HIDDEN)),
        'hy_filt_w2': nrm(ks[13], (DEPTH, FILTER_HIDDEN, FILTER_HIDDEN), FILTER_HIDDEN ** -0.5),
        'hy_filt_b2': nrm(ks[14], (DEPTH, FILTER_HIDDEN), 0.02),
        'hy_filt_f2': gain(ks[15], (DEPTH, FILTER_HIDDEN)),
        'hy_filt_w3': nrm(ks[16], (DEPTH, FILTER_HIDDEN, 2 * HYENA_WIDTH), 0.03 * FILTER_HIDDEN ** -0.5),
        'hy_decay': jax.random.uniform(ks[17], (DEPTH, 2, HYENA_WIDTH), f32, 3.0, 15.0),
        'hy_skip': nrm(ks[18], (DEPTH, HYENA_WIDTH), 0.5),
        'w_attn_o': nrm(ks[19], (DEPTH, ATTN_WIDTH, D_MODEL), ATTN_WIDTH ** -0.5),
        'w_hyena_o': nrm(ks[20], (DEPTH, HYENA_WIDTH, D_MODEL), HYENA_WIDTH ** -0.5),
        'w_out': nrm(ks[21], (DEPTH, D_MODEL, D_MODEL), D_MODEL ** -0.5),
        'g_ffn': gain(ks[22], (DEPTH, D_MODEL)),
        'w_up': nrm(ks[23], (DEPTH, D_MODEL, 2 * D_FF), D_MODEL ** -0.5),
        'ffn_conv_w': nrm(ks[24], (DEPTH, SHORT_CONV, D_FF), SHORT_CONV ** -0.5),
        'ffn_conv_b': nrm(ks[25], (DEPTH, D_FF), 0.02),
        'w_down': nrm(ks[26], (DEPTH, D_FF, D_MODEL), D_FF ** -0.5),
        'g_ple': gain(ks[27], (DEPTH, D_MODEL)),
        'w_ple_gate': nrm(ks[28], (DEPTH, D_MODEL, D_MODEL), D_MODEL ** -0.5),
        'w_ple': nrm(ks[29], (DEPTH, PLE_DIM, D_MODEL), PLE_DIM ** -0.5),
        'g_final': gain(ks[30], (D_MODEL,)),
    }


def reference(x_prompt, x_sample, p_prompt, p_sample, rel_bias, g_mix, w_in, attn_sink,
              hy_short_w, hy_short_b, hy_filt_w1, hy_filt_b1, hy_filt_f1, hy_filt_w2,
              hy_filt_b2, hy_filt_f2, hy_filt_w3, hy_decay, hy_skip, w_attn_o, w_hyena_o,
              w_out, g_ffn, w_up, ffn_conv_w, ffn_conv_b, w_down, g_ple, w_ple_gate, w_ple,
              g_final):
    layer_weights = (g_mix, w_in, attn_sink, hy_short_w, hy_short_b, hy_filt_w1, hy_filt_b1,
                     hy_filt_f1, hy_filt_w2, hy_filt_b2, hy_filt_f2, hy_filt_w3, hy_decay,
                     hy_skip, w_attn_o, w_hyena_o, w_out, g_ffn, w_up, ffn_conv_w, ffn_conv_b,
                     w_down, g_ple, w_ple_gate, w_ple)
    y_prompt = trunk(x_prompt, p_prompt, rel_bias, layer_weights, g_final)
    y_sample = trunk(x_sample, p_sample, rel_bias, layer_weights, g_final)
    return (y_prompt, y_sample)
```

```python
import math
from contextlib import ExitStack
import numpy as np
import ml_dtypes
import concourse.bass as bass
import concourse.mybir as mybir
from concourse.bass_utils import run_bass_kernel_spmd

F32 = mybir.dt.float32
BF16 = mybir.dt.bfloat16
AF = mybir.ActivationFunctionType
ALU = mybir.AluOpType

D = 4096
LT = 4096
NT = 8
TW = 514
DFF = 11008
NFF = 86
IN_COLS = 17408
EPS = 1e-6
NEG = -1e30
ENGS = ['sync', 'scalar', 'vector', 'gpsimd', 'tensor']


class Buf:
    __slots__ = ('name', 'w', 'r', 'track')

    def __init__(self, name, track=True):
        self.name = name
        self.w = {}
        self.r = {}
        self.track = track


class Prog:
    def __init__(self, nc, es):
        self.nc = nc
        self.es = es
        self.ops = {e: [] for e in ENGS}
        self.waited = {e: {} for e in ENGS}
        self.esem = {}
        self.ecnt = {}
        for e in ['scalar', 'vector', 'tensor', 'gpsimd']:
            self.esem[e] = es.enter_context(nc.semaphore('es_' + e))
            self.ecnt[e] = 0
        self.dpool = {}
        self.dnext = {}
        for q, n in [('sync', 14), ('gpsimd', 8)]:
            self.dpool[q] = [[es.enter_context(nc.semaphore(f'd_{q}{i}')), 0] for i in range(n)]
            self.dnext[q] = 0
        self.sid = {}

    def _sid(self, h):
        return id(h)

    def _need(self, eng, reads, writes):
        need = {}
        own = self._sid(self.esem[eng]) if eng in self.esem else None

        def add(d, skip_own):
            for sid, (h, v) in d.items():
                if skip_own and sid == own:
                    continue
                if sid not in need or need[sid][1] < v:
                    need[sid] = (h, v)
        for b in reads:
            if b.track:
                add(b.w, False)
        for b in writes:
            if b.track:
                add(b.r, True)
                add(b.w, True)
        wd = self.waited[eng]
        for sid, (h, v) in need.items():
            if wd.get(sid, 0) < v:
                wd[sid] = v
                self.ops[eng].append(lambda e, h=h, v=v: e.wait_ge(h, v))

    def _mark(self, reads, writes, h, v):
        sid = self._sid(h)
        for b in reads:
            if b.track:
                b.r[sid] = (h, v)
        for b in writes:
            if b.track:
                b.w = {sid: (h, v)}
                b.r = {}

    def op(self, eng, fns, reads=(), writes=()):
        if not isinstance(fns, (list, tuple)):
            fns = [fns]
        self._need(eng, reads, writes)
        self.ecnt[eng] += 1
        v = self.ecnt[eng]
        h = self.esem[eng]
        for f in fns[:-1]:
            self.ops[eng].append(f)
        last = fns[-1]
        self.ops[eng].append(lambda e, last=last, h=h: last(e).then_inc(h, 1))
        self._mark(reads, writes, h, v)

    def dma(self, q, out, in_, reads=(), writes=(), **kw):
        self._need(q, reads, writes)
        pool = self.dpool[q]
        i = self.dnext[q]
        self.dnext[q] = (i + 1) % len(pool)
        h, cnt = pool[i]
        wd = self.waited[q]
        sid = self._sid(h)
        if cnt > 0 and wd.get(sid, 0) < cnt:
            wd[sid] = cnt
            self.ops[q].append(lambda e, h=h, v=cnt: e.wait_ge(h, v))
        cnt += 16
        pool[i][1] = cnt
        self.ops[q].append(lambda e, out=out, in_=in_, h=h, kw=kw: e.dma_start(out=out, in_=in_, **kw).then_inc(h, 16))
        self._mark(reads, writes, h, cnt)

    def barrier(self):
        deps = []
        for e2, h in self.esem.items():
            if self.ecnt[e2] > 0:
                deps.append((h, self.ecnt[e2]))
        for q in self.dpool:
            for h, cnt in self.dpool[q]:
                if cnt > 0:
                    deps.append((h, cnt))
        for eng in ENGS:
            wd = self.waited[eng]
            for h, v in deps:
                sid = self._sid(h)
                if wd.get(sid, 0) < v:
                    wd[sid] = v
                    self.ops[eng].append(lambda e, h=h, v=v: e.wait_ge(h, v))

    def finish(self):
        for q in self.dpool:
            for h, cnt in self.dpool[q]:
                if cnt > 0:
                    self.ops[q].append(lambda e, h=h, v=cnt: e.wait_ge(h, v))
        for q in ['sync']:
            for e2 in self.esem:
                if self.ecnt[e2] > 0:
                    self.ops[q].append(lambda e, h=self.esem[e2], v=self.ecnt[e2]: e.wait_ge(h, v))
            for h, cnt in self.dpool['gpsimd']:
                if cnt > 0:
                    self.ops[q].append(lambda e, h=h, v=cnt: e.wait_ge(h, v))


def build(phases=('p1', 'p2', 'p3', 'p4', 'p5', 'p6'), dbg=()):
    nc = bass.Bass("TRN2", target_bir_lowering=False)

    def din(name, shape, dt=F32):
        return nc.dram_tensor(name, list(shape), dt, kind="ExternalInput").ap()

    def dscr(name, shape, dt=F32):
        kind = "ExternalOutput" if name in dbg else "Internal"
        return nc.dram_tensor(name, list(shape), dt, kind=kind).ap()

    need = set(phases)
    xT = din("xT", [D, LT + 2])
    valid = din("valid", [128, LT + 2])
    gains = din("gains", [128, 4, 32])
    identb = din("identb", [128, 128], BF16)
    w_in = din("w_in", [D, IN_COLS]) if need & {'p1', 'p4'} else None
    hcw = din("hcw", [128, 48, 4]) if 'p1' in need else None
    kmask = din("kmask", [128, 32]) if 'p3' in need else None
    sinkr = din("sinkr", [128, 16]) if 'p3' in need else None
    rbext = din("rbext", [33, 16]) if 'p3' in need else None
    bkt = din("bkt", [33, 768]) if 'p3' in need else None
    w_ao = din("w_attn_o", [2048, D]) if 'p4' in need else None
    w_ho = din("w_hyena_o", [2048, D]) if 'p4' in need else None
    w_out = din("w_out", [D, D]) if 'p4' in need else None
    hskip = din("hskip", [128, 16]) if 'p4' in need else None
    if 'p2' in need:
        zfT = din("zfT", [33, LT]); tnb = din("tnb", [128, LT])
        fw1 = din("fw1", [33, 64]); fw2 = din("fw2", [64, 64]); fw3 = din("fw3", [64, 4096])
        fpar = din("fpar", [64, 4]); decay = din("decay", [128, 32])
        fh_t = din("fh_t", [32, 99], BF16); g_t = din("g_t", [128, 33 * 256], BF16)
        gi_t = din("gi_t", [128, 256], BF16); mi_t = din("mi_t", [128, 4096], BF16)
    w_up = din("w_up", [D, 2 * DFF]) if 'p5' in need else None
    w_down = din("w_down", [DFF, D]) if 'p5' in need else None
    fcw = din("fcw", [128, NFF, 4]) if 'p5' in need else None
    w_pg = din("w_ple_gate", [D, D]) if 'p6' in need else None
    w_ple = din("w_ple", [256, D]) if 'p6' in need else None
    pT = din("pT", [256, LT]) if 'p6' in need else None
    yT = nc.dram_tensor("yT", [D, LT], F32, kind="ExternalOutput").ap()

    qT = dscr("qT", [2048, LT], BF16)
    kT = dscr("kT", [512, LT], BF16)
    vS = dscr("vS", [LT, 512], BF16)
    uT = dscr("uT", [2048, LT])
    hx0T = dscr("hx0T", [2048, LT])
    ycT = dscr("ycT", [2048, LT])
    attnT = dscr("attnT", [2048, LT], BF16)
    x1T = dscr("x1T", [D, LT + 2])
    x2T = dscr("x2T", [D, LT + 2])
    hfb = dscr("hfb", [4096, LT], BF16)

    B_qT, B_kT, B_vS, B_uT, B_hx0T, B_ycT, B_attnT, B_x1T, B_x2T = [Buf(n, track=False) for n in ('qT','kT','vS','uT','hx0T','ycT','attnT','x1T','x2T')]
    with ExitStack() as es:
        E = es.enter_context
        P = Prog(nc, es)

        def sb(name, shape, dt):
            return E(nc.sbuf_tensor(name, list(shape), dt))

        xin = [sb(f"xin{i}", [128, 2, TW], F32) for i in range(2)]
        B_xin = [Buf(f"xin{i}") for i in range(2)]
        sq = [sb(f"sq{i}", [128, 2, TW], BF16) for i in range(2)]
        B_sq = [Buf(f"sq{i}") for i in range(2)]
        hg = sb("hg", [128, 32, TW], BF16)
        B_hg = Buf("hg")
        rt = sb("rt", [128, TW], F32)
        B_rt = Buf("rt")
        rstd = sb("rstd", [128, TW], F32)
        B_rstd = Buf("rstd")
        NWB = 4
        wb = [sb(f"wb{i}", [128, 32, 128], BF16) for i in range(NWB)]
        B_wb = [Buf(f"wb{i}") for i in range(NWB)]
        NOB = 3
        obf = [sb(f"obf{i}", [128, 512], BF16) for i in range(NOB)]
        B_obf = [Buf(f"obf{i}") for i in range(NOB)]
        of32 = [sb(f"of{i}", [128, 512], F32) for i in range(NOB)]
        B_of32 = [Buf(f"of{i}") for i in range(NOB)]
        zt = [sb(f"zt{i}", [128, TW], F32) for i in range(2)]
        B_zt = [Buf(f"zt{i}") for i in range(2)]
        cva = [sb(f"cva{i}", [128, 512], F32) for i in range(2)]
        B_cva = [Buf(f"cva{i}") for i in range(2)]
        cvb = [sb(f"cvb{i}", [128, 512], F32) for i in range(2)]
        B_cvb = [Buf(f"cvb{i}") for i in range(2)]
        hvc = sb("hvc", [128, 512], F32)
        B_hvc = Buf("hvc")
        vmask = sb("vmask", [128, TW], F32)
        B_vmask = Buf("vmask")
        big = sb("big", [128, NFF * 512], BF16)
        B_big = Buf("big")
        ones_b = sb("ones_b", [128, 128], BF16)
        ident = sb("ident", [128, 128], BF16)
        eps_t = sb("eps_t", [128, 1], F32)
        gains_s = sb("gains_s", [128, 4, 32], F32)
        hcw_s = sb("hcw_s", [128, 48, 4], F32)
        fcw_s = sb("fcw_s", [128, NFF, 4], F32)
        hskip_s = sb("hskip_s", [128, 16], F32)
        B_const = Buf("const")

        psm = [E(nc.psum_tensor(f"psm{i}", [128, 512], F32)) for i in range(3)]
        B_psm = [Buf(f"psm{i}") for i in range(3)]
        psh = E(nc.psum_tensor("psh", [128, 512], F32))
        B_psh = [Buf(f"psh{i}") for i in range(8)]
        pss = E(nc.psum_tensor("pss", [128, 512], F32))
        B_pss = Buf("pss")
        pss2 = E(nc.psum_tensor("pss2", [128, 512], F32))
        B_pss2 = Buf("pss2")
        psx = E(nc.psum_tensor("psx", [128, 512], F32))
        B_psx = Buf("psx")
        pst = E(nc.psum_tensor("pst", [128, 1024], BF16))
        B_pst = Buf("pst")

        block = E(nc.Block())

        P.op('vector', lambda e: e.memset(ones_b[:], 1.0), writes=[B_const])
        P.op('vector', lambda e: e.memset(eps_t[:], EPS), writes=[B_const])
        P.dma('sync', ident[:], identb, writes=[B_const])
        P.dma('sync', gains_s[:], gains, writes=[B_const])
        if hcw is not None:
            P.dma('sync', hcw_s[:], hcw, writes=[B_const])
        if fcw is not None:
            P.dma('sync', fcw_s[:], fcw, writes=[B_const])
        if hskip is not None:
            P.dma('sync', hskip_s[:], hskip, writes=[B_const])

        cnt = {'pm': 0, 'ph': 0, 'ob': 0, 'of': 0, 'zt': 0, 'cv': 0, 'wb': 0}

        def rot(key, n):
            v = cnt[key] % n
            cnt[key] += 1
            return v

        def prep(src, s, gi):
            for qd in range(16):
                sl = qd % 2
                P.dma('sync', xin[sl][:], src[qd * 256:(qd + 1) * 256, s:s + TW].rearrange("(kc p) t -> p kc t", p=128),
                      writes=[B_xin[sl]])
                P.op('scalar', lambda e, sl=sl: e.activation(out=sq[sl][:], in_=xin[sl][:], func=AF.Square),
                     reads=[B_xin[sl]], writes=[B_sq[sl]])
                fns = []
                for j in range(2):
                    kc = qd * 2 + j
                    fns.append(lambda e, sl=sl, j=j, kc=kc: e.tensor_scalar(
                        out=hg[:, kc, :], in0=xin[sl][:, j, :], scalar1=gains_s[:, gi, kc:kc + 1], scalar2=None,
                        op0=ALU.mult))
                P.op('vector', fns, reads=[B_xin[sl], B_const], writes=[B_hg])
                fns = []
                for j in range(2):
                    kc = qd * 2 + j
                    fns.append(lambda e, sl=sl, j=j, kc=kc: e.matmul(
                        pss[:, :], lhsT=ones_b[:], rhs=sq[sl][:, j, 1:513], start=(kc == 0), stop=(kc == 31)))
                    fns.append(lambda e, sl=sl, j=j, kc=kc: e.matmul(
                        pss2[:, 0:2], lhsT=ones_b[:], rhs=sq[sl][:, j, 0:TW:513], start=(kc == 0), stop=(kc == 31)))
                P.op('tensor', fns, reads=[B_sq[sl], B_const], writes=[B_pss, B_pss2])
            P.op('scalar', [
                lambda e: e.activation(out=rt[:, 1:513], in_=pss[:, :], func=AF.Sqrt, bias=eps_t[:, 0:1], scale=1.0 / D),
                lambda e: e.activation(out=rt[:, 0:TW:513], in_=pss2[:, 0:2], func=AF.Sqrt, bias=eps_t[:, 0:1],
                                       scale=1.0 / D)],
                 reads=[B_pss, B_pss2, B_const], writes=[B_rt])
            P.op('vector', lambda e: e.reciprocal(out=rstd[:], in_=rt[:]), reads=[B_rt], writes=[B_rstd])

        def wload(src_rows, nkc):
            sl = rot('wb', NWB)
            P.dma('gpsimd', wb[sl][:, 0:nkc, :], src_rows.rearrange("(kc p) c -> p kc c", p=128), writes=[B_wb[sl]])
            return sl

        def mm_group(ps_ap, B_ps, pieces, rhs_fn, extra_reads, n_total=None):
            fns = []
            tot = sum(p[1] for p in pieces)
            i = 0
            rd = list(extra_reads)
            for (sl, nkc, kb) in pieces:
                rd.append(B_wb[sl])
                for k in range(nkc):
                    fns.append(lambda e, sl=sl, k=k, kb=kb, i=i: e.matmul(
                        ps_ap, lhsT=wb[sl][:, k, :], rhs=rhs_fn(kb + k), start=(i == 0), stop=(i == tot - 1)))
                    i += 1
            P.op('tensor', fns, reads=rd, writes=[B_ps])

        def phase1():
            chunks = [('q', h * 128, h) for h in range(16)]
            chunks += [('k', 2048 + h * 128, h) for h in range(4)]
            chunks += [('v', 2560 + h * 128, h) for h in range(4)]
            for c in range(16):
                chunks += [('hv', 3072 + c * 128, c), ('hx1', 5120 + c * 128, 16 + c), ('hx0', 7168 + c * 128, 32 + c)]
            SC = 128.0 ** -0.5
            for ti in range(NT):
                s = ti * 512
                prep(xT, s, 0)
                P.dma('sync', vmask[:], valid[:, s:s + TW], writes=[B_vmask])
                pend = [wload(w_in[:, chunks[i][1]:chunks[i][1] + 128], 32) for i in range(2)]
                for n, (kind, col0, idx) in enumerate(chunks):
                    if n + 2 < len(chunks):
                        c2 = chunks[n + 2][1]
                        pend.append(wload(w_in[:, c2:c2 + 128], 32))
                    sl = pend.pop(0)
                    pm = rot('pm', 3)
                    mm_group(psm[pm][:, :], B_psm[pm], [(sl, 32, 0)], lambda kc: hg[:, kc, 1:513], [B_hg])
                    if kind in ('q', 'k', 'v'):
                        o = rot('ob', NOB)
                        if kind == 'q':
                            P.op('vector', lambda e, o=o, pm=pm: e.scalar_tensor_tensor(
                                out=obf[o][:], in0=psm[pm][:, :], scalar=SC, in1=rstd[:, 1:513], op0=ALU.mult,
                                op1=ALU.mult), reads=[B_psm[pm], B_rstd], writes=[B_obf[o]])
                            P.dma('sync', qT[idx * 128:(idx + 1) * 128, s:s + 512], obf[o][:], reads=[B_obf[o]])
                        elif kind == 'k':
                            P.op('vector', lambda e, o=o, pm=pm: e.tensor_tensor(
                                out=obf[o][:], in0=psm[pm][:, :], in1=rstd[:, 1:513], op=ALU.mult),
                                reads=[B_psm[pm], B_rstd], writes=[B_obf[o]])
                            P.dma('sync', kT[idx * 128:(idx + 1) * 128, s:s + 512], obf[o][:], reads=[B_obf[o]])
                        else:
                            P.op('vector', lambda e, o=o, pm=pm: e.tensor_tensor(
                                out=obf[o][:], in0=psm[pm][:, :], in1=rstd[:, 1:513], op=ALU.mult),
                                reads=[B_psm[pm], B_rstd], writes=[B_obf[o]])
                            fns = [lambda e, o=o, b=b: e.transpose(pst[:, b * 128:(b + 1) * 128],
                                                                   obf[o][:, b * 128:(b + 1) * 128], ident[:])
                                   for b in range(4)]
                            P.op('tensor', fns, reads=[B_obf[o], B_const], writes=[B_pst])
                            o2 = rot('ob', NOB)
                            P.op('scalar', lambda e, o2=o2: e.copy(out=obf[o2][:], in_=pst[:, 0:512]),
                                 reads=[B_pst], writes=[B_obf[o2]])
                            P.dma('sync', vS[s:s + 512, idx * 128:(idx + 1) * 128].rearrange("(b p) d -> p b d", p=128),
                                  obf[o2][:].rearrange("p (b d) -> p b d", b=4), reads=[B_obf[o2]])
                        continue
                    ph = rot('ph', 8)
                    mm_group(psh[:, ph * 4:ph * 4 + 2], B_psh[ph], [(sl, 32, 0)], lambda kc: hg[:, kc, 0:TW:513], [B_hg])
                    z = rot('zt', 2)
                    P.op('vector', [
                        lambda e, z=z, pm=pm: e.tensor_tensor(out=zt[z][:, 1:513], in0=psm[pm][:, :], in1=rstd[:, 1:513],
                                                             op=ALU.mult),
                        lambda e, z=z, ph=ph: e.tensor_tensor(out=zt[z][:, 0:TW:513], in0=psh[:, ph * 4:ph * 4 + 2],
                                                             in1=rstd[:, 0:TW:513], op=ALU.mult)],
                         reads=[B_psm[pm], B_psh[ph], B_rstd], writes=[B_zt[z]])
                    cv = rot('cv', 2)
                    P.op('vector', lambda e, z=z, cv=cv, idx=idx: e.tensor_scalar(
                        out=cva[cv][:], in0=zt[z][:, 0:512], scalar1=hcw_s[:, idx, 0:1], scalar2=hcw_s[:, idx, 3:4],
                        op0=ALU.mult, op1=ALU.add), reads=[B_zt[z], B_const], writes=[B_cva[cv]])
                    P.op('vector', lambda e, z=z, cv=cv, idx=idx: e.scalar_tensor_tensor(
                        out=cvb[cv][:], in0=zt[z][:, 1:513], scalar=hcw_s[:, idx, 1:2], in1=cva[cv][:], op0=ALU.mult,
                        op1=ALU.add), reads=[B_zt[z], B_cva[cv], B_const], writes=[B_cvb[cv]])
                    if kind == 'hv':
                        P.op('vector', lambda e, z=z, cv=cv, idx=idx: e.scalar_tensor_tensor(
                            out=hvc[:], in0=zt[z][:, 2:514], scalar=hcw_s[:, idx, 2:3], in1=cvb[cv][:], op0=ALU.mult,
                            op1=ALU.add), reads=[B_zt[z], B_cvb[cv], B_const], writes=[B_hvc])
                    elif kind == 'hx1':
                        P.op('vector', lambda e, z=z, cv=cv, idx=idx: e.scalar_tensor_tensor(
                            out=cva[cv][:], in0=zt[z][:, 2:514], scalar=hcw_s[:, idx, 2:3], in1=cvb[cv][:], op0=ALU.mult,
                            op1=ALU.add), reads=[B_zt[z], B_cvb[cv], B_const], writes=[B_cva[cv]])
                        P.op('vector', lambda e, cv=cv: e.tensor_tensor(
                            out=cvb[cv][:], in0=cva[cv][:], in1=hvc[:], op=ALU.mult),
                            reads=[B_cva[cv], B_hvc], writes=[B_cvb[cv]])
                        o = rot('of', NOB)
                        P.op('vector', lambda e, cv=cv, o=o: e.tensor_tensor(
                            out=of32[o][:], in0=cvb[cv][:], in1=vmask[:, 1:513], op=ALU.mult),
                            reads=[B_cvb[cv], B_vmask], writes=[B_of32[o]])
                        c = idx - 16
                        P.dma('sync', uT[c * 128:(c + 1) * 128, s:s + 512], of32[o][:], reads=[B_of32[o]])
                    else:
                        o = rot('of', NOB)
                        P.op('vector', lambda e, z=z, cv=cv, idx=idx, o=o: e.scalar_tensor_tensor(
                            out=of32[o][:], in0=zt[z][:, 2:514], scalar=hcw_s[:, idx, 2:3], in1=cvb[cv][:], op0=ALU.mult,
                            op1=ALU.add), reads=[B_zt[z], B_cvb[cv], B_const], writes=[B_of32[o]])
                        c = idx - 32
                        P.dma('sync', hx0T[c * 128:(c + 1) * 128, s:s + 512], of32[o][:], reads=[B_of32[o]])


        att_o = [sb(f"att_o{i}", [128, 512], BF16) for i in range(2)]
        B_att_o = [Buf(f"att_o{i}") for i in range(2)]
        biasT = big[:, 24576:30720].rearrange("p (a t) -> p a t", a=12)
        bkt_s = big[0:33, 30720:32256].bitcast(F32)
        rb_s = sb("rb_s", [33, 16], F32)
        esink = sb("esink", [128, 16], F32)
        kmask_s = sb("kmask_s", [128, 32], F32)
        B_c3 = Buf("c3")
        pb_s = sb("pb_s", [128, 2, 512], BF16)
        B_pb = Buf("pb")
        zero_c = sb("zero_c", [128, 32], F32)
        P.op('vector', lambda e: e.memset(zero_c[:], 0.0), writes=[B_const])
        accs = [(psm[0], B_psm[0]), (psm[1], B_psm[1]), (psm[2], B_psm[2]), (psx, B_psx)]
        cnt['acc'] = 0
        cnt['ao'] = 0

        def acc():
            return accs[rot('acc', 4)]

        def phase2():
            PI = math.pi
            hgf = hg[:].rearrange("p a b -> p (a b)")
            w3_s = hgf[0:64, 0:8192].bitcast(F32)
            w1_s = hgf[0:33, 8192:8320].bitcast(F32)
            w2_s = hgf[0:64, 8320:8448].bitcast(F32)
            fp_s = hgf[0:64, 8448:8464].bitcast(F32)
            nd_s = hgf[:, 8464:8528].bitcast(F32)
            tnb_s = big[:, 0:8192].bitcast(F32)
            zf_s = big[0:33, 8192:16384].bitcast(F32)
            B_f = Buf("filt")
            P.dma('sync', w3_s, fw3, writes=[B_f])
            P.dma('sync', w1_s, fw1, writes=[B_f])
            P.dma('sync', w2_s, fw2, writes=[B_f])
            P.dma('sync', fp_s[:, 0:4], fpar, writes=[B_f])
            P.dma('sync', nd_s, decay, writes=[B_f])
            P.dma('sync', tnb_s, tnb, writes=[B_f])
            P.dma('sync', zf_s, zfT, writes=[B_f])
            P.op('vector', [
                lambda e: e.tensor_tensor(out=fp_s[:, 4:5], in0=fp_s[:, 0:1], in1=fp_s[:, 1:2], op=ALU.mult),
                lambda e: e.tensor_tensor(out=fp_s[:, 5:6], in0=fp_s[:, 2:3], in1=fp_s[:, 3:4], op=ALU.mult),
                lambda e: e.tensor_scalar(out=nd_s, in0=nd_s, scalar1=-1.0, scalar2=None, op0=ALU.mult)],
                 reads=[B_f], writes=[B_f])
            for tt in range(8):
                tsl = slice(tt * 512, (tt + 1) * 512)
                gprev = None
                for layer in range(2):
                    pa, Ba = acc()
                    if layer == 0:
                        P.op('tensor', lambda e, pa=pa, tsl=tsl: e.matmul(pa[0:64, :], lhsT=w1_s, rhs=zf_s[:, tsl],
                                                                         start=True, stop=True), reads=[B_f], writes=[Ba])
                    else:
                        P.op('tensor', lambda e, pa=pa, gp=gprev: e.matmul(pa[0:64, :], lhsT=w2_s, rhs=cva[gp][0:64, :],
                                                                          start=True, stop=True),
                             reads=[B_f, B_cva[gprev]], writes=[Ba])
                    cv = rot('cv', 2)
                    fi, fbi = (1, 4) if layer == 0 else (3, 5)
                    P.op('vector', lambda e, cv=cv, pa=pa, fi=fi, fbi=fbi: e.tensor_scalar(
                        out=cvb[cv][0:64, :], in0=pa[0:64, :], scalar1=fp_s[:, fi:fi + 1], scalar2=fp_s[:, fbi:fbi + 1],
                        op0=ALU.mult, op1=ALU.add), reads=[Ba, B_f], writes=[B_cvb[cv]])
                    z = rot('zt', 2)
                    P.op('vector', [
                        lambda e, cv=cv, z=z: e.tensor_scalar(out=zt[z][0:64, 0:512], in0=cvb[cv][0:64, :], scalar1=PI,
                                                             scalar2=-2 * PI, op0=ALU.is_gt, op1=ALU.mult),
                        lambda e, cv=cv: e.tensor_scalar(out=cva[cv][0:64, :], in0=cvb[cv][0:64, :], scalar1=-PI,
                                                        scalar2=2 * PI, op0=ALU.is_lt, op1=ALU.mult)],
                         reads=[B_cvb[cv]], writes=[B_zt[z], B_cva[cv]])
                    P.op('vector', lambda e, cv=cv, z=z: e.tensor_tensor(out=zt[z][0:64, 0:512], in0=zt[z][0:64, 0:512],
                                                                        in1=cva[cv][0:64, :], op=ALU.add),
                         reads=[B_zt[z], B_cva[cv]], writes=[B_zt[z]])
                    P.op('vector', lambda e, cv=cv, z=z: e.tensor_tensor(out=cva[cv][0:64, :], in0=cvb[cv][0:64, :],
                                                                        in1=zt[z][0:64, 0:512], op=ALU.add),
                         reads=[B_zt[z], B_cvb[cv]], writes=[B_cva[cv]])
                    P.op('scalar', lambda e, cv=cv: e.activation(out=cva[cv][0:64, :], in_=cva[cv][0:64, :], func=AF.Sin),
                         reads=[B_cva[cv]], writes=[B_cva[cv]])
                    gprev = cv
                for k in range(32):
                    pa, Ba = acc()
                    P.op('tensor', lambda e, pa=pa, k=k, gp=gprev: e.matmul(
                        pa[:, :], lhsT=w3_s[:, k * 128:(k + 1) * 128], rhs=cva[gp][0:64, :], start=True, stop=True),
                        reads=[B_f, B_cva[gprev]], writes=[Ba])
                    z = rot('zt', 2)
                    P.op('scalar', lambda e, z=z, k=k, tsl=tsl: e.activation(
                        out=zt[z][:, 0:512], in_=tnb_s[:, tsl], func=AF.Exp, scale=nd_s[:, k:k + 1]),
                        reads=[B_f], writes=[B_zt[z]])
                    o = rot('ob', NOB)
                    fns = [lambda e, o=o, pa=pa, z=z: e.tensor_tensor(out=obf[o][:], in0=pa[:, :], in1=zt[z][:, 0:512],
                                                                      op=ALU.mult)]
                    if k >= 16 and tt == 0:
                        fns.append(lambda e, o=o: e.memset(obf[o][:, 0:1], 0.0))
                    P.op('vector', fns, reads=[Ba, B_zt[z]], writes=[B_obf[o]])
                    P.dma('sync', hfb[k * 128:(k + 1) * 128, tsl], obf[o][:], reads=[B_obf[o]])
            P.barrier()
            NFA = 33
            A_sb = big[:, 0:9504].rearrange("p (r c) -> p r c", c=96)
            slab = [big[0:32, 9504 + i * 4096:9504 + (i + 1) * 4096].rearrange("p (c t) -> p c t", t=128) for i in range(3)]
            B_slab = [Buf(f"slab{i}") for i in range(3)]
            Ya = big[:, 21792:25888].rearrange("p (c r) -> p c r", r=128)
            Yb = big[:, 25888:29984].rearrange("p (c r) -> p c r", r=128)
            y_sb = big[0:32, 29984:38176].bitcast(F32).rearrange("p (c t) -> p c t", t=128)
            D_sb = hgf[:, 0:4096].rearrange("p (t c) -> p t c", c=32)
            G_s = hgf[:, 4096:4096 + 8448].rearrange("p (a s f) -> p a s f", s=2, f=128)
            MI_s = wb[0][:].rearrange("p a b -> p (a b)").rearrange("p (t h) -> p t h", h=32)
            wb1 = wb[1][:].rearrange("p a b -> p (a b)")
            GI_s = wb1[:, 0:256].rearrange("p (s t) -> p s t", s=2)
            FH_s = wb1[0:32, 256:355]
            B_A, B_Y, B_D, B_ysb, B_tab = Buf("A"), Buf("Y"), Buf("D"), Buf("ysb"), Buf("tab")
            P.dma('sync', G_s, g_t.rearrange("p (a s f) -> p a s f", s=2, f=128), writes=[B_tab])
            P.dma('sync', MI_s, mi_t.rearrange("p (t h) -> p t h", h=32), writes=[B_tab])
            P.dma('sync', GI_s, gi_t.rearrange("p (s t) -> p s t", s=2), writes=[B_tab])
            P.dma('sync', FH_s, fh_t, writes=[B_tab])
            P.op('vector', [lambda e: e.memset(Ya, 0.0), lambda e: e.memset(Yb, 0.0)], writes=[B_Y])
            for g in range(64):
                c0 = g * 32
                P.dma('gpsimd', slab[0], uT[c0:c0 + 32, :].rearrange("c (th tl) -> th c tl", tl=128), writes=[B_slab[0]])
                P.dma('sync', slab[1], hfb[c0:c0 + 32, :].rearrange("c (th tl) -> th c tl", tl=128), writes=[B_slab[1]])
                P.dma('sync', slab[2], hfb[2048 + c0:2048 + c0 + 32, :].rearrange("c (th tl) -> th c tl", tl=128),
                      writes=[B_slab[2]])
                for si in range(3):
                    for cb in range(8):
                        pa, Ba = acc()
                        fns = [lambda e, pa=pa, si=si, cb=cb, k=k: e.matmul(
                            pa[:, k * 128:k * 128 + 99], lhsT=slab[si][:, cb * 4 + k, :], rhs=FH_s, start=True, stop=True)
                            for k in range(4)]
                        P.op('tensor', fns, reads=[B_slab[si], B_tab], writes=[Ba])
                        col = si * 32 + cb * 4
                        P.op('scalar', lambda e, pa=pa, col=col: e.copy(
                            out=A_sb[:, :, col:col + 4].rearrange("p r k -> p k r"),
                            in_=pa[:, :].rearrange("p (k r) -> p k r", r=128)[:, :, 0:99]),
                            reads=[Ba], writes=[B_A])
                for fa0 in range(0, NFA, 2):
                    nf = min(2, NFA - fa0)
                    pa, Ba = acc()
                    fns = []
                    for fl in range(nf):
                        fa = fa0 + fl
                        b0 = fl * 192
                        fns.append(lambda e, pa=pa, fa=fa, b0=b0: e.matmul(pa[:, b0:b0 + 96], lhsT=G_s[:, fa, 0, :],
                                                                           rhs=A_sb[:, fa, :], start=True, stop=False))
                        fns.append(lambda e, pa=pa, fa=fa, b0=b0: e.matmul(pa[:, b0:b0 + 96], lhsT=G_s[:, fa, 1, :],
                                                                           rhs=A_sb[:, 66 + fa, :], start=False, stop=True))
                        fns.append(lambda e, pa=pa, fa=fa, b0=b0: e.matmul(pa[:, b0 + 96:b0 + 192], lhsT=G_s[:, fa, 1, :],
                                                                           rhs=A_sb[:, fa, :], start=True, stop=False))
                        fns.append(lambda e, pa=pa, fa=fa, b0=b0: e.matmul(pa[:, b0 + 96:b0 + 192], lhsT=G_s[:, fa, 0, :],
                                                                           rhs=A_sb[:, 33 + fa, :], start=False, stop=True))
                    P.op('tensor', fns, reads=[B_A, B_tab], writes=[Ba])
                    z = rot('zt', 2)
                    P.op('scalar', lambda e, z=z, pa=pa, nf=nf: e.copy(out=zt[z][:, 0:nf * 192], in_=pa[:, 0:nf * 192]),
                         reads=[Ba], writes=[B_zt[z]])
                    X = zt[z][:, 0:nf * 192].rearrange("p (f r s c) -> p f r s c", r=2, s=3, c=32)
                    cv = rot('cv', 2)
                    T = cva[cv][:, 0:384].rearrange("p (k f c) -> p k f c", k=6, c=32)[:, :, 0:nf, :]
                    U = cvb[cv][:, 0:128].rearrange("p (k f c) -> p k f c", k=2, c=32)[:, :, 0:nf, :]
                    yav = Ya[:, :, fa0:fa0 + nf].rearrange("p c f -> p f c")
                    yai = Ya[:, :, 64 + fa0:64 + fa0 + nf].rearrange("p c f -> p f c")
                    ybv = Yb[:, :, fa0:fa0 + nf].rearrange("p c f -> p f c")
                    ybi = Yb[:, :, 64 + fa0:64 + fa0 + nf].rearrange("p c f -> p f c")
                    P.op('vector', [
                        lambda e, X=X, U=U: e.tensor_tensor(out=U[:, 0], in0=X[:, :, 0, 1, :], in1=X[:, :, 0, 2, :], op=ALU.add),
                        lambda e, X=X, U=U: e.tensor_tensor(out=U[:, 1], in0=X[:, :, 1, 1, :], in1=X[:, :, 1, 2, :],
                                                           op=ALU.subtract)],
                         reads=[B_zt[z]], writes=[B_cvb[cv]])
                    P.op('vector', [
                        lambda e, X=X, U=U, T=T: e.tensor_tensor(out=T[:, 0], in0=X[:, :, 0, 0, :], in1=U[:, 0], op=ALU.mult),
                        lambda e, X=X, U=U, T=T: e.tensor_tensor(out=T[:, 1], in0=X[:, :, 1, 0, :], in1=U[:, 1], op=ALU.mult),
                        lambda e, X=X, U=U, T=T: e.tensor_tensor(out=T[:, 2], in0=X[:, :, 0, 0, :], in1=U[:, 1], op=ALU.mult),
                        lambda e, X=X, U=U, T=T: e.tensor_tensor(out=T[:, 3], in0=X[:, :, 1, 0, :], in1=U[:, 0], op=ALU.mult)],
                         reads=[B_zt[z], B_cvb[cv]], writes=[B_cva[cv]])
                    P.op('vector', [
                        lambda e, T=T, yav=yav: e.tensor_tensor(out=yav, in0=T[:, 0], in1=T[:, 1], op=ALU.subtract),
                        lambda e, T=T, ybi=ybi: e.tensor_tensor(out=ybi, in0=T[:, 0], in1=T[:, 1], op=ALU.subtract),
                        lambda e, T=T, yai=yai: e.tensor_tensor(out=yai, in0=T[:, 2], in1=T[:, 3], op=ALU.add),
                        lambda e, T=T, ybv=ybv: e.scalar_tensor_tensor(out=ybv, in0=T[:, 2], scalar=-1.0, in1=T[:, 3],
                                                                      op0=ALU.mult, op1=ALU.subtract)],
                         reads=[B_cva[cv]], writes=[B_Y])
                for cb in range(8):
                    pa, Ba = acc()
                    fns = []
                    for k in range(4):
                        c = cb * 4 + k
                        fns.append(lambda e, pa=pa, c=c, k=k: e.matmul(pa[:, k * 128:(k + 1) * 128], lhsT=Ya[:, c, :],
                                                                       rhs=GI_s[:, 0, :], start=True, stop=False))
                        fns.append(lambda e, pa=pa, c=c, k=k: e.matmul(pa[:, k * 128:(k + 1) * 128], lhsT=Yb[:, c, :],
                                                                       rhs=GI_s[:, 1, :], start=False, stop=True))
                    P.op('tensor', fns, reads=[B_Y, B_tab], writes=[Ba])
                    P.op('scalar', lambda e, pa=pa, cb=cb: e.copy(
                        out=D_sb[:, :, cb * 4:cb * 4 + 4].rearrange("p t k -> p k t"),
                        in_=pa[:, :].rearrange("p (k t) -> p k t", t=128)), reads=[Ba], writes=[B_D])
                for tb in range(8):
                    pa, Ba = acc()
                    fns = [lambda e, pa=pa, tb=tb, tl=tl: e.matmul(
                        pa[0:32, tl * 32:(tl + 1) * 32], lhsT=MI_s[:, tb * 16 + tl, :], rhs=D_sb[:, tb * 16 + tl, :],
                        start=True, stop=True) for tl in range(16)]
                    P.op('tensor', fns, reads=[B_D, B_tab], writes=[Ba])
                    P.op('vector', lambda e, pa=pa, tb=tb: e.tensor_copy(
                        out=y_sb[:, :, tb * 16:(tb + 1) * 16].rearrange("p c t -> p t c"),
                        in_=pa[0:32, :].rearrange("p (t c) -> p t c", c=32)), reads=[Ba], writes=[B_ysb])
                P.dma('sync', ycT[c0:c0 + 32, :].rearrange("c (th tl) -> th c tl", tl=128), y_sb, reads=[B_ysb])

        def phase3():
            P.dma('sync', bkt_s[:], bkt, writes=[B_c3])
            P.dma('sync', rb_s[:], rbext, writes=[B_c3])
            P.dma('sync', esink[:], sinkr, writes=[B_c3])
            P.dma('sync', kmask_s[:], kmask, writes=[B_c3])
            P.op('scalar', lambda e: e.activation(out=esink[:], in_=esink[:], func=AF.Exp), reads=[B_c3], writes=[B_c3])
            for ri, r in enumerate((-1, 0, 1)):
                for hv in range(4):
                    fns = []
                    for qi in range(128):
                        st = 384 - qi - 128 * r
                        fns.append(lambda e, qi=qi, st=st, hv=hv: e.matmul(
                            psx[:, qi:512:128], lhsT=bkt_s[:, st:st + 128], rhs=rb_s[:, hv * 4:hv * 4 + 4],
                            start=True, stop=True))
                    P.op('tensor', fns, reads=[B_c3], writes=[B_psx])
                    P.op('scalar', lambda e, ri=ri, hv=hv: e.copy(out=biasT[:, ri * 4 + hv, :], in_=psx[:, :]),
                         reads=[B_psx], writes=[B_c3])
            q4 = big[:, 0:16384].rearrange("p (g t) -> p g t", g=4)
            kh = big[:, 16384:20480]
            vh = big[:, 20480:24576].rearrange("p (b d) -> p b d", b=32)
            for hv in range(4):
                P.dma('sync', q4, qT[hv * 512:(hv + 1) * 512, 0:LT].rearrange("(g p) t -> p g t", p=128),
                      reads=[B_qT], writes=[B_big])
                P.dma('sync', kh, kT[hv * 128:(hv + 1) * 128, 0:LT], reads=[B_kT], writes=[B_big])
                P.dma('sync', vh, vS[:, hv * 128:(hv + 1) * 128].rearrange("(b p) d -> p b d", p=128),
                      reads=[B_vS], writes=[B_big])
                for i in range(32):
                    js = [j for j in (i - 1, i, i + 1) if 0 <= j < 32]
                    par = i % 2
                    ps_o, B_o = (pss, B_pss) if par == 0 else (psx, B_psx)
                    ps_d, B_d = (pss2, B_pss2) if par == 0 else (psh, B_psh[0])
                    for jn, j in enumerate(js):
                        ri = (i - j) + 1
                        pm = rot('pm', 3)
                        P.op('tensor', [
                            lambda e, j=j, i=i, pm=pm: e.matmul(psm[pm][:, :], lhsT=kh[:, j * 128:(j + 1) * 128],
                                                                 rhs=q4[:, :, i * 128:(i + 1) * 128], start=True, stop=False),
                            lambda e, ri=ri, hv=hv, pm=pm: e.matmul(psm[pm][:, :], lhsT=ident[:], rhs=biasT[:, ri * 4 + hv, :],
                                                                    start=False, stop=True)],
                             reads=[B_big, B_c3, B_const], writes=[B_psm[pm]])
                        o = rot('ob', NOB)
                        P.op('scalar', lambda e, o=o, pm=pm, j=j: e.activation(
                            out=obf[o][:], in_=psm[pm][:, :], func=AF.Exp, bias=kmask_s[:, j:j + 1], scale=1.0),
                            reads=[B_psm[pm], B_c3], writes=[B_obf[o]])
                        P.op('tensor', [
                            lambda e, o=o, j=j, jn=jn, ps_o=ps_o: e.matmul(ps_o[:, :], lhsT=vh[:, j, :], rhs=obf[o][:],
                                                                           start=(jn == 0), stop=(jn == len(js) - 1)),
                            lambda e, o=o, jn=jn, ps_d=ps_d: e.matmul(ps_d[:, :], lhsT=ones_b[:], rhs=obf[o][:],
                                                                      start=(jn == 0), stop=(jn == len(js) - 1))],
                             reads=[B_obf[o], B_big, B_const], writes=[B_o, B_d])
                    cv = rot('cv', 2)
                    P.op('vector', [lambda e, g=g, cv=cv, ps_d=ps_d, hv=hv: e.tensor_scalar(
                        out=cva[cv][:, g * 128:(g + 1) * 128], in0=ps_d[:, g * 128:(g + 1) * 128],
                        scalar1=esink[:, hv * 4 + g:hv * 4 + g + 1], scalar2=None, op0=ALU.add) for g in range(4)],
                         reads=[B_d, B_c3], writes=[B_cva[cv]])
                    P.op('vector', lambda e, cv=cv: e.reciprocal(out=cvb[cv][:], in_=cva[cv][:]),
                         reads=[B_cva[cv]], writes=[B_cvb[cv]])
                    ao = rot('ao', 2)
                    P.op('vector', lambda e, cv=cv, ao=ao, ps_o=ps_o: e.tensor_tensor(
                        out=att_o[ao][:], in0=ps_o[:, :], in1=cvb[cv][:], op=ALU.mult),
                        reads=[B_o, B_cvb[cv]], writes=[B_att_o[ao]])
                    P.dma('sync', attnT[hv * 512:(hv + 1) * 512, i * 128:(i + 1) * 128].rearrange("(g d) q -> d g q", d=128),
                          att_o[ao][:].rearrange("d (g q) -> d g q", g=4), reads=[B_att_o[ao]], writes=[B_attnT])

        def phase4(use_yc):
            at_s = big[:, 0:8192].rearrange("p (c t) -> p c t", c=16)
            hy_s = big[:, 8192:16384].rearrange("p (c t) -> p c t", c=16)
            m_s = big[:, 16384:32768].rearrange("p (c t) -> p c t", c=32)
            B_at, B_hy, B_m = Buf("at"), Buf("hy"), Buf("m")
            for half in range(2):
                P.dma('sync', x1T[:, half * (LT + 1):half * (LT + 1) + 1].rearrange("(kc p) o -> p kc o", p=128),
                      zero_c[:].rearrange("p (k o) -> p k o", o=1), reads=[B_const], writes=[B_x1T], allow_slow_non_contiguous=True)
            for ti in range(NT):
                s = ti * 512
                prep(xT, s, 0)
                P.dma('sync', vmask[:], valid[:, s:s + TW], writes=[B_vmask])
                P.dma('sync', at_s, attnT[:, s:s + 512].rearrange("(c p) t -> p c t", p=128), reads=[B_attnT], writes=[B_at])
                for c in range(16):
                    z = rot('zt', 2)
                    cv = rot('cv', 2)
                    P.dma('sync', cva[cv][:], uT[c * 128:(c + 1) * 128, s:s + 512], reads=[B_uT], writes=[B_cva[cv]])
                    P.dma('sync', cvb[cv][:], hx0T[c * 128:(c + 1) * 128, s:s + 512], reads=[B_hx0T], writes=[B_cvb[cv]])
                    if use_yc:
                        P.dma('sync', zt[z][:, 0:512], ycT[c * 128:(c + 1) * 128, s:s + 512], reads=[B_ycT], writes=[B_zt[z]])
                        P.op('vector', lambda e, z=z, cv=cv, c=c: e.scalar_tensor_tensor(
                            out=zt[z][:, 0:512], in0=cva[cv][:], scalar=hskip_s[:, c:c + 1], in1=zt[z][:, 0:512],
                            op0=ALU.mult, op1=ALU.add), reads=[B_cva[cv], B_zt[z], B_const], writes=[B_zt[z]])
                    else:
                        P.op('vector', lambda e, z=z, cv=cv, c=c: e.tensor_scalar(
                            out=zt[z][:, 0:512], in0=cva[cv][:], scalar1=hskip_s[:, c:c + 1], scalar2=None,
                            op0=ALU.mult), reads=[B_cva[cv], B_const], writes=[B_zt[z]])
                    P.op('vector', lambda e, z=z, cv=cv, c=c: e.tensor_tensor(
                        out=hy_s[:, c, :], in0=zt[z][:, 0:512], in1=cvb[cv][:], op=ALU.mult),
                        reads=[B_zt[z], B_cvb[cv]], writes=[B_hy])
                for j in range(32):
                    c0 = j * 128
                    s_a = wload(w_ao[:, c0:c0 + 128], 16)
                    s_h = wload(w_ho[:, c0:c0 + 128], 16)
                    s_ga = wload(w_in[:, 9216 + c0:9216 + c0 + 128], 32)
                    s_gh = wload(w_in[:, 13312 + c0:13312 + c0 + 128], 32)
                    pa, Ba = acc()
                    mm_group(pa[:, :], Ba, [(s_a, 16, 0)], lambda kc: at_s[:, kc, :], [B_at])
                    ph_, Bh = acc()
                    mm_group(ph_[:, :], Bh, [(s_h, 16, 0)], lambda kc: hy_s[:, kc, :], [B_hy])
                    pga, Bga = acc()
                    mm_group(pga[:, :], Bga, [(s_ga, 32, 0)], lambda kc: hg[:, kc, 1:513], [B_hg])
                    pgh, Bgh = acc()
                    mm_group(pgh[:, :], Bgh, [(s_gh, 32, 0)], lambda kc: hg[:, kc, 1:513], [B_hg])
                    res = []
                    for (pg, Bg, pp, Bp) in ((pga, Bga, pa, Ba), (pgh, Bgh, ph_, Bh)):
                        z = rot('zt', 2)
                        P.op('vector', lambda e, z=z, pg=pg: e.tensor_tensor(
                            out=zt[z][:, 0:512], in0=pg[:, :], in1=rstd[:, 1:513], op=ALU.mult),
                            reads=[Bg, B_rstd], writes=[B_zt[z]])
                        P.op('scalar', lambda e, z=z: e.activation(out=zt[z][:, 0:512], in_=zt[z][:, 0:512], func=AF.Sigmoid),
                             reads=[B_zt[z]], writes=[B_zt[z]])
                        cv = rot('cv', 2)
                        P.op('vector', lambda e, z=z, cv=cv, pp=pp: e.tensor_tensor(
                            out=cva[cv][:], in0=zt[z][:, 0:512], in1=pp[:, :], op=ALU.mult),
                            reads=[B_zt[z], Bp], writes=[B_cva[cv]])
                        res.append(cv)
                    P.op('vector', lambda e, j=j, a=res[0], b=res[1]: e.tensor_tensor(
                        out=m_s[:, j, :], in0=cva[a][:], in1=cva[b][:], op=ALU.add),
                        reads=[B_cva[res[0]], B_cva[res[1]]], writes=[B_m])
                for j in range(32):
                    c0 = j * 128
                    s_o = wload(w_out[:, c0:c0 + 128], 32)
                    po, Bo = acc()
                    mm_group(po[:, :], Bo, [(s_o, 32, 0)], lambda kc: m_s[:, kc, :], [B_m])
                    cv = rot('cv', 2)
                    P.dma('sync', cvb[cv][:], xT[c0:c0 + 128, 1 + s:1 + s + 512], writes=[B_cvb[cv]])
                    P.op('vector', lambda e, cv=cv, po=po: e.tensor_tensor(
                        out=cva[cv][:], in0=po[:, :], in1=cvb[cv][:], op=ALU.add),
                        reads=[Bo, B_cvb[cv]], writes=[B_cva[cv]])
                    o = rot('of', NOB)
                    P.op('vector', lambda e, cv=cv, o=o: e.tensor_tensor(
                        out=of32[o][:], in0=cva[cv][:], in1=vmask[:, 1:513], op=ALU.mult),
                        reads=[B_cva[cv], B_vmask], writes=[B_of32[o]])
                    P.dma('sync', x1T[c0:c0 + 128, 1 + s:1 + s + 512], of32[o][:], reads=[B_of32[o]], writes=[B_x1T])

        def phase5():
            act_s = big[:, :].rearrange("p (c t) -> p c t", c=NFF)
            B_act = Buf("act")
            for ti in range(NT):
                s = ti * 512
                prep(x1T, s, 1)
                for i in range(NFF):
                    c0 = i * 128
                    s_g = wload(w_up[:, c0:c0 + 128], 32)
                    s_v = wload(w_up[:, DFF + c0:DFF + c0 + 128], 32)
                    pg, Bg = acc()
                    mm_group(pg[:, :], Bg, [(s_g, 32, 0)], lambda kc: hg[:, kc, 1:513], [B_hg])
                    ph = rot('ph', 8)
                    mm_group(psh[:, ph * 4:ph * 4 + 2], B_psh[ph], [(s_g, 32, 0)], lambda kc: hg[:, kc, 0:TW:513], [B_hg])
                    pv, Bv = acc()
                    mm_group(pv[:, :], Bv, [(s_v, 32, 0)], lambda kc: hg[:, kc, 1:513], [B_hg])
                    z = rot('zt', 2)
                    P.op('vector', [
                        lambda e, z=z, pg=pg: e.tensor_tensor(out=zt[z][:, 1:513], in0=pg[:, :], in1=rstd[:, 1:513], op=ALU.mult),
                        lambda e, z=z, ph=ph: e.tensor_tensor(out=zt[z][:, 0:TW:513], in0=psh[:, ph * 4:ph * 4 + 2],
                                                             in1=rstd[:, 0:TW:513], op=ALU.mult)],
                         reads=[Bg, B_psh[ph], B_rstd], writes=[B_zt[z]])
                    cv = rot('cv', 2)
                    P.op('vector', lambda e, z=z, cv=cv, i=i: e.tensor_scalar(
                        out=cva[cv][:], in0=zt[z][:, 0:512], scalar1=fcw_s[:, i, 0:1], scalar2=fcw_s[:, i, 3:4],
                        op0=ALU.mult, op1=ALU.add), reads=[B_zt[z], B_const], writes=[B_cva[cv]])
                    P.op('vector', lambda e, z=z, cv=cv, i=i: e.scalar_tensor_tensor(
                        out=cvb[cv][:], in0=zt[z][:, 1:513], scalar=fcw_s[:, i, 1:2], in1=cva[cv][:], op0=ALU.mult,
                        op1=ALU.add), reads=[B_zt[z], B_cva[cv], B_const], writes=[B_cvb[cv]])
                    P.op('vector', lambda e, z=z, cv=cv, i=i: e.scalar_tensor_tensor(
                        out=cva[cv][:], in0=zt[z][:, 2:514], scalar=fcw_s[:, i, 2:3], in1=cvb[cv][:], op0=ALU.mult,
                        op1=ALU.add), reads=[B_zt[z], B_cvb[cv], B_const], writes=[B_cva[cv]])
                    P.op('scalar', lambda e, cv=cv: e.activation(out=cvb[cv][:], in_=cva[cv][:], func=AF.Gelu),
                         reads=[B_cva[cv]], writes=[B_cvb[cv]])
                    o = rot('of', NOB)
                    P.op('vector', lambda e, o=o, pv=pv: e.tensor_tensor(
                        out=of32[o][:], in0=pv[:, :], in1=rstd[:, 1:513], op=ALU.mult),
                        reads=[Bv, B_rstd], writes=[B_of32[o]])
                    P.op('vector', lambda e, o=o, cv=cv, i=i: e.tensor_tensor(
                        out=act_s[:, i, :], in0=of32[o][:], in1=cvb[cv][:], op=ALU.mult),
                        reads=[B_of32[o], B_cvb[cv]], writes=[B_act])
                for j in range(32):
                    c0 = j * 128
                    pcs = [(wload(w_down[kb * 128:(kb + n) * 128, c0:c0 + 128], n), n, kb)
                           for (kb, n) in ((0, 32), (32, 32), (64, 22))]
                    pd, Bd = acc()
                    mm_group(pd[:, :], Bd, pcs, lambda kc: act_s[:, kc, :], [B_act])
                    cv = rot('cv', 2)
                    P.dma('sync', cvb[cv][:], x1T[c0:c0 + 128, 1 + s:1 + s + 512], reads=[B_x1T], writes=[B_cvb[cv]])
                    o = rot('of', NOB)
                    P.op('vector', lambda e, cv=cv, pd=pd, o=o: e.tensor_tensor(
                        out=of32[o][:], in0=pd[:, :], in1=cvb[cv][:], op=ALU.add),
                        reads=[Bd, B_cvb[cv]], writes=[B_of32[o]])
                    P.dma('sync', x2T[c0:c0 + 128, 1 + s:1 + s + 512], of32[o][:], reads=[B_of32[o]], writes=[B_x2T])

        def phase6():
            x3_s = big[:, 0:32768].bitcast(F32).rearrange("p (c t) -> p c t", c=32)
            B_x3 = Buf("x3")
            for ti in range(NT):
                s = ti * 512
                prep(x2T, s, 2)
                for kc in range(2):
                    cv = rot('cv', 2)
                    P.dma('sync', cva[cv][:], pT[kc * 128:(kc + 1) * 128, s:s + 512], writes=[B_cva[cv]])
                    P.op('scalar', lambda e, cv=cv, kc=kc: e.copy(out=pb_s[:, kc, :], in_=cva[cv][:]),
                         reads=[B_cva[cv]], writes=[B_pb])
                for j in range(32):
                    c0 = j * 128
                    s_g = wload(w_pg[:, c0:c0 + 128], 32)
                    s_p = wload(w_ple[:, c0:c0 + 128], 2)
                    pg, Bg = acc()
                    mm_group(pg[:, :], Bg, [(s_g, 32, 0)], lambda kc: hg[:, kc, 1:513], [B_hg])
                    pp, Bp = acc()
                    mm_group(pp[:, :], Bp, [(s_p, 2, 0)], lambda kc: pb_s[:, kc, :], [B_pb])
                    z = rot('zt', 2)
                    P.op('vector', lambda e, z=z, pg=pg: e.tensor_tensor(
                        out=zt[z][:, 0:512], in0=pg[:, :], in1=rstd[:, 1:513], op=ALU.mult),
                        reads=[Bg, B_rstd], writes=[B_zt[z]])
                    P.op('scalar', lambda e, z=z: e.activation(out=zt[z][:, 0:512], in_=zt[z][:, 0:512], func=AF.Sigmoid),
                         reads=[B_zt[z]], writes=[B_zt[z]])
                    cv = rot('cv', 2)
                    P.op('vector', lambda e, z=z, cv=cv, pp=pp: e.tensor_tensor(
                        out=cva[cv][:], in0=zt[z][:, 0:512], in1=pp[:, :], op=ALU.mult),
                        reads=[B_zt[z], Bp], writes=[B_cva[cv]])
                    P.dma('sync', cvb[cv][:], x2T[c0:c0 + 128, 1 + s:1 + s + 512], reads=[B_x2T], writes=[B_cvb[cv]])
                    P.op('vector', lambda e, cv=cv, j=j: e.tensor_tensor(
                        out=x3_s[:, j, :], in0=cva[cv][:], in1=cvb[cv][:], op=ALU.add),
                        reads=[B_cva[cv], B_cvb[cv]], writes=[B_x3])
                    o = rot('ob', NOB)
                    P.op('scalar', lambda e, o=o, j=j: e.activation(out=obf[o][:], in_=x3_s[:, j, :], func=AF.Square),
                         reads=[B_x3], writes=[B_obf[o]])
                    P.op('tensor', lambda e, o=o, j=j: e.matmul(pss[:, :], lhsT=ones_b[:], rhs=obf[o][:],
                                                                start=(j == 0), stop=(j == 31)),
                         reads=[B_obf[o], B_const], writes=[B_pss])
                P.op('scalar', lambda e: e.activation(out=rt[:, 1:513], in_=pss[:, :], func=AF.Sqrt, bias=eps_t[:, 0:1],
                                                      scale=1.0 / D), reads=[B_pss, B_const], writes=[B_rt])
                P.op('vector', lambda e: e.reciprocal(out=rstd[:, 1:513], in_=rt[:, 1:513]), reads=[B_rt], writes=[B_rstd])
                for j in range(32):
                    o = rot('of', NOB)
                    P.op('vector', lambda e, o=o, j=j: e.scalar_tensor_tensor(
                        out=of32[o][:], in0=x3_s[:, j, :], scalar=gains_s[:, 3, j:j + 1], in1=rstd[:, 1:513],
                        op0=ALU.mult, op1=ALU.mult), reads=[B_x3, B_rstd, B_const], writes=[B_of32[o]])
                    P.dma('sync', yT[j * 128:(j + 1) * 128, s:s + 512], of32[o][:], reads=[B_of32[o]])

        for ph_name in ('p1', 'p2', 'p3', 'p4', 'p5', 'p6'):
            if ph_name not in phases:
                continue
            if ph_name == 'p1':
                phase1()
            elif ph_name == 'p2':
                phase2()
            elif ph_name == 'p3':
                phase3()
            elif ph_name == 'p4':
                phase4('p2' in phases)
            elif ph_name == 'p5':
                phase5()
            elif ph_name == 'p6':
                phase6()
            P.barrier()

        P.finish()

        @block.sync
        def _(e):
            for f in P.ops['sync']:
                f(e)

        @block.scalar
        def _(e):
            for f in P.ops['scalar']:
                f(e)

        @block.vector
        def _(e):
            for f in P.ops['vector']:
                f(e)

        @block.gpsimd
        def _(e):
            for f in P.ops['gpsimd']:
                f(e)

        @block.tensor
        def _(e):
            for f in P.ops['tensor']:
                f(e)
    return nc


def _t5_bucket(rel):
    half = 16
    max_exact = 8
    ret = np.where(rel > 0, half, 0)
    n = np.abs(rel)
    nf = np.maximum(n, 1).astype(np.float32)
    large = max_exact + (np.log(nf / max_exact) / math.log(128 / max_exact) * (half - max_exact)).astype(np.int32)
    large = np.minimum(large, half - 1)
    return ret + np.where(n < max_exact, n, large)


def _chunked(v, n):
    return np.ascontiguousarray(np.asarray(v, np.float32).reshape(n, 128).T)


def make_inputs(inp, core):
    f32 = np.float32
    if core < 4:
        x = np.asarray(inp['x_prompt'][core]); p = np.asarray(inp['p_prompt'][0, core]); L = 4096
    else:
        x = np.asarray(inp['x_sample'][core - 4]); p = np.asarray(inp['p_sample'][0, core - 4]); L = 2048
    xTp = np.zeros((D, LT + 2), f32)
    xTp[:, 1:L + 1] = x.T
    pTp = np.zeros((256, LT), f32)
    pTp[:, :L] = p.T
    valid = np.zeros((128, LT + 2), f32)
    valid[:, 1:L + 1] = 1.0
    kmask = np.zeros((128, 32), f32)
    tok = np.arange(32)[None, :] * 128 + np.arange(128)[:, None]
    kmask[tok >= L] = NEG
    pos = np.arange(LT, dtype=f32)
    t = (pos / f32(L - 1)).astype(f32)
    bands = np.linspace(1e-4, 15, 16, dtype=f32)
    ang = (f32(2.0 * math.pi / L) * pos[:, None] * bands[None, :]).astype(f32)
    zf = np.concatenate([t[:, None], np.cos(ang), -np.sin(ang)], axis=-1).astype(f32)
    m = {'xT': xTp, 'pT': pTp, 'valid': valid, 'kmask': kmask,
         'zfT': np.ascontiguousarray(zf.T), 'tnb': np.ascontiguousarray(np.broadcast_to(t[None, :], (128, LT)))}
    return m


def shared_inputs(inp):
    f32 = np.float32
    m = {}
    m['w_in'] = np.asarray(inp['w_in'][0]); m['w_attn_o'] = np.asarray(inp['w_attn_o'][0])
    m['w_hyena_o'] = np.asarray(inp['w_hyena_o'][0]); m['w_out'] = np.asarray(inp['w_out'][0])
    m['w_up'] = np.asarray(inp['w_up'][0]); m['w_down'] = np.asarray(inp['w_down'][0])
    m['w_ple_gate'] = np.asarray(inp['w_ple_gate'][0]); m['w_ple'] = np.asarray(inp['w_ple'][0])
    g = np.stack([_chunked(inp['g_mix'][0], 32), _chunked(inp['g_ffn'][0], 32), _chunked(inp['g_ple'][0], 32),
                  _chunked(inp['g_final'], 32)], axis=1)
    m['gains'] = np.ascontiguousarray(g)
    hw = np.asarray(inp['hy_short_w'][0]); hb = np.asarray(inp['hy_short_b'][0])
    m['hcw'] = np.ascontiguousarray(np.stack([_chunked(hw[0], 48), _chunked(hw[1], 48), _chunked(hw[2], 48),
                                              _chunked(hb, 48)], axis=2))
    fw = np.asarray(inp['ffn_conv_w'][0]); fb = np.asarray(inp['ffn_conv_b'][0])
    m['fcw'] = np.ascontiguousarray(np.stack([_chunked(fw[0], NFF), _chunked(fw[1], NFF), _chunked(fw[2], NFF),
                                              _chunked(fb, NFF)], axis=2))
    m['sinkr'] = np.ascontiguousarray(np.broadcast_to(np.asarray(inp['attn_sink'][0], f32)[None, :], (128, 16)))
    m['rbext'] = np.concatenate([np.asarray(inp['rel_bias'], f32), np.full((1, 16), NEG, f32)], axis=0)
    d = np.arange(768) - 384
    bk = _t5_bucket(d)
    T = np.zeros((33, 768), f32)
    T[bk, np.arange(768)] = 1.0
    T[32, :] = (np.abs(d) > 128).astype(f32)
    m['bkt'] = T
    m['identb'] = np.eye(128, dtype=f32).astype(ml_dtypes.bfloat16)
    m['hskip'] = _chunked(inp['hy_skip'][0], 16)
    m['fw1'] = np.asarray(inp['hy_filt_w1'][0], f32); m['fw2'] = np.asarray(inp['hy_filt_w2'][0], f32)
    m['fw3'] = np.asarray(inp['hy_filt_w3'][0], f32)
    m['fpar'] = np.ascontiguousarray(np.stack([inp['hy_filt_b1'][0], inp['hy_filt_f1'][0], inp['hy_filt_b2'][0],
                                               inp['hy_filt_f2'][0]], axis=1).astype(f32))
    m['decay'] = _chunked(np.asarray(inp['hy_decay'][0]).reshape(-1), 32)
    m.update(_fft_tables())
    return m


def _fft_tables():
    bf = ml_dtypes.bfloat16
    N = 8192
    th = np.arange(32)[:, None]; fa = np.arange(33)[None, :]
    a = 2 * np.pi * th * fa / 64.0
    fh = np.concatenate([np.cos(a), -np.sin(a), np.sin(a)], axis=1)
    tl = np.arange(128)[:, None, None]; fa3 = np.arange(33)[None, :, None]; fb = np.arange(128)[None, None, :]
    ph = 2 * np.pi * ((tl * (fa3 + 64 * fb)) % N) / N
    g = np.stack([np.cos(ph), -np.sin(ph)], axis=2)
    fbp = np.arange(128)[:, None]; tlo = np.arange(128)[None, :]
    p2 = 2 * np.pi * ((fbp * tlo) % 128) / 128.0
    gi = np.stack([np.cos(p2), np.sin(p2)], axis=1)
    fa_ = np.arange(64)[:, None, None]; tl_ = np.arange(128)[None, :, None]; th_ = np.arange(32)[None, None, :]
    th3 = 2 * np.pi * ((fa_ * (128 * th_ + tl_)) % N) / N
    w = np.zeros(64); w[0] = 1; w[32] = 1; w[1:32] = 2
    mi = np.concatenate([w[:, None, None] * np.cos(th3) / N, -w[:, None, None] * np.sin(th3) / N], axis=0)
    return {'fh_t': fh.astype(np.float32).astype(bf), 'g_t': g.reshape(128, -1).astype(np.float32).astype(bf),
            'gi_t': gi.reshape(128, -1).astype(np.float32).astype(bf), 'mi_t': mi.reshape(128, -1).astype(np.float32).astype(bf)}


_NC_CACHE = {}


def kernel(**inputs):
    key = 'full'
    if key not in _NC_CACHE:
        _NC_CACHE[key] = build()
    nc = _NC_CACHE[key]
    sh = shared_inputs(inputs)
    in_maps = []
    for c in range(8):
        m = dict(sh)
        m.update(make_inputs(inputs, c))
        in_maps.append(m)
    names = set()
    for alloc in nc.allocations:
        if isinstance(alloc, mybir.MemoryLocationSet) and alloc.kind == "ExternalInput":
            names.add(alloc.memorylocations[0].name)
    in_maps = [{k: v for k, v in m.items() if k in names} for m in in_maps]
    res = run_bass_kernel_spmd(nc, in_maps, core_ids=list(range(8)))
    yp = np.stack([res.results[c]['yT'].T for c in range(4)], axis=0).astype(np.float32)
    ys = np.stack([res.results[c]['yT'][:, :2048].T for c in range(4, 8)], axis=0).astype(np.float32)
    return (np.ascontiguousarray(yp), np.ascontiguousarray(ys))
```

```python
import math
from contextlib import ExitStack
import numpy as np
import ml_dtypes
import concourse.bass as bass
import concourse.mybir as mybir
from concourse.bass_utils import run_bass_kernel_spmd

F32 = mybir.dt.float32
BF16 = mybir.dt.bfloat16
AF = mybir.ActivationFunctionType
ALU = mybir.AluOpType

D = 4096
LT = 4096
NT = 8
TW = 514
DFF = 11008
NFF = 86
IN_COLS = 17408
EPS = 1e-6
NEG = -1e30
ENGS = ['sync', 'scalar', 'vector', 'gpsimd', 'tensor']


class Buf:
    __slots__ = ('name', 'w', 'r', 'track')

    def __init__(self, name, track=True):
        self.name = name
        self.w = {}
        self.r = {}
        self.track = track


class Prog:
    def __init__(self, nc, es):
        self.nc = nc
        self.es = es
        self.ops = {e: [] for e in ENGS}
        self.waited = {e: {} for e in ENGS}
        self.esem = {}
        self.ecnt = {}
        for e in ['scalar', 'vector', 'tensor', 'gpsimd']:
            self.esem[e] = es.enter_context(nc.semaphore('es_' + e))
            self.ecnt[e] = 0
        self.dpool = {}
        self.dnext = {}
        for q, n in [('sync', 14), ('gpsimd', 8)]:
            self.dpool[q] = [[es.enter_context(nc.semaphore(f'd_{q}{i}')), 0] for i in range(n)]
            self.dnext[q] = 0
        self.sid = {}

    def _sid(self, h):
        return id(h)

    def _need(self, eng, reads, writes):
        need = {}
        own = self._sid(self.esem[eng]) if eng in self.esem else None

        def add(d, skip_own):
            for sid, (h, v) in d.items():
                if skip_own and sid == own:
                    continue
                if sid not in need or need[sid][1] < v:
                    need[sid] = (h, v)
        for b in reads:
            if b.track:
                add(b.w, False)
        for b in writes:
            if b.track:
                add(b.r, True)
                add(b.w, True)
        wd = self.waited[eng]
        for sid, (h, v) in need.items():
            if wd.get(sid, 0) < v:
                wd[sid] = v
                self.ops[eng].append(lambda e, h=h, v=v: e.wait_ge(h, v))

    def _mark(self, reads, writes, h, v):
        sid = self._sid(h)
        for b in reads:
            if b.track:
                b.r[sid] = (h, v)
        for b in writes:
            if b.track:
                b.w = {sid: (h, v)}
                b.r = {}

    def op(self, eng, fns, reads=(), writes=()):
        if not isinstance(fns, (list, tuple)):
            fns = [fns]
        self._need(eng, reads, writes)
        self.ecnt[eng] += 1
        v = self.ecnt[eng]
        h = self.esem[eng]
        for f in fns[:-1]:
            self.ops[eng].append(f)
        last = fns[-1]
        self.ops[eng].append(lambda e, last=last, h=h: last(e).then_inc(h, 1))
        self._mark(reads, writes, h, v)

    def dma(self, q, out, in_, reads=(), writes=(), **kw):
        self._need(q, reads, writes)
        pool = self.dpool[q]
        i = self.dnext[q]
        self.dnext[q] = (i + 1) % len(pool)
        h, cnt = pool[i]
        wd = self.waited[q]
        sid = self._sid(h)
        if cnt > 0 and wd.get(sid, 0) < cnt:
            wd[sid] = cnt
            self.ops[q].append(lambda e, h=h, v=cnt: e.wait_ge(h, v))
        cnt += 16
        pool[i][1] = cnt
        self.ops[q].append(lambda e, out=out, in_=in_, h=h, kw=kw: e.dma_start(out=out, in_=in_, **kw).then_inc(h, 16))
        self._mark(reads, writes, h, cnt)

    def barrier(self):
        deps = []
        for e2, h in self.esem.items():
            if self.ecnt[e2] > 0:
                deps.append((h, self.ecnt[e2]))
        for q in self.dpool:
            for h, cnt in self.dpool[q]:
                if cnt > 0:
                    deps.append((h, cnt))
        for eng in ENGS:
            wd = self.waited[eng]
            for h, v in deps:
                sid = self._sid(h)
                if wd.get(sid, 0) < v:
                    wd[sid] = v
                    self.ops[eng].append(lambda e, h=h, v=v: e.wait_ge(h, v))

    def finish(self):
        for q in self.dpool:
            for h, cnt in self.dpool[q]:
                if cnt > 0:
                    self.ops[q].append(lambda e, h=h, v=cnt: e.wait_ge(h, v))
        for q in ['sync']:
            for e2 in self.esem:
                if self.ecnt[e2] > 0:
                    self.ops[q].append(lambda e, h=self.esem[e2], v=self.ecnt[e2]: e.wait_ge(h, v))
            for h, cnt in self.dpool['gpsimd']:
                if cnt > 0:
                    self.ops[q].append(lambda e, h=h, v=cnt: e.wait_ge(h, v))


def build(phases=('p1', 'p2', 'p3', 'p4', 'p5', 'p6'), dbg=()):
    nc = bass.Bass("TRN2", target_bir_lowering=False)

    def din(name, shape, dt=F32):
        return nc.dram_tensor(name, list(shape), dt, kind="ExternalInput").ap()

    def dscr(name, shape, dt=F32):
        kind = "ExternalOutput" if name in dbg else "Internal"
        return nc.dram_tensor(name, list(shape), dt, kind=kind).ap()

    need = set(phases)
    xT = din("xT", [D, LT + 2])
    valid = din("valid", [128, LT + 2])
    gains = din("gains", [128, 4, 32])
    identb = din("identb", [128, 128], BF16)
    w_in = din("w_in", [D, IN_COLS]) if need & {'p1', 'p4'} else None
    hcw = din("hcw", [128, 48, 4]) if 'p1' in need else None
    kmask = din("kmask", [128, 32]) if 'p3' in need else None
    sinkr = din("sinkr", [128, 16]) if 'p3' in need else None
    rbext = din("rbext", [33, 16]) if 'p3' in need else None
    bkt = din("bkt", [33, 768]) if 'p3' in need else None
    w_ao = din("w_attn_o", [2048, D]) if 'p4' in need else None
    w_ho = din("w_hyena_o", [2048, D]) if 'p4' in need else None
    w_out = din("w_out", [D, D]) if 'p4' in need else None
    hskip = din("hskip", [128, 16]) if 'p4' in need else None
    if 'p2' in need:
        zfT = din("zfT", [33, LT]); tnb = din("tnb", [128, LT])
        fw1 = din("fw1", [33, 64]); fw2 = din("fw2", [64, 64]); fw3 = din("fw3", [64, 4096])
        fpar = din("fpar", [64, 4]); decay = din("decay", [128, 32])
        fh_t = din("fh_t", [32, 99], BF16); g_t = din("g_t", [128, 33 * 256], BF16)
        gi_t = din("gi_t", [128, 256], BF16); mi_t = din("mi_t", [128, 4096], BF16)
    w_up = din("w_up", [D, 2 * DFF]) if 'p5' in need else None
    w_down = din("w_down", [DFF, D]) if 'p5' in need else None
    fcw = din("fcw", [128, NFF, 4]) if 'p5' in need else None
    w_pg = din("w_ple_gate", [D, D]) if 'p6' in need else None
    w_ple = din("w_ple", [256, D]) if 'p6' in need else None
    pT = din("pT", [256, LT]) if 'p6' in need else None
    yT = nc.dram_tensor("yT", [D, LT], F32, kind="ExternalOutput").ap()

    qT = dscr("qT", [2048, LT], BF16)
    kT = dscr("kT", [512, LT], BF16)
    vS = dscr("vS", [LT, 512], BF16)
    uT = dscr("uT", [2048, LT])
    hx0T = dscr("hx0T", [2048, LT])
    ycT = dscr("ycT", [2048, LT])
    attnT = dscr("attnT", [2048, LT], BF16)
    x1T = dscr("x1T", [D, LT + 2])
    x2T = dscr("x2T", [D, LT + 2])
    hfb = dscr("hfb", [4096, LT], BF16)
    NPIECE = 72 + 160 + 268 + 64
    wcaches = [dscr(f"wcache{i}", [128, 128, 4096], BF16) for i in range((NPIECE + 127) // 128)]

    B_qT, B_kT, B_vS, B_uT, B_hx0T, B_ycT, B_attnT, B_x1T, B_x2T = [Buf(n, track=False) for n in ('qT','kT','vS','uT','hx0T','ycT','attnT','x1T','x2T')]
    with ExitStack() as es:
        E = es.enter_context
        P = Prog(nc, es)

        def sb(name, shape, dt):
            return E(nc.sbuf_tensor(name, list(shape), dt))

        xin = [sb(f"xin{i}", [128, 2, TW], F32) for i in range(2)]
        B_xin = [Buf(f"xin{i}") for i in range(2)]
        sq = [sb(f"sq{i}", [128, 2, TW], BF16) for i in range(2)]
        B_sq = [Buf(f"sq{i}") for i in range(2)]
        hg = sb("hg", [128, 32, TW], BF16)
        B_hg = Buf("hg")
        rt = sb("rt", [128, TW], F32)
        B_rt = Buf("rt")
        rstd = sb("rstd", [128, TW], F32)
        B_rstd = Buf("rstd")
        NWB = 4
        wb = [sb(f"wb{i}", [128, 32, 128], BF16) for i in range(NWB)]
        B_wb = [Buf(f"wb{i}") for i in range(NWB)]
        NOB = 3
        obf = [sb(f"obf{i}", [128, 512], BF16) for i in range(NOB)]
        B_obf = [Buf(f"obf{i}") for i in range(NOB)]
        of32 = [sb(f"of{i}", [128, 512], F32) for i in range(NOB)]
        B_of32 = [Buf(f"of{i}") for i in range(NOB)]
        zt = [sb(f"zt{i}", [128, TW], F32) for i in range(2)]
        B_zt = [Buf(f"zt{i}") for i in range(2)]
        cva = [sb(f"cva{i}", [128, 512], F32) for i in range(2)]
        B_cva = [Buf(f"cva{i}") for i in range(2)]
        cvb = [sb(f"cvb{i}", [128, 512], F32) for i in range(2)]
        B_cvb = [Buf(f"cvb{i}") for i in range(2)]
        hvc = sb("hvc", [128, 512], F32)
        B_hvc = Buf("hvc")
        vmask = sb("vmask", [128, TW], F32)
        B_vmask = Buf("vmask")
        big = sb("big", [128, NFF * 512], BF16)
        B_big = Buf("big")
        ones_b = sb("ones_b", [128, 128], BF16)
        ident = sb("ident", [128, 128], BF16)
        eps_t = sb("eps_t", [128, 1], F32)
        gains_s = sb("gains_s", [128, 4, 32], F32)
        hcw_s = sb("hcw_s", [128, 48, 4], F32)
        fcw_s = sb("fcw_s", [128, NFF, 4], F32)
        hskip_s = sb("hskip_s", [128, 16], F32)
        B_const = Buf("const")

        psm = [E(nc.psum_tensor(f"psm{i}", [128, 512], F32)) for i in range(3)]
        B_psm = [Buf(f"psm{i}") for i in range(3)]
        psh = E(nc.psum_tensor("psh", [128, 512], F32))
        B_psh = [Buf(f"psh{i}") for i in range(8)]
        pss = E(nc.psum_tensor("pss", [128, 512], F32))
        B_pss = Buf("pss")
        pss2 = E(nc.psum_tensor("pss2", [128, 512], F32))
        B_pss2 = Buf("pss2")
        psx = E(nc.psum_tensor("psx", [128, 512], F32))
        B_psx = Buf("psx")
        pst = E(nc.psum_tensor("pst", [128, 1024], BF16))
        B_pst = Buf("pst")

        block = E(nc.Block())

        P.op('vector', lambda e: e.memset(ones_b[:], 1.0), writes=[B_const])
        P.op('vector', lambda e: e.memset(eps_t[:], EPS), writes=[B_const])
        P.dma('sync', ident[:], identb, writes=[B_const])
        P.dma('sync', gains_s[:], gains, writes=[B_const])
        if hcw is not None:
            P.dma('sync', hcw_s[:], hcw, writes=[B_const])
        if fcw is not None:
            P.dma('sync', fcw_s[:], fcw, writes=[B_const])
        if hskip is not None:
            P.dma('sync', hskip_s[:], hskip, writes=[B_const])

        cnt = {'pm': 0, 'ph': 0, 'ob': 0, 'of': 0, 'zt': 0, 'cv': 0, 'wb': 0}

        def rot(key, n):
            v = cnt[key] % n
            cnt[key] += 1
            return v

        def prep(src, s, gi):
            for qd in range(16):
                sl = qd % 2
                P.dma('sync', xin[sl][:], src[qd * 256:(qd + 1) * 256, s:s + TW].rearrange("(kc p) t -> p kc t", p=128),
                      writes=[B_xin[sl]])
                P.op('scalar', lambda e, sl=sl: e.activation(out=sq[sl][:], in_=xin[sl][:], func=AF.Square),
                     reads=[B_xin[sl]], writes=[B_sq[sl]])
                fns = []
                for j in range(2):
                    kc = qd * 2 + j
                    fns.append(lambda e, sl=sl, j=j, kc=kc: e.tensor_scalar(
                        out=hg[:, kc, :], in0=xin[sl][:, j, :], scalar1=gains_s[:, gi, kc:kc + 1], scalar2=None,
                        op0=ALU.mult))
                P.op('vector', fns, reads=[B_xin[sl], B_const], writes=[B_hg])
                fns = []
                for j in range(2):
                    kc = qd * 2 + j
                    fns.append(lambda e, sl=sl, j=j, kc=kc: e.matmul(
                        pss[:, :], lhsT=ones_b[:], rhs=sq[sl][:, j, 1:513], start=(kc == 0), stop=(kc == 31)))
                    fns.append(lambda e, sl=sl, j=j, kc=kc: e.matmul(
                        pss2[:, 0:2], lhsT=ones_b[:], rhs=sq[sl][:, j, 0:TW:513], start=(kc == 0), stop=(kc == 31)))
                P.op('tensor', fns, reads=[B_sq[sl], B_const], writes=[B_pss, B_pss2])
            P.op('scalar', [
                lambda e: e.activation(out=rt[:, 1:513], in_=pss[:, :], func=AF.Sqrt, bias=eps_t[:, 0:1], scale=1.0 / D),
                lambda e: e.activation(out=rt[:, 0:TW:513], in_=pss2[:, 0:2], func=AF.Sqrt, bias=eps_t[:, 0:1],
                                       scale=1.0 / D)],
                 reads=[B_pss, B_pss2, B_const], writes=[B_rt])
            P.op('vector', lambda e: e.reciprocal(out=rstd[:], in_=rt[:]), reads=[B_rt], writes=[B_rstd])

        pieces = {}

        def wload(src_rows, nkc, key=None):
            sl = rot('wb', NWB)
            if key is not None and key in pieces:
                idx, Bp = pieces[key]
                P.dma('gpsimd', wb[sl][:, 0:nkc, :], wcaches[idx // 128][idx % 128, :, 0:nkc * 128].rearrange("p (k c) -> p k c", c=128),
                      reads=[Bp], writes=[B_wb[sl]])
                return sl
            P.dma('gpsimd', wb[sl][:, 0:nkc, :], src_rows.rearrange("(kc p) c -> p kc c", p=128), writes=[B_wb[sl]])
            if key is not None:
                idx = len(pieces)
                Bp = Buf(f"piece{idx}")
                pieces[key] = (idx, Bp)
                P.dma('sync', wcaches[idx // 128][idx % 128, :, 0:nkc * 128].rearrange("p (k c) -> p k c", c=128), wb[sl][:, 0:nkc, :],
                      reads=[B_wb[sl]], writes=[Bp])
            return sl

        def mm_group(ps_ap, B_ps, pieces, rhs_fn, extra_reads, n_total=None):
            fns = []
            tot = sum(p[1] for p in pieces)
            i = 0
            rd = list(extra_reads)
            for (sl, nkc, kb) in pieces:
                rd.append(B_wb[sl])
                for k in range(nkc):
                    fns.append(lambda e, sl=sl, k=k, kb=kb, i=i: e.matmul(
                        ps_ap, lhsT=wb[sl][:, k, :], rhs=rhs_fn(kb + k), start=(i == 0), stop=(i == tot - 1)))
                    i += 1
            P.op('tensor', fns, reads=rd, writes=[B_ps])

        def phase1():
            chunks = [('q', h * 128, h) for h in range(16)]
            chunks += [('k', 2048 + h * 128, h) for h in range(4)]
            chunks += [('v', 2560 + h * 128, h) for h in range(4)]
            for c in range(16):
                chunks += [('hv', 3072 + c * 128, c), ('hx1', 5120 + c * 128, 16 + c), ('hx0', 7168 + c * 128, 32 + c)]
            SC = 128.0 ** -0.5
            for ti in range(NT):
                s = ti * 512
                prep(xT, s, 0)
                P.dma('sync', vmask[:], valid[:, s:s + TW], writes=[B_vmask])
                pend = [wload(w_in[:, chunks[i][1]:chunks[i][1] + 128], 32, ('p1', i)) for i in range(2)]
                for n, (kind, col0, idx) in enumerate(chunks):
                    if n + 2 < len(chunks):
                        c2 = chunks[n + 2][1]
                        pend.append(wload(w_in[:, c2:c2 + 128], 32, ('p1', n + 2)))
                    sl = pend.pop(0)
                    pm = rot('pm', 3)
                    mm_group(psm[pm][:, :], B_psm[pm], [(sl, 32, 0)], lambda kc: hg[:, kc, 1:513], [B_hg])
                    if kind in ('q', 'k', 'v'):
                        o = rot('ob', NOB)
                        if kind == 'q':
                            P.op('vector', lambda e, o=o, pm=pm: e.scalar_tensor_tensor(
                                out=obf[o][:], in0=psm[pm][:, :], scalar=SC, in1=rstd[:, 1:513], op0=ALU.mult,
                                op1=ALU.mult), reads=[B_psm[pm], B_rstd], writes=[B_obf[o]])
                            P.dma('sync', qT[idx * 128:(idx + 1) * 128, s:s + 512], obf[o][:], reads=[B_obf[o]])
                        elif kind == 'k':
                            P.op('vector', lambda e, o=o, pm=pm: e.tensor_tensor(
                                out=obf[o][:], in0=psm[pm][:, :], in1=rstd[:, 1:513], op=ALU.mult),
                                reads=[B_psm[pm], B_rstd], writes=[B_obf[o]])
                            P.dma('sync', kT[idx * 128:(idx + 1) * 128, s:s + 512], obf[o][:], reads=[B_obf[o]])
                        else:
                            P.op('vector', lambda e, o=o, pm=pm: e.tensor_tensor(
                                out=obf[o][:], in0=psm[pm][:, :], in1=rstd[:, 1:513], op=ALU.mult),
                                reads=[B_psm[pm], B_rstd], writes=[B_obf[o]])
                            fns = [lambda e, o=o, b=b: e.transpose(pst[:, b * 128:(b + 1) * 128],
                                                                   obf[o][:, b * 128:(b + 1) * 128], ident[:])
                                   for b in range(4)]
                            P.op('tensor', fns, reads=[B_obf[o], B_const], writes=[B_pst])
                            o2 = rot('ob', NOB)
                            P.op('scalar', lambda e, o2=o2: e.copy(out=obf[o2][:], in_=pst[:, 0:512]),
                                 reads=[B_pst], writes=[B_obf[o2]])
                            P.dma('sync', vS[s:s + 512, idx * 128:(idx + 1) * 128].rearrange("(b p) d -> p b d", p=128),
                                  obf[o2][:].rearrange("p (b d) -> p b d", b=4), reads=[B_obf[o2]])
                        continue
                    ph = rot('ph', 8)
                    mm_group(psh[:, ph * 4:ph * 4 + 2], B_psh[ph], [(sl, 32, 0)], lambda kc: hg[:, kc, 0:TW:513], [B_hg])
                    z = rot('zt', 2)
                    P.op('vector', [
                        lambda e, z=z, pm=pm: e.tensor_tensor(out=zt[z][:, 1:513], in0=psm[pm][:, :], in1=rstd[:, 1:513],
                                                             op=ALU.mult),
                        lambda e, z=z, ph=ph: e.tensor_tensor(out=zt[z][:, 0:TW:513], in0=psh[:, ph * 4:ph * 4 + 2],
                                                             in1=rstd[:, 0:TW:513], op=ALU.mult)],
                         reads=[B_psm[pm], B_psh[ph], B_rstd], writes=[B_zt[z]])
                    cv = rot('cv', 2)
                    P.op('vector', lambda e, z=z, cv=cv, idx=idx: e.tensor_scalar(
                        out=cva[cv][:], in0=zt[z][:, 0:512], scalar1=hcw_s[:, idx, 0:1], scalar2=hcw_s[:, idx, 3:4],
                        op0=ALU.mult, op1=ALU.add), reads=[B_zt[z], B_const], writes=[B_cva[cv]])
                    P.op('vector', lambda e, z=z, cv=cv, idx=idx: e.scalar_tensor_tensor(
                        out=cvb[cv][:], in0=zt[z][:, 1:513], scalar=hcw_s[:, idx, 1:2], in1=cva[cv][:], op0=ALU.mult,
                        op1=ALU.add), reads=[B_zt[z], B_cva[cv], B_const], writes=[B_cvb[cv]])
                    if kind == 'hv':
                        P.op('vector', lambda e, z=z, cv=cv, idx=idx: e.scalar_tensor_tensor(
                            out=hvc[:], in0=zt[z][:, 2:514], scalar=hcw_s[:, idx, 2:3], in1=cvb[cv][:], op0=ALU.mult,
                            op1=ALU.add), reads=[B_zt[z], B_cvb[cv], B_const], writes=[B_hvc])
                    elif kind == 'hx1':
                        P.op('vector', lambda e, z=z, cv=cv, idx=idx: e.scalar_tensor_tensor(
                            out=cva[cv][:], in0=zt[z][:, 2:514], scalar=hcw_s[:, idx, 2:3], in1=cvb[cv][:], op0=ALU.mult,
                            op1=ALU.add), reads=[B_zt[z], B_cvb[cv], B_const], writes=[B_cva[cv]])
                        P.op('vector', lambda e, cv=cv: e.tensor_tensor(
                            out=cvb[cv][:], in0=cva[cv][:], in1=hvc[:], op=ALU.mult),
                            reads=[B_cva[cv], B_hvc], writes=[B_cvb[cv]])
                        o = rot('of', NOB)
                        P.op('vector', lambda e, cv=cv, o=o: e.tensor_tensor(
                            out=of32[o][:], in0=cvb[cv][:], in1=vmask[:, 1:513], op=ALU.mult),
                            reads=[B_cvb[cv], B_vmask], writes=[B_of32[o]])
                        c = idx - 16
                        P.dma('sync', uT[c * 128:(c + 1) * 128, s:s + 512], of32[o][:], reads=[B_of32[o]])
                    else:
                        o = rot('of', NOB)
                        P.op('vector', lambda e, z=z, cv=cv, idx=idx, o=o: e.scalar_tensor_tensor(
                            out=of32[o][:], in0=zt[z][:, 2:514], scalar=hcw_s[:, idx, 2:3], in1=cvb[cv][:], op0=ALU.mult,
                            op1=ALU.add), reads=[B_zt[z], B_cvb[cv], B_const], writes=[B_of32[o]])
                        c = idx - 32
                        P.dma('sync', hx0T[c * 128:(c + 1) * 128, s:s + 512], of32[o][:], reads=[B_of32[o]])


        att_o = [sb(f"att_o{i}", [128, 512], BF16) for i in range(2)]
        B_att_o = [Buf(f"att_o{i}") for i in range(2)]
        biasT = big[:, 24576:30720].rearrange("p (a t) -> p a t", a=12)
        bkt_s = big[0:33, 30720:32256].bitcast(F32)
        rb_s = sb("rb_s", [33, 16], F32)
        esink = sb("esink", [128, 16], F32)
        kmask_s = sb("kmask_s", [128, 32], F32)
        B_c3 = Buf("c3")
        pb_s = sb("pb_s", [128, 2, 512], BF16)
        B_pb = Buf("pb")
        zero_c = sb("zero_c", [128, 32], F32)
        P.op('vector', lambda e: e.memset(zero_c[:], 0.0), writes=[B_const])
        accs = [(psm[0], B_psm[0]), (psm[1], B_psm[1]), (psm[2], B_psm[2]), (psx, B_psx)]
        cnt['acc'] = 0
        cnt['ao'] = 0

        def acc():
            return accs[rot('acc', 4)]

        def phase2():
            PI = math.pi
            hgf = hg[:].rearrange("p a b -> p (a b)")
            w3_s = hgf[0:64, 0:8192].bitcast(F32)
            w1_s = hgf[0:33, 8192:8320].bitcast(F32)
            w2_s = hgf[0:64, 8320:8448].bitcast(F32)
            fp_s = hgf[0:64, 8448:8464].bitcast(F32)
            nd_s = hgf[:, 8464:8528].bitcast(F32)
            tnb_s = big[:, 0:8192].bitcast(F32)
            zf_s = big[0:33, 8192:16384].bitcast(F32)
            B_f = Buf("filt")
            P.dma('sync', w3_s, fw3, writes=[B_f])
            P.dma('sync', w1_s, fw1, writes=[B_f])
            P.dma('sync', w2_s, fw2, writes=[B_f])
            P.dma('sync', fp_s[:, 0:4], fpar, writes=[B_f])
            P.dma('sync', nd_s, decay, writes=[B_f])
            P.dma('sync', tnb_s, tnb, writes=[B_f])
            P.dma('sync', zf_s, zfT, writes=[B_f])
            P.op('vector', [
                lambda e: e.tensor_tensor(out=fp_s[:, 4:5], in0=fp_s[:, 0:1], in1=fp_s[:, 1:2], op=ALU.mult),
                lambda e: e.tensor_tensor(out=fp_s[:, 5:6], in0=fp_s[:, 2:3], in1=fp_s[:, 3:4], op=ALU.mult),
                lambda e: e.tensor_scalar(out=nd_s, in0=nd_s, scalar1=-1.0, scalar2=None, op0=ALU.mult)],
                 reads=[B_f], writes=[B_f])
            for tt in range(8):
                tsl = slice(tt * 512, (tt + 1) * 512)
                gprev = None
                for layer in range(2):
                    pa, Ba = acc()
                    if layer == 0:
                        P.op('tensor', lambda e, pa=pa, tsl=tsl: e.matmul(pa[0:64, :], lhsT=w1_s, rhs=zf_s[:, tsl],
                                                                         start=True, stop=True), reads=[B_f], writes=[Ba])
                    else:
                        P.op('tensor', lambda e, pa=pa, gp=gprev: e.matmul(pa[0:64, :], lhsT=w2_s, rhs=cva[gp][0:64, :],
                                                                          start=True, stop=True),
                             reads=[B_f, B_cva[gprev]], writes=[Ba])
                    cv = rot('cv', 2)
                    fi, fbi = (1, 4) if layer == 0 else (3, 5)
                    P.op('vector', lambda e, cv=cv, pa=pa, fi=fi, fbi=fbi: e.tensor_scalar(
                        out=cvb[cv][0:64, :], in0=pa[0:64, :], scalar1=fp_s[:, fi:fi + 1], scalar2=fp_s[:, fbi:fbi + 1],
                        op0=ALU.mult, op1=ALU.add), reads=[Ba, B_f], writes=[B_cvb[cv]])
                    z = rot('zt', 2)
                    P.op('vector', [
                        lambda e, cv=cv, z=z: e.tensor_scalar(out=zt[z][0:64, 0:512], in0=cvb[cv][0:64, :], scalar1=PI,
                                                             scalar2=-2 * PI, op0=ALU.is_gt, op1=ALU.mult),
                        lambda e, cv=cv: e.tensor_scalar(out=cva[cv][0:64, :], in0=cvb[cv][0:64, :], scalar1=-PI,
                                                        scalar2=2 * PI, op0=ALU.is_lt, op1=ALU.mult)],
                         reads=[B_cvb[cv]], writes=[B_zt[z], B_cva[cv]])
                    P.op('vector', lambda e, cv=cv, z=z: e.tensor_tensor(out=zt[z][0:64, 0:512], in0=zt[z][0:64, 0:512],
                                                                        in1=cva[cv][0:64, :], op=ALU.add),
                         reads=[B_zt[z], B_cva[cv]], writes=[B_zt[z]])
                    P.op('vector', lambda e, cv=cv, z=z: e.tensor_tensor(out=cva[cv][0:64, :], in0=cvb[cv][0:64, :],
                                                                        in1=zt[z][0:64, 0:512], op=ALU.add),
                         reads=[B_zt[z], B_cvb[cv]], writes=[B_cva[cv]])
                    P.op('scalar', lambda e, cv=cv: e.activation(out=cva[cv][0:64, :], in_=cva[cv][0:64, :], func=AF.Sin),
                         reads=[B_cva[cv]], writes=[B_cva[cv]])
                    gprev = cv
                for k in range(32):
                    pa, Ba = acc()
                    P.op('tensor', lambda e, pa=pa, k=k, gp=gprev: e.matmul(
                        pa[:, :], lhsT=w3_s[:, k * 128:(k + 1) * 128], rhs=cva[gp][0:64, :], start=True, stop=True),
                        reads=[B_f, B_cva[gprev]], writes=[Ba])
                    z = rot('zt', 2)
                    P.op('scalar', lambda e, z=z, k=k, tsl=tsl: e.activation(
                        out=zt[z][:, 0:512], in_=tnb_s[:, tsl], func=AF.Exp, scale=nd_s[:, k:k + 1]),
                        reads=[B_f], writes=[B_zt[z]])
                    o = rot('ob', NOB)
                    fns = [lambda e, o=o, pa=pa, z=z: e.tensor_tensor(out=obf[o][:], in0=pa[:, :], in1=zt[z][:, 0:512],
                                                                      op=ALU.mult)]
                    if k >= 16 and tt == 0:
                        fns.append(lambda e, o=o: e.memset(obf[o][:, 0:1], 0.0))
                    P.op('vector', fns, reads=[Ba, B_zt[z]], writes=[B_obf[o]])
                    P.dma('sync', hfb[k * 128:(k + 1) * 128, tsl], obf[o][:], reads=[B_obf[o]])
            P.barrier()
            NFA = 33
            A_sb = big[:, 0:9504].rearrange("p (r c) -> p r c", c=96)
            slab = [big[0:32, 9504 + i * 4096:9504 + (i + 1) * 4096].rearrange("p (c t) -> p c t", t=128) for i in range(3)]
            B_slab = [Buf(f"slab{i}") for i in range(3)]
            Ya = big[:, 21792:25888].rearrange("p (c r) -> p c r", r=128)
            Yb = big[:, 25888:29984].rearrange("p (c r) -> p c r", r=128)
            y_sb = big[0:32, 29984:38176].bitcast(F32).rearrange("p (c t) -> p c t", t=128)
            D_sb = hgf[:, 0:4096].rearrange("p (t c) -> p t c", c=32)
            G_s = hgf[:, 4096:4096 + 8448].rearrange("p (a s f) -> p a s f", s=2, f=128)
            MI_s = wb[0][:].rearrange("p a b -> p (a b)").rearrange("p (t h) -> p t h", h=32)
            wb1 = wb[1][:].rearrange("p a b -> p (a b)")
            GI_s = wb1[:, 0:256].rearrange("p (s t) -> p s t", s=2)
            FH_s = wb1[0:32, 256:355]
            B_A, B_Y, B_D, B_ysb, B_tab = Buf("A"), Buf("Y"), Buf("D"), Buf("ysb"), Buf("tab")
            P.dma('sync', G_s, g_t.rearrange("p (a s f) -> p a s f", s=2, f=128), writes=[B_tab])
            P.dma('sync', MI_s, mi_t.rearrange("p (t h) -> p t h", h=32), writes=[B_tab])
            P.dma('sync', GI_s, gi_t.rearrange("p (s t) -> p s t", s=2), writes=[B_tab])
            P.dma('sync', FH_s, fh_t, writes=[B_tab])
            P.op('vector', [lambda e: e.memset(Ya, 0.0), lambda e: e.memset(Yb, 0.0)], writes=[B_Y])
            for g in range(64):
                c0 = g * 32
                P.dma('gpsimd', slab[0], uT[c0:c0 + 32, :].rearrange("c (th tl) -> th c tl", tl=128), writes=[B_slab[0]])
                P.dma('sync', slab[1], hfb[c0:c0 + 32, :].rearrange("c (th tl) -> th c tl", tl=128), writes=[B_slab[1]])
                P.dma('sync', slab[2], hfb[2048 + c0:2048 + c0 + 32, :].rearrange("c (th tl) -> th c tl", tl=128),
                      writes=[B_slab[2]])
                for si in range(3):
                    for cb in range(8):
                        pa, Ba = acc()
                        fns = [lambda e, pa=pa, si=si, cb=cb, k=k: e.matmul(
                            pa[:, k * 128:k * 128 + 99], lhsT=slab[si][:, cb * 4 + k, :], rhs=FH_s, start=True, stop=True)
                            for k in range(4)]
                        P.op('tensor', fns, reads=[B_slab[si], B_tab], writes=[Ba])
                        col = si * 32 + cb * 4
                        P.op('scalar', lambda e, pa=pa, col=col: e.copy(
                            out=A_sb[:, :, col:col + 4].rearrange("p r k -> p k r"),
                            in_=pa[:, :].rearrange("p (k r) -> p k r", r=128)[:, :, 0:99]),
                            reads=[Ba], writes=[B_A])
                for fa0 in range(0, NFA, 2):
                    nf = min(2, NFA - fa0)
                    pa, Ba = acc()
                    fns = []
                    for fl in range(nf):
                        fa = fa0 + fl
                        b0 = fl * 192
                        fns.append(lambda e, pa=pa, fa=fa, b0=b0: e.matmul(pa[:, b0:b0 + 96], lhsT=G_s[:, fa, 0, :],
                                                                           rhs=A_sb[:, fa, :], start=True, stop=False))
                        fns.append(lambda e, pa=pa, fa=fa, b0=b0: e.matmul(pa[:, b0:b0 + 96], lhsT=G_s[:, fa, 1, :],
                                                                           rhs=A_sb[:, 66 + fa, :], start=False, stop=True))
                        fns.append(lambda e, pa=pa, fa=fa, b0=b0: e.matmul(pa[:, b0 + 96:b0 + 192], lhsT=G_s[:, fa, 1, :],
                                                                           rhs=A_sb[:, fa, :], start=True, stop=False))
                        fns.append(lambda e, pa=pa, fa=fa, b0=b0: e.matmul(pa[:, b0 + 96:b0 + 192], lhsT=G_s[:, fa, 0, :],
                                                                           rhs=A_sb[:, 33 + fa, :], start=False, stop=True))
                    P.op('tensor', fns, reads=[B_A, B_tab], writes=[Ba])
                    z = rot('zt', 2)
                    P.op('scalar', lambda e, z=z, pa=pa, nf=nf: e.copy(out=zt[z][:, 0:nf * 192], in_=pa[:, 0:nf * 192]),
                         reads=[Ba], writes=[B_zt[z]])
                    X = zt[z][:, 0:nf * 192].rearrange("p (f r s c) -> p f r s c", r=2, s=3, c=32)
                    cv = rot('cv', 2)
                    T = cva[cv][:, 0:384].rearrange("p (k f c) -> p k f c", k=6, c=32)[:, :, 0:nf, :]
                    U = cvb[cv][:, 0:128].rearrange("p (k f c) -> p k f c", k=2, c=32)[:, :, 0:nf, :]
                    yav = Ya[:, :, fa0:fa0 + nf].rearrange("p c f -> p f c")
                    yai = Ya[:, :, 64 + fa0:64 + fa0 + nf].rearrange("p c f -> p f c")
                    ybv = Yb[:, :, fa0:fa0 + nf].rearrange("p c f -> p f c")
                    ybi = Yb[:, :, 64 + fa0:64 + fa0 + nf].rearrange("p c f -> p f c")
                    P.op('vector', [
                        lambda e, X=X, U=U: e.tensor_tensor(out=U[:, 0], in0=X[:, :, 0, 1, :], in1=X[:, :, 0, 2, :], op=ALU.add),
                        lambda e, X=X, U=U: e.tensor_tensor(out=U[:, 1], in0=X[:, :, 1, 1, :], in1=X[:, :, 1, 2, :],
                                                           op=ALU.subtract)],
                         reads=[B_zt[z]], writes=[B_cvb[cv]])
                    P.op('vector', [
                        lambda e, X=X, U=U, T=T: e.tensor_tensor(out=T[:, 0], in0=X[:, :, 0, 0, :], in1=U[:, 0], op=ALU.mult),
                        lambda e, X=X, U=U, T=T: e.tensor_tensor(out=T[:, 1], in0=X[:, :, 1, 0, :], in1=U[:, 1], op=ALU.mult),
                        lambda e, X=X, U=U, T=T: e.tensor_tensor(out=T[:, 2], in0=X[:, :, 0, 0, :], in1=U[:, 1], op=ALU.mult),
                        lambda e, X=X, U=U, T=T: e.tensor_tensor(out=T[:, 3], in0=X[:, :, 1, 0, :], in1=U[:, 0], op=ALU.mult)],
                         reads=[B_zt[z], B_cvb[cv]], writes=[B_cva[cv]])
                    P.op('vector', [
                        lambda e, T=T, yav=yav: e.tensor_tensor(out=yav, in0=T[:, 0], in1=T[:, 1], op=ALU.subtract),
                        lambda e, T=T, ybi=ybi: e.tensor_tensor(out=ybi, in0=T[:, 0], in1=T[:, 1], op=ALU.subtract),
                        lambda e, T=T, yai=yai: e.tensor_tensor(out=yai, in0=T[:, 2], in1=T[:, 3], op=ALU.add),
                        lambda e, T=T, ybv=ybv: e.scalar_tensor_tensor(out=ybv, in0=T[:, 2], scalar=-1.0, in1=T[:, 3],
                                                                      op0=ALU.mult, op1=ALU.subtract)],
                         reads=[B_cva[cv]], writes=[B_Y])
                for cb in range(8):
                    pa, Ba = acc()
                    fns = []
                    for k in range(4):
                        c = cb * 4 + k
                        fns.append(lambda e, pa=pa, c=c, k=k: e.matmul(pa[:, k * 128:(k + 1) * 128], lhsT=Ya[:, c, :],
                                                                       rhs=GI_s[:, 0, :], start=True, stop=False))
                        fns.append(lambda e, pa=pa, c=c, k=k: e.matmul(pa[:, k * 128:(k + 1) * 128], lhsT=Yb[:, c, :],
                                                                       rhs=GI_s[:, 1, :], start=False, stop=True))
                    P.op('tensor', fns, reads=[B_Y, B_tab], writes=[Ba])
                    P.op('scalar', lambda e, pa=pa, cb=cb: e.copy(
                        out=D_sb[:, :, cb * 4:cb * 4 + 4].rearrange("p t k -> p k t"),
                        in_=pa[:, :].rearrange("p (k t) -> p k t", t=128)), reads=[Ba], writes=[B_D])
                for tb in range(8):
                    pa, Ba = acc()
                    fns = [lambda e, pa=pa, tb=tb, tl=tl: e.matmul(
                        pa[0:32, tl * 32:(tl + 1) * 32], lhsT=MI_s[:, tb * 16 + tl, :], rhs=D_sb[:, tb * 16 + tl, :],
                        start=True, stop=True) for tl in range(16)]
                    P.op('tensor', fns, reads=[B_D, B_tab], writes=[Ba])
                    P.op('vector', lambda e, pa=pa, tb=tb: e.tensor_copy(
                        out=y_sb[:, :, tb * 16:(tb + 1) * 16].rearrange("p c t -> p t c"),
                        in_=pa[0:32, :].rearrange("p (t c) -> p t c", c=32)), reads=[Ba], writes=[B_ysb])
                P.dma('sync', ycT[c0:c0 + 32, :].rearrange("c (th tl) -> th c tl", tl=128), y_sb, reads=[B_ysb])

        def phase3():
            P.dma('sync', bkt_s[:], bkt, writes=[B_c3])
            P.dma('sync', rb_s[:], rbext, writes=[B_c3])
            P.dma('sync', esink[:], sinkr, writes=[B_c3])
            P.dma('sync', kmask_s[:], kmask, writes=[B_c3])
            P.op('scalar', lambda e: e.activation(out=esink[:], in_=esink[:], func=AF.Exp), reads=[B_c3], writes=[B_c3])
            for ri, r in enumerate((-1, 0, 1)):
                for hv in range(4):
                    fns = []
                    for qi in range(128):
                        st = 384 - qi - 128 * r
                        fns.append(lambda e, qi=qi, st=st, hv=hv: e.matmul(
                            psx[:, qi:512:128], lhsT=bkt_s[:, st:st + 128], rhs=rb_s[:, hv * 4:hv * 4 + 4],
                            start=True, stop=True))
                    P.op('tensor', fns, reads=[B_c3], writes=[B_psx])
                    P.op('scalar', lambda e, ri=ri, hv=hv: e.copy(out=biasT[:, ri * 4 + hv, :], in_=psx[:, :]),
                         reads=[B_psx], writes=[B_c3])
            q4 = big[:, 0:16384].rearrange("p (g t) -> p g t", g=4)
            kh = big[:, 16384:20480]
            vh = big[:, 20480:24576].rearrange("p (b d) -> p b d", b=32)
            for hv in range(4):
                P.dma('sync', q4, qT[hv * 512:(hv + 1) * 512, 0:LT].rearrange("(g p) t -> p g t", p=128),
                      reads=[B_qT], writes=[B_big])
                P.dma('sync', kh, kT[hv * 128:(hv + 1) * 128, 0:LT], reads=[B_kT], writes=[B_big])
                P.dma('sync', vh, vS[:, hv * 128:(hv + 1) * 128].rearrange("(b p) d -> p b d", p=128),
                      reads=[B_vS], writes=[B_big])
                for i in range(32):
                    js = [j for j in (i - 1, i, i + 1) if 0 <= j < 32]
                    par = i % 2
                    ps_o, B_o = (pss, B_pss) if par == 0 else (psx, B_psx)
                    ps_d, B_d = (pss2, B_pss2) if par == 0 else (psh, B_psh[0])
                    for jn, j in enumerate(js):
                        ri = (i - j) + 1
                        pm = rot('pm', 3)
                        P.op('tensor', [
                            lambda e, j=j, i=i, pm=pm: e.matmul(psm[pm][:, :], lhsT=kh[:, j * 128:(j + 1) * 128],
                                                                 rhs=q4[:, :, i * 128:(i + 1) * 128], start=True, stop=False),
                            lambda e, ri=ri, hv=hv, pm=pm: e.matmul(psm[pm][:, :], lhsT=ident[:], rhs=biasT[:, ri * 4 + hv, :],
                                                                    start=False, stop=True)],
                             reads=[B_big, B_c3, B_const], writes=[B_psm[pm]])
                        o = rot('ob', NOB)
                        P.op('scalar', lambda e, o=o, pm=pm, j=j: e.activation(
                            out=obf[o][:], in_=psm[pm][:, :], func=AF.Exp, bias=kmask_s[:, j:j + 1], scale=1.0),
                            reads=[B_psm[pm], B_c3], writes=[B_obf[o]])
                        P.op('tensor', [
                            lambda e, o=o, j=j, jn=jn, ps_o=ps_o: e.matmul(ps_o[:, :], lhsT=vh[:, j, :], rhs=obf[o][:],
                                                                           start=(jn == 0), stop=(jn == len(js) - 1)),
                            lambda e, o=o, jn=jn, ps_d=ps_d: e.matmul(ps_d[:, :], lhsT=ones_b[:], rhs=obf[o][:],
                                                                      start=(jn == 0), stop=(jn == len(js) - 1))],
                             reads=[B_obf[o], B_big, B_const], writes=[B_o, B_d])
                    cv = rot('cv', 2)
                    P.op('vector', [lambda e, g=g, cv=cv, ps_d=ps_d, hv=hv: e.tensor_scalar(
                        out=cva[cv][:, g * 128:(g + 1) * 128], in0=ps_d[:, g * 128:(g + 1) * 128],
                        scalar1=esink[:, hv * 4 + g:hv * 4 + g + 1], scalar2=None, op0=ALU.add) for g in range(4)],
                         reads=[B_d, B_c3], writes=[B_cva[cv]])
                    P.op('vector', lambda e, cv=cv: e.reciprocal(out=cvb[cv][:], in_=cva[cv][:]),
                         reads=[B_cva[cv]], writes=[B_cvb[cv]])
                    ao = rot('ao', 2)
                    P.op('vector', lambda e, cv=cv, ao=ao, ps_o=ps_o: e.tensor_tensor(
                        out=att_o[ao][:], in0=ps_o[:, :], in1=cvb[cv][:], op=ALU.mult),
                        reads=[B_o, B_cvb[cv]], writes=[B_att_o[ao]])
                    P.dma('sync', attnT[hv * 512:(hv + 1) * 512, i * 128:(i + 1) * 128].rearrange("(g d) q -> d g q", d=128),
                          att_o[ao][:].rearrange("d (g q) -> d g q", g=4), reads=[B_att_o[ao]], writes=[B_attnT])

        def phase4(use_yc):
            at_s = big[:, 0:8192].rearrange("p (c t) -> p c t", c=16)
            hy_s = big[:, 8192:16384].rearrange("p (c t) -> p c t", c=16)
            m_s = big[:, 16384:32768].rearrange("p (c t) -> p c t", c=32)
            B_at, B_hy, B_m = Buf("at"), Buf("hy"), Buf("m")
            for half in range(2):
                P.dma('sync', x1T[:, half * (LT + 1):half * (LT + 1) + 1].rearrange("(kc p) o -> p kc o", p=128),
                      zero_c[:].rearrange("p (k o) -> p k o", o=1), reads=[B_const], writes=[B_x1T], allow_slow_non_contiguous=True)
            for ti in range(NT):
                s = ti * 512
                prep(xT, s, 0)
                P.dma('sync', vmask[:], valid[:, s:s + TW], writes=[B_vmask])
                P.dma('sync', at_s, attnT[:, s:s + 512].rearrange("(c p) t -> p c t", p=128), reads=[B_attnT], writes=[B_at])
                for c in range(16):
                    z = rot('zt', 2)
                    cv = rot('cv', 2)
                    P.dma('sync', cva[cv][:], uT[c * 128:(c + 1) * 128, s:s + 512], reads=[B_uT], writes=[B_cva[cv]])
                    P.dma('sync', cvb[cv][:], hx0T[c * 128:(c + 1) * 128, s:s + 512], reads=[B_hx0T], writes=[B_cvb[cv]])
                    if use_yc:
                        P.dma('sync', zt[z][:, 0:512], ycT[c * 128:(c + 1) * 128, s:s + 512], reads=[B_ycT], writes=[B_zt[z]])
                        P.op('vector', lambda e, z=z, cv=cv, c=c: e.scalar_tensor_tensor(
                            out=zt[z][:, 0:512], in0=cva[cv][:], scalar=hskip_s[:, c:c + 1], in1=zt[z][:, 0:512],
                            op0=ALU.mult, op1=ALU.add), reads=[B_cva[cv], B_zt[z], B_const], writes=[B_zt[z]])
                    else:
                        P.op('vector', lambda e, z=z, cv=cv, c=c: e.tensor_scalar(
                            out=zt[z][:, 0:512], in0=cva[cv][:], scalar1=hskip_s[:, c:c + 1], scalar2=None,
                            op0=ALU.mult), reads=[B_cva[cv], B_const], writes=[B_zt[z]])
                    P.op('vector', lambda e, z=z, cv=cv, c=c: e.tensor_tensor(
                        out=hy_s[:, c, :], in0=zt[z][:, 0:512], in1=cvb[cv][:], op=ALU.mult),
                        reads=[B_zt[z], B_cvb[cv]], writes=[B_hy])
                for j in range(32):
                    c0 = j * 128
                    s_a = wload(w_ao[:, c0:c0 + 128], 16, ('ao', j))
                    s_h = wload(w_ho[:, c0:c0 + 128], 16, ('ho', j))
                    s_ga = wload(w_in[:, 9216 + c0:9216 + c0 + 128], 32, ('ga', j))
                    s_gh = wload(w_in[:, 13312 + c0:13312 + c0 + 128], 32, ('gh', j))
                    pa, Ba = acc()
                    mm_group(pa[:, :], Ba, [(s_a, 16, 0)], lambda kc: at_s[:, kc, :], [B_at])
                    ph_, Bh = acc()
                    mm_group(ph_[:, :], Bh, [(s_h, 16, 0)], lambda kc: hy_s[:, kc, :], [B_hy])
                    pga, Bga = acc()
                    mm_group(pga[:, :], Bga, [(s_ga, 32, 0)], lambda kc: hg[:, kc, 1:513], [B_hg])
                    pgh, Bgh = acc()
                    mm_group(pgh[:, :], Bgh, [(s_gh, 32, 0)], lambda kc: hg[:, kc, 1:513], [B_hg])
                    res = []
                    for (pg, Bg, pp, Bp) in ((pga, Bga, pa, Ba), (pgh, Bgh, ph_, Bh)):
                        z = rot('zt', 2)
                        P.op('vector', lambda e, z=z, pg=pg: e.tensor_tensor(
                            out=zt[z][:, 0:512], in0=pg[:, :], in1=rstd[:, 1:513], op=ALU.mult),
                            reads=[Bg, B_rstd], writes=[B_zt[z]])
                        P.op('scalar', lambda e, z=z: e.activation(out=zt[z][:, 0:512], in_=zt[z][:, 0:512], func=AF.Sigmoid),
                             reads=[B_zt[z]], writes=[B_zt[z]])
                        cv = rot('cv', 2)
                        P.op('vector', lambda e, z=z, cv=cv, pp=pp: e.tensor_tensor(
                            out=cva[cv][:], in0=zt[z][:, 0:512], in1=pp[:, :], op=ALU.mult),
                            reads=[B_zt[z], Bp], writes=[B_cva[cv]])
                        res.append(cv)
                    P.op('vector', lambda e, j=j, a=res[0], b=res[1]: e.tensor_tensor(
                        out=m_s[:, j, :], in0=cva[a][:], in1=cva[b][:], op=ALU.add),
                        reads=[B_cva[res[0]], B_cva[res[1]]], writes=[B_m])
                for j in range(32):
                    c0 = j * 128
                    s_o = wload(w_out[:, c0:c0 + 128], 32, ('wo', j))
                    po, Bo = acc()
                    mm_group(po[:, :], Bo, [(s_o, 32, 0)], lambda kc: m_s[:, kc, :], [B_m])
                    cv = rot('cv', 2)
                    P.dma('sync', cvb[cv][:], xT[c0:c0 + 128, 1 + s:1 + s + 512], writes=[B_cvb[cv]])
                    P.op('vector', lambda e, cv=cv, po=po: e.tensor_tensor(
                        out=cva[cv][:], in0=po[:, :], in1=cvb[cv][:], op=ALU.add),
                        reads=[Bo, B_cvb[cv]], writes=[B_cva[cv]])
                    o = rot('of', NOB)
                    P.op('vector', lambda e, cv=cv, o=o: e.tensor_tensor(
                        out=of32[o][:], in0=cva[cv][:], in1=vmask[:, 1:513], op=ALU.mult),
                        reads=[B_cva[cv], B_vmask], writes=[B_of32[o]])
                    P.dma('sync', x1T[c0:c0 + 128, 1 + s:1 + s + 512], of32[o][:], reads=[B_of32[o]], writes=[B_x1T])

        def phase5():
            act_s = big[:, :].rearrange("p (c t) -> p c t", c=NFF)
            B_act = Buf("act")
            for ti in range(NT):
                s = ti * 512
                prep(x1T, s, 1)
                for i in range(NFF):
                    c0 = i * 128
                    s_g = wload(w_up[:, c0:c0 + 128], 32, ('ug', i))
                    s_v = wload(w_up[:, DFF + c0:DFF + c0 + 128], 32, ('uv', i))
                    pg, Bg = acc()
                    mm_group(pg[:, :], Bg, [(s_g, 32, 0)], lambda kc: hg[:, kc, 1:513], [B_hg])
                    ph = rot('ph', 8)
                    mm_group(psh[:, ph * 4:ph * 4 + 2], B_psh[ph], [(s_g, 32, 0)], lambda kc: hg[:, kc, 0:TW:513], [B_hg])
                    pv, Bv = acc()
                    mm_group(pv[:, :], Bv, [(s_v, 32, 0)], lambda kc: hg[:, kc, 1:513], [B_hg])
                    z = rot('zt', 2)
                    P.op('vector', [
                        lambda e, z=z, pg=pg: e.tensor_tensor(out=zt[z][:, 1:513], in0=pg[:, :], in1=rstd[:, 1:513], op=ALU.mult),
                        lambda e, z=z, ph=ph: e.tensor_tensor(out=zt[z][:, 0:TW:513], in0=psh[:, ph * 4:ph * 4 + 2],
                                                             in1=rstd[:, 0:TW:513], op=ALU.mult)],
                         reads=[Bg, B_psh[ph], B_rstd], writes=[B_zt[z]])
                    cv = rot('cv', 2)
                    P.op('vector', lambda e, z=z, cv=cv, i=i: e.tensor_scalar(
                        out=cva[cv][:], in0=zt[z][:, 0:512], scalar1=fcw_s[:, i, 0:1], scalar2=fcw_s[:, i, 3:4],
                        op0=ALU.mult, op1=ALU.add), reads=[B_zt[z], B_const], writes=[B_cva[cv]])
                    P.op('vector', lambda e, z=z, cv=cv, i=i: e.scalar_tensor_tensor(
                        out=cvb[cv][:], in0=zt[z][:, 1:513], scalar=fcw_s[:, i, 1:2], in1=cva[cv][:], op0=ALU.mult,
                        op1=ALU.add), reads=[B_zt[z], B_cva[cv], B_const], writes=[B_cvb[cv]])
                    P.op('vector', lambda e, z=z, cv=cv, i=i: e.scalar_tensor_tensor(
                        out=cva[cv][:], in0=zt[z][:, 2:514], scalar=fcw_s[:, i, 2:3], in1=cvb[cv][:], op0=ALU.mult,
                        op1=ALU.add), reads=[B_zt[z], B_cvb[cv], B_const], writes=[B_cva[cv]])
                    P.op('scalar', lambda e, cv=cv: e.activation(out=cvb[cv][:], in_=cva[cv][:], func=AF.Gelu),
                         reads=[B_cva[cv]], writes=[B_cvb[cv]])
                    o = rot('of', NOB)
                    P.op('vector', lambda e, o=o, pv=pv: e.tensor_tensor(
                        out=of32[o][:], in0=pv[:, :], in1=rstd[:, 1:513], op=ALU.mult),
                        reads=[Bv, B_rstd], writes=[B_of32[o]])
                    P.op('vector', lambda e, o=o, cv=cv, i=i: e.tensor_tensor(
                        out=act_s[:, i, :], in0=of32[o][:], in1=cvb[cv][:], op=ALU.mult),
                        reads=[B_of32[o], B_cvb[cv]], writes=[B_act])
                for j in range(32):
                    c0 = j * 128
                    pcs = [(wload(w_down[kb * 128:(kb + n) * 128, c0:c0 + 128], n, ('dn', j, kb)), n, kb)
                           for (kb, n) in ((0, 32), (32, 32), (64, 22))]
                    pd, Bd = acc()
                    mm_group(pd[:, :], Bd, pcs, lambda kc: act_s[:, kc, :], [B_act])
                    cv = rot('cv', 2)
                    P.dma('sync', cvb[cv][:], x1T[c0:c0 + 128, 1 + s:1 + s + 512], reads=[B_x1T], writes=[B_cvb[cv]])
                    o = rot('of', NOB)
                    P.op('vector', lambda e, cv=cv, pd=pd, o=o: e.tensor_tensor(
                        out=of32[o][:], in0=pd[:, :], in1=cvb[cv][:], op=ALU.add),
                        reads=[Bd, B_cvb[cv]], writes=[B_of32[o]])
                    P.dma('sync', x2T[c0:c0 + 128, 1 + s:1 + s + 512], of32[o][:], reads=[B_of32[o]], writes=[B_x2T])

        def phase6():
            x3_s = big[:, 0:32768].bitcast(F32).rearrange("p (c t) -> p c t", c=32)
            B_x3 = Buf("x3")
            for ti in range(NT):
                s = ti * 512
                prep(x2T, s, 2)
                for kc in range(2):
                    cv = rot('cv', 2)
                    P.dma('sync', cva[cv][:], pT[kc * 128:(kc + 1) * 128, s:s + 512], writes=[B_cva[cv]])
                    P.op('scalar', lambda e, cv=cv, kc=kc: e.copy(out=pb_s[:, kc, :], in_=cva[cv][:]),
                         reads=[B_cva[cv]], writes=[B_pb])
                for j in range(32):
                    c0 = j * 128
                    s_g = wload(w_pg[:, c0:c0 + 128], 32, ('pg', j))
                    s_p = wload(w_ple[:, c0:c0 + 128], 2, ('pl', j))
                    pg, Bg = acc()
                    mm_group(pg[:, :], Bg, [(s_g, 32, 0)], lambda kc: hg[:, kc, 1:513], [B_hg])
                    pp, Bp = acc()
                    mm_group(pp[:, :], Bp, [(s_p, 2, 0)], lambda kc: pb_s[:, kc, :], [B_pb])
                    z = rot('zt', 2)
                    P.op('vector', lambda e, z=z, pg=pg: e.tensor_tensor(
                        out=zt[z][:, 0:512], in0=pg[:, :], in1=rstd[:, 1:513], op=ALU.mult),
                        reads=[Bg, B_rstd], writes=[B_zt[z]])
                    P.op('scalar', lambda e, z=z: e.activation(out=zt[z][:, 0:512], in_=zt[z][:, 0:512], func=AF.Sigmoid),
                         reads=[B_zt[z]], writes=[B_zt[z]])
                    cv = rot('cv', 2)
                    P.op('vector', lambda e, z=z, cv=cv, pp=pp: e.tensor_tensor(
                        out=cva[cv][:], in0=zt[z][:, 0:512], in1=pp[:, :], op=ALU.mult),
                        reads=[B_zt[z], Bp], writes=[B_cva[cv]])
                    P.dma('sync', cvb[cv][:], x2T[c0:c0 + 128, 1 + s:1 + s + 512], reads=[B_x2T], writes=[B_cvb[cv]])
                    P.op('vector', lambda e, cv=cv, j=j: e.tensor_tensor(
                        out=x3_s[:, j, :], in0=cva[cv][:], in1=cvb[cv][:], op=ALU.add),
                        reads=[B_cva[cv], B_cvb[cv]], writes=[B_x3])
                    o = rot('ob', NOB)
                    P.op('scalar', lambda e, o=o, j=j: e.activation(out=obf[o][:], in_=x3_s[:, j, :], func=AF.Square),
                         reads=[B_x3], writes=[B_obf[o]])
                    P.op('tensor', lambda e, o=o, j=j: e.matmul(pss[:, :], lhsT=ones_b[:], rhs=obf[o][:],
                                                                start=(j == 0), stop=(j == 31)),
                         reads=[B_obf[o], B_const], writes=[B_pss])
                P.op('scalar', lambda e: e.activation(out=rt[:, 1:513], in_=pss[:, :], func=AF.Sqrt, bias=eps_t[:, 0:1],
                                                      scale=1.0 / D), reads=[B_pss, B_const], writes=[B_rt])
                P.op('vector', lambda e: e.reciprocal(out=rstd[:, 1:513], in_=rt[:, 1:513]), reads=[B_rt], writes=[B_rstd])
                for j in range(32):
                    o = rot('of', NOB)
                    P.op('vector', lambda e, o=o, j=j: e.scalar_tensor_tensor(
                        out=of32[o][:], in0=x3_s[:, j, :], scalar=gains_s[:, 3, j:j + 1], in1=rstd[:, 1:513],
                        op0=ALU.mult, op1=ALU.mult), reads=[B_x3, B_rstd, B_const], writes=[B_of32[o]])
                    P.dma('sync', yT[j * 128:(j + 1) * 128, s:s + 512], of32[o][:], reads=[B_of32[o]])

        for ph_name in ('p1', 'p2', 'p3', 'p4', 'p5', 'p6'):
            if ph_name not in phases:
                continue
            if ph_name == 'p1':
                phase1()
            elif ph_name == 'p2':
                phase2()
            elif ph_name == 'p3':
                phase3()
            elif ph_name == 'p4':
                phase4('p2' in phases)
            elif ph_name == 'p5':
                phase5()
            elif ph_name == 'p6':
                phase6()
            P.barrier()

        P.finish()

        @block.sync
        def _(e):
            for f in P.ops['sync']:
                f(e)

        @block.scalar
        def _(e):
            for f in P.ops['scalar']:
                f(e)

        @block.vector
        def _(e):
            for f in P.ops['vector']:
                f(e)

        @block.gpsimd
        def _(e):
            for f in P.ops['gpsimd']:
                f(e)

        @block.tensor
        def _(e):
            for f in P.ops['tensor']:
                f(e)
    return nc


def _t5_bucket(rel):
    half = 16
    max_exact = 8
    ret = np.where(rel > 0, half, 0)
    n = np.abs(rel)
    nf = np.maximum(n, 1).astype(np.float32)
    large = max_exact + (np.log(nf / max_exact) / math.log(128 / max_exact) * (half - max_exact)).astype(np.int32)
    large = np.minimum(large, half - 1)
    return ret + np.where(n < max_exact, n, large)


def _chunked(v, n):
    return np.ascontiguousarray(np.asarray(v, np.float32).reshape(n, 128).T)


def make_inputs(inp, core):
    f32 = np.float32
    if core < 4:
        x = np.asarray(inp['x_prompt'][core]); p = np.asarray(inp['p_prompt'][0, core]); L = 4096
    else:
        x = np.asarray(inp['x_sample'][core - 4]); p = np.asarray(inp['p_sample'][0, core - 4]); L = 2048
    xTp = np.zeros((D, LT + 2), f32)
    xTp[:, 1:L + 1] = x.T
    pTp = np.zeros((256, LT), f32)
    pTp[:, :L] = p.T
    valid = np.zeros((128, LT + 2), f32)
    valid[:, 1:L + 1] = 1.0
    kmask = np.zeros((128, 32), f32)
    tok = np.arange(32)[None, :] * 128 + np.arange(128)[:, None]
    kmask[tok >= L] = NEG
    pos = np.arange(LT, dtype=f32)
    t = (pos / f32(L - 1)).astype(f32)
    bands = np.linspace(1e-4, 15, 16, dtype=f32)
    ang = (f32(2.0 * math.pi / L) * pos[:, None] * bands[None, :]).astype(f32)
    zf = np.concatenate([t[:, None], np.cos(ang), -np.sin(ang)], axis=-1).astype(f32)
    m = {'xT': xTp, 'pT': pTp, 'valid': valid, 'kmask': kmask,
         'zfT': np.ascontiguousarray(zf.T), 'tnb': np.ascontiguousarray(np.broadcast_to(t[None, :], (128, LT)))}
    return m


def shared_inputs(inp):
    f32 = np.float32
    m = {}
    m['w_in'] = np.asarray(inp['w_in'][0]); m['w_attn_o'] = np.asarray(inp['w_attn_o'][0])
    m['w_hyena_o'] = np.asarray(inp['w_hyena_o'][0]); m['w_out'] = np.asarray(inp['w_out'][0])
    m['w_up'] = np.asarray(inp['w_up'][0]); m['w_down'] = np.asarray(inp['w_down'][0])
    m['w_ple_gate'] = np.asarray(inp['w_ple_gate'][0]); m['w_ple'] = np.asarray(inp['w_ple'][0])
    g = np.stack([_chunked(inp['g_mix'][0], 32), _chunked(inp['g_ffn'][0], 32), _chunked(inp['g_ple'][0], 32),
                  _chunked(inp['g_final'], 32)], axis=1)
    m['gains'] = np.ascontiguousarray(g)
    hw = np.asarray(inp['hy_short_w'][0]); hb = np.asarray(inp['hy_short_b'][0])
    m['hcw'] = np.ascontiguousarray(np.stack([_chunked(hw[0], 48), _chunked(hw[1], 48), _chunked(hw[2], 48),
                                              _chunked(hb, 48)], axis=2))
    fw = np.asarray(inp['ffn_conv_w'][0]); fb = np.asarray(inp['ffn_conv_b'][0])
    m['fcw'] = np.ascontiguousarray(np.stack([_chunked(fw[0], NFF), _chunked(fw[1], NFF), _chunked(fw[2], NFF),
                                              _chunked(fb, NFF)], axis=2))
    m['sinkr'] = np.ascontiguousarray(np.broadcast_to(np.asarray(inp['attn_sink'][0], f32)[None, :], (128, 16)))
    m['rbext'] = np.concatenate([np.asarray(inp['rel_bias'], f32), np.full((1, 16), NEG, f32)], axis=0)
    d = np.arange(768) - 384
    bk = _t5_bucket(d)
    T = np.zeros((33, 768), f32)
    T[bk, np.arange(768)] = 1.0
    T[32, :] = (np.abs(d) > 128).astype(f32)
    m['bkt'] = T
    m['identb'] = np.eye(128, dtype=f32).astype(ml_dtypes.bfloat16)
    m['hskip'] = _chunked(inp['hy_skip'][0], 16)
    m['fw1'] = np.asarray(inp['hy_filt_w1'][0], f32); m['fw2'] = np.asarray(inp['hy_filt_w2'][0], f32)
    m['fw3'] = np.asarray(inp['hy_filt_w3'][0], f32)
    m['fpar'] = np.ascontiguousarray(np.stack([inp['hy_filt_b1'][0], inp['hy_filt_f1'][0], inp['hy_filt_b2'][0],
                                               inp['hy_filt_f2'][0]], axis=1).astype(f32))
    m['decay'] = _chunked(np.asarray(inp['hy_decay'][0]).reshape(-1), 32)
    m.update(_fft_tables())
    return m


def _fft_tables():
    bf = ml_dtypes.bfloat16
    N = 8192
    th = np.arange(32)[:, None]; fa = np.arange(33)[None, :]
    a = 2 * np.pi * th * fa / 64.0
    fh = np.concatenate([np.cos(a), -np.sin(a), np.sin(a)], axis=1)
    tl = np.arange(128)[:, None, None]; fa3 = np.arange(33)[None, :, None]; fb = np.arange(128)[None, None, :]
    ph = 2 * np.pi * ((tl * (fa3 + 64 * fb)) % N) / N
    g = np.stack([np.cos(ph), -np.sin(ph)], axis=2)
    fbp = np.arange(128)[:, None]; tlo = np.arange(128)[None, :]
    p2 = 2 * np.pi * ((fbp * tlo) % 128) / 128.0
    gi = np.stack([np.cos(p2), np.sin(p2)], axis=1)
    fa_ = np.arange(64)[:, None, None]; tl_ = np.arange(128)[None, :, None]; th_ = np.arange(32)[None, None, :]
    th3 = 2 * np.pi * ((fa_ * (128 * th_ + tl_)) % N) / N
    w = np.zeros(64); w[0] = 1; w[32] = 1; w[1:32] = 2
    mi = np.concatenate([w[:, None, None] * np.cos(th3) / N, -w[:, None, None] * np.sin(th3) / N], axis=0)
    return {'fh_t': fh.astype(np.float32).astype(bf), 'g_t': g.reshape(128, -1).astype(np.float32).astype(bf),
            'gi_t': gi.reshape(128, -1).astype(np.float32).astype(bf), 'mi_t': mi.reshape(128, -1).astype(np.float32).astype(bf)}


_NC_CACHE = {}


def kernel(**inputs):
    key = 'full'
    if key not in _NC_CACHE:
        _NC_CACHE[key] = build()
    nc = _NC_CACHE[key]
    sh = shared_inputs(inputs)
    in_maps = []
    for c in range(8):
        m = dict(sh)
        m.update(make_inputs(inputs, c))
        in_maps.append(m)
    names = set()
    for alloc in nc.allocations:
        if isinstance(alloc, mybir.MemoryLocationSet) and alloc.kind == "ExternalInput":
            names.add(alloc.memorylocations[0].name)
    in_maps = [{k: v for k, v in m.items() if k in names} for m in in_maps]
    res = run_bass_kernel_spmd(nc, in_maps, core_ids=list(range(8)))
    yp = np.stack([res.results[c]['yT'].T for c in range(4)], axis=0).astype(np.float32)
    ys = np.stack([res.results[c]['yT'][:, :2048].T for c in range(4, 8)], axis=0).astype(np.float32)
    return (np.ascontiguousarray(yp), np.ascontiguousarray(ys))
```

```python
import math
from contextlib import ExitStack
import numpy as np
import ml_dtypes
import concourse.bass as bass
import concourse.mybir as mybir
from concourse.bass_utils import run_bass_kernel_spmd

F32 = mybir.dt.float32
BF16 = mybir.dt.bfloat16
AF = mybir.ActivationFunctionType
ALU = mybir.AluOpType

D = 4096
LT = 4096
NT = 8
TW = 514
DFF = 11008
NFF = 86
IN_COLS = 17408
EPS = 1e-6
NEG = -1e30
ENGS = ['sync', 'scalar', 'vector', 'gpsimd', 'tensor']


class Buf:
    __slots__ = ('name', 'w', 'r', 'track')

    def __init__(self, name, track=True):
        self.name = name
        self.w = {}
        self.r = {}
        self.track = track


class Prog:
    def __init__(self, nc, es):
        self.nc = nc
        self.es = es
        self.ops = {e: [] for e in ENGS}
        self.waited = {e: {} for e in ENGS}
        self.esem = {}
        self.ecnt = {}
        for e in ['scalar', 'vector', 'tensor', 'gpsimd']:
            self.esem[e] = es.enter_context(nc.semaphore('es_' + e))
            self.ecnt[e] = 0
        self.dpool = {}
        self.dnext = {}
        for q, n in [('sync', 14), ('gpsimd', 8)]:
            self.dpool[q] = [[es.enter_context(nc.semaphore(f'd_{q}{i}')), 0] for i in range(n)]
            self.dnext[q] = 0
        self.sid = {}

    def _sid(self, h):
        return id(h)

    def _need(self, eng, reads, writes):
        need = {}
        own = self._sid(self.esem[eng]) if eng in self.esem else None

        def add(d, skip_own):
            for sid, (h, v) in d.items():
                if skip_own and sid == own:
                    continue
                if sid not in need or need[sid][1] < v:
                    need[sid] = (h, v)
        for b in reads:
            if b.track:
                add(b.w, False)
        for b in writes:
            if b.track:
                add(b.r, True)
                add(b.w, True)
        wd = self.waited[eng]
        for sid, (h, v) in need.items():
            if wd.get(sid, 0) < v:
                wd[sid] = v
                self.ops[eng].append(lambda e, h=h, v=v: e.wait_ge(h, v))

    def _mark(self, reads, writes, h, v):
        sid = self._sid(h)
        for b in reads:
            if b.track:
                b.r[sid] = (h, v)
        for b in writes:
            if b.track:
                b.w = {sid: (h, v)}
                b.r = {}

    def op(self, eng, fns, reads=(), writes=()):
        if not isinstance(fns, (list, tuple)):
            fns = [fns]
        self._need(eng, reads, writes)
        self.ecnt[eng] += 1
        v = self.ecnt[eng]
        h = self.esem[eng]
        for f in fns[:-1]:
            self.ops[eng].append(f)
        last = fns[-1]
        self.ops[eng].append(lambda e, last=last, h=h: last(e).then_inc(h, 1))
        self._mark(reads, writes, h, v)

    def dma(self, q, out, in_, reads=(), writes=(), **kw):
        self._need(q, reads, writes)
        pool = self.dpool[q]
        i = self.dnext[q]
        self.dnext[q] = (i + 1) % len(pool)
        h, cnt = pool[i]
        wd = self.waited[q]
        sid = self._sid(h)
        if cnt > 0 and wd.get(sid, 0) < cnt:
            wd[sid] = cnt
            self.ops[q].append(lambda e, h=h, v=cnt: e.wait_ge(h, v))
        cnt += 16
        pool[i][1] = cnt
        self.ops[q].append(lambda e, out=out, in_=in_, h=h, kw=kw: e.dma_start(out=out, in_=in_, **kw).then_inc(h, 16))
        self._mark(reads, writes, h, cnt)

    def barrier(self):
        deps = []
        for e2, h in self.esem.items():
            if self.ecnt[e2] > 0:
                deps.append((h, self.ecnt[e2]))
        for q in self.dpool:
            for h, cnt in self.dpool[q]:
                if cnt > 0:
                    deps.append((h, cnt))
        for eng in ENGS:
            wd = self.waited[eng]
            for h, v in deps:
                sid = self._sid(h)
                if wd.get(sid, 0) < v:
                    wd[sid] = v
                    self.ops[eng].append(lambda e, h=h, v=v: e.wait_ge(h, v))

    def finish(self):
        for q in self.dpool:
            for h, cnt in self.dpool[q]:
                if cnt > 0:
                    self.ops[q].append(lambda e, h=h, v=cnt: e.wait_ge(h, v))
        for q in ['sync']:
            for e2 in self.esem:
                if self.ecnt[e2] > 0:
                    self.ops[q].append(lambda e, h=self.esem[e2], v=self.ecnt[e2]: e.wait_ge(h, v))
            for h, cnt in self.dpool['gpsimd']:
                if cnt > 0:
                    self.ops[q].append(lambda e, h=h, v=cnt: e.wait_ge(h, v))


def build(phases=('p1', 'p2', 'p3', 'p4', 'p5', 'p6'), dbg=()):
    nc = bass.Bass("TRN2", target_bir_lowering=False)

    def din(name, shape, dt=F32):
        return nc.dram_tensor(name, list(shape), dt, kind="ExternalInput").ap()

    def dscr(name, shape, dt=F32):
        kind = "ExternalOutput" if name in dbg else "Internal"
        return nc.dram_tensor(name, list(shape), dt, kind=kind).ap()

    need = set(phases)
    xT = din("xT", [D, LT + 2])
    valid = din("valid", [128, LT + 2])
    gains = din("gains", [128, 4, 32])
    identb = din("identb", [128, 128], BF16)
    w_in = din("w_in", [D, IN_COLS]) if need & {'p1', 'p4'} else None
    hcw = din("hcw", [128, 48, 4]) if 'p1' in need else None
    kmask = din("kmask", [128, 32]) if 'p3' in need else None
    sinkr = din("sinkr", [128, 16]) if 'p3' in need else None
    rbext = din("rbext", [33, 16]) if 'p3' in need else None
    bkt = din("bkt", [33, 768]) if 'p3' in need else None
    w_ao = din("w_attn_o", [2048, D]) if 'p4' in need else None
    w_ho = din("w_hyena_o", [2048, D]) if 'p4' in need else None
    w_out = din("w_out", [D, D]) if 'p4' in need else None
    hskip = din("hskip", [128, 16]) if 'p4' in need else None
    if 'p2' in need:
        zfT = din("zfT", [33, LT]); tnb = din("tnb", [128, LT])
        fw1 = din("fw1", [33, 64]); fw2 = din("fw2", [64, 64]); fw3 = din("fw3", [64, 4096])
        fpar = din("fpar", [64, 4]); decay = din("decay", [128, 32])
        fh_t = din("fh_t", [32, 99], BF16); g_t = din("g_t", [128, 33 * 256], BF16)
        gi_t = din("gi_t", [128, 256], BF16); mi_t = din("mi_t", [128, 4096], BF16)
    w_up = din("w_up", [D, 2 * DFF]) if 'p5' in need else None
    w_down = din("w_down", [DFF, D]) if 'p5' in need else None
    fcw = din("fcw", [128, NFF, 4]) if 'p5' in need else None
    w_pg = din("w_ple_gate", [D, D]) if 'p6' in need else None
    w_ple = din("w_ple", [256, D]) if 'p6' in need else None
    pT = din("pT", [256, LT]) if 'p6' in need else None
    yT = nc.dram_tensor("yT", [D, LT], F32, kind="ExternalOutput").ap()

    qT = dscr("qT", [2048, LT], BF16)
    kT = dscr("kT", [512, LT], BF16)
    vS = dscr("vS", [LT, 512], BF16)
    uT = dscr("uT", [2048, LT])
    hx0T = dscr("hx0T", [2048, LT])
    ycT = dscr("ycT", [2048, LT])
    attnT = dscr("attnT", [2048, LT], BF16)
    x1T = dscr("x1T", [D, LT + 2])
    x2T = dscr("x2T", [D, LT + 2])
    hfb = dscr("hfb", [4096, LT], BF16)
    NPIECE = 72 + 160 + 268 + 64
    wcaches = [dscr(f"wcache{i}", [128, 128, 4096], BF16) for i in range((NPIECE + 127) // 128)]

    B_qT, B_kT, B_vS, B_uT, B_hx0T, B_ycT, B_attnT, B_x1T, B_x2T = [Buf(n, track=False) for n in ('qT','kT','vS','uT','hx0T','ycT','attnT','x1T','x2T')]
    with ExitStack() as es:
        E = es.enter_context
        P = Prog(nc, es)

        def sb(name, shape, dt):
            return E(nc.sbuf_tensor(name, list(shape), dt))

        xin = [sb(f"xin{i}", [128, 2, TW], F32) for i in range(2)]
        B_xin = [Buf(f"xin{i}") for i in range(2)]
        sq = [sb(f"sq{i}", [128, 2, TW], BF16) for i in range(2)]
        B_sq = [Buf(f"sq{i}") for i in range(2)]
        hg = sb("hg", [128, 32, TW], BF16)
        B_hg = Buf("hg")
        rt = sb("rt", [128, TW], F32)
        B_rt = Buf("rt")
        rstd = sb("rstd", [128, TW], F32)
        B_rstd = Buf("rstd")
        NWB = 4
        wb = [sb(f"wb{i}", [128, 32, 128], BF16) for i in range(NWB)]
        B_wb = [Buf(f"wb{i}") for i in range(NWB)]
        NOB = 3
        obf = [sb(f"obf{i}", [128, 512], BF16) for i in range(NOB)]
        B_obf = [Buf(f"obf{i}") for i in range(NOB)]
        of32 = [sb(f"of{i}", [128, 512], F32) for i in range(NOB)]
        B_of32 = [Buf(f"of{i}") for i in range(NOB)]
        zt = [sb(f"zt{i}", [128, TW], F32) for i in range(2)]
        B_zt = [Buf(f"zt{i}") for i in range(2)]
        cva = [sb(f"cva{i}", [128, 512], F32) for i in range(2)]
        B_cva = [Buf(f"cva{i}") for i in range(2)]
        cvb = [sb(f"cvb{i}", [128, 512], F32) for i in range(2)]
        B_cvb = [Buf(f"cvb{i}") for i in range(2)]
        hvc = sb("hvc", [128, 512], F32)
        B_hvc = Buf("hvc")
        vmask = sb("vmask", [128, TW], F32)
        B_vmask = Buf("vmask")
        big = sb("big", [128, NFF * 512], BF16)
        B_big = Buf("big")
        ones_b = sb("ones_b", [128, 128], BF16)
        ident = sb("ident", [128, 128], BF16)
        eps_t = sb("eps_t", [128, 1], F32)
        gains_s = sb("gains_s", [128, 4, 32], F32)
        hcw_s = sb("hcw_s", [128, 48, 4], F32)
        fcw_s = sb("fcw_s", [128, NFF, 4], F32)
        hskip_s = sb("hskip_s", [128, 16], F32)
        B_const = Buf("const")

        psm = [E(nc.psum_tensor(f"psm{i}", [128, 512], F32)) for i in range(3)]
        B_psm = [Buf(f"psm{i}") for i in range(3)]
        psh = E(nc.psum_tensor("psh", [128, 512], F32))
        B_psh = [Buf(f"psh{i}") for i in range(8)]
        pss = E(nc.psum_tensor("pss", [128, 512], F32))
        B_pss = Buf("pss")
        pss2 = E(nc.psum_tensor("pss2", [128, 512], F32))
        B_pss2 = Buf("pss2")
        psx = E(nc.psum_tensor("psx", [128, 512], F32))
        B_psx = Buf("psx")
        pst = E(nc.psum_tensor("pst", [128, 1024], BF16))
        B_pst = Buf("pst")

        block = E(nc.Block())

        P.op('vector', lambda e: e.memset(ones_b[:], 1.0), writes=[B_const])
        P.op('vector', lambda e: e.memset(eps_t[:], EPS), writes=[B_const])
        P.dma('sync', ident[:], identb, writes=[B_const])
        P.dma('sync', gains_s[:], gains, writes=[B_const])
        if hcw is not None:
            P.dma('sync', hcw_s[:], hcw, writes=[B_const])
        if fcw is not None:
            P.dma('sync', fcw_s[:], fcw, writes=[B_const])
        if hskip is not None:
            P.dma('sync', hskip_s[:], hskip, writes=[B_const])

        cnt = {'pm': 0, 'ph': 0, 'ob': 0, 'of': 0, 'zt': 0, 'cv': 0, 'wb': 0}

        def rot(key, n):
            v = cnt[key] % n
            cnt[key] += 1
            return v

        def prep(src, s, gi):
            for qd in range(16):
                sl = qd % 2
                P.dma('sync', xin[sl][:], src[qd * 256:(qd + 1) * 256, s:s + TW].rearrange("(kc p) t -> p kc t", p=128),
                      writes=[B_xin[sl]])
                P.op('scalar', lambda e, sl=sl: e.activation(out=sq[sl][:], in_=xin[sl][:], func=AF.Square),
                     reads=[B_xin[sl]], writes=[B_sq[sl]])
                fns = []
                for j in range(2):
                    kc = qd * 2 + j
                    fns.append(lambda e, sl=sl, j=j, kc=kc: e.tensor_scalar(
                        out=hg[:, kc, :], in0=xin[sl][:, j, :], scalar1=gains_s[:, gi, kc:kc + 1], scalar2=None,
                        op0=ALU.mult))
                P.op('vector', fns, reads=[B_xin[sl], B_const], writes=[B_hg])
                fns = []
                for j in range(2):
                    kc = qd * 2 + j
                    fns.append(lambda e, sl=sl, j=j, kc=kc: e.matmul(
                        pss[:, :], lhsT=ones_b[:], rhs=sq[sl][:, j, 1:513], start=(kc == 0), stop=(kc == 31)))
                    fns.append(lambda e, sl=sl, j=j, kc=kc: e.matmul(
                        pss2[:, 0:2], lhsT=ones_b[:], rhs=sq[sl][:, j, 0:TW:513], start=(kc == 0), stop=(kc == 31)))
                P.op('tensor', fns, reads=[B_sq[sl], B_const], writes=[B_pss, B_pss2])
            P.op('scalar', [
                lambda e: e.activation(out=rt[:, 1:513], in_=pss[:, :], func=AF.Sqrt, bias=eps_t[:, 0:1], scale=1.0 / D),
                lambda e: e.activation(out=rt[:, 0:TW:513], in_=pss2[:, 0:2], func=AF.Sqrt, bias=eps_t[:, 0:1],
                                       scale=1.0 / D)],
                 reads=[B_pss, B_pss2, B_const], writes=[B_rt])
            P.op('vector', lambda e: e.reciprocal(out=rstd[:], in_=rt[:]), reads=[B_rt], writes=[B_rstd])

        pieces = {}

        def wload(src_rows, nkc, key=None):
            sl = rot('wb', NWB)
            if key is not None and key in pieces:
                idx, Bp = pieces[key]
                P.dma('gpsimd', wb[sl][:, 0:nkc, :], wcaches[idx // 128][idx % 128, :, 0:nkc * 128].rearrange("p (k c) -> p k c", c=128),
                      reads=[Bp], writes=[B_wb[sl]])
                return sl
            P.dma('gpsimd', wb[sl][:, 0:nkc, :], src_rows.rearrange("(kc p) c -> p kc c", p=128), writes=[B_wb[sl]])
            if key is not None:
                idx = len(pieces)
                Bp = Buf(f"piece{idx}")
                pieces[key] = (idx, Bp)
                P.dma('sync', wcaches[idx // 128][idx % 128, :, 0:nkc * 128].rearrange("p (k c) -> p k c", c=128), wb[sl][:, 0:nkc, :],
                      reads=[B_wb[sl]], writes=[Bp])
            return sl

        def mm_group(ps_ap, B_ps, pieces, rhs_fn, extra_reads, n_total=None):
            fns = []
            tot = sum(p[1] for p in pieces)
            i = 0
            rd = list(extra_reads)
            for (sl, nkc, kb) in pieces:
                rd.append(B_wb[sl])
                for k in range(nkc):
                    fns.append(lambda e, sl=sl, k=k, kb=kb, i=i: e.matmul(
                        ps_ap, lhsT=wb[sl][:, k, :], rhs=rhs_fn(kb + k), start=(i == 0), stop=(i == tot - 1)))
                    i += 1
            P.op('tensor', fns, reads=rd, writes=[B_ps])

        def phase1():
            chunks = [('q', h * 128, h) for h in range(16)]
            chunks += [('k', 2048 + h * 128, h) for h in range(4)]
            chunks += [('v', 2560 + h * 128, h) for h in range(4)]
            for c in range(16):
                chunks += [('hv', 3072 + c * 128, c), ('hx1', 5120 + c * 128, 16 + c), ('hx0', 7168 + c * 128, 32 + c)]
            SC = 128.0 ** -0.5
            for ti in range(NT):
                s = ti * 512
                prep(xT, s, 0)
                P.dma('sync', vmask[:], valid[:, s:s + TW], writes=[B_vmask])
                pend = [wload(w_in[:, chunks[i][1]:chunks[i][1] + 128], 32, ('p1', i)) for i in range(2)]
                for n, (kind, col0, idx) in enumerate(chunks):
                    if n + 2 < len(chunks):
                        c2 = chunks[n + 2][1]
                        pend.append(wload(w_in[:, c2:c2 + 128], 32, ('p1', n + 2)))
                    sl = pend.pop(0)
                    pm = rot('pm', 3)
                    mm_group(psm[pm][:, :], B_psm[pm], [(sl, 32, 0)], lambda kc: hg[:, kc, 1:513], [B_hg])
                    if kind in ('q', 'k', 'v'):
                        o = rot('ob', NOB)
                        if kind == 'q':
                            P.op('vector', lambda e, o=o, pm=pm: e.scalar_tensor_tensor(
                                out=obf[o][:], in0=psm[pm][:, :], scalar=SC, in1=rstd[:, 1:513], op0=ALU.mult,
                                op1=ALU.mult), reads=[B_psm[pm], B_rstd], writes=[B_obf[o]])
                            P.dma('sync', qT[idx * 128:(idx + 1) * 128, s:s + 512], obf[o][:], reads=[B_obf[o]])
                        elif kind == 'k':
                            P.op('vector', lambda e, o=o, pm=pm: e.tensor_tensor(
                                out=obf[o][:], in0=psm[pm][:, :], in1=rstd[:, 1:513], op=ALU.mult),
                                reads=[B_psm[pm], B_rstd], writes=[B_obf[o]])
                            P.dma('sync', kT[idx * 128:(idx + 1) * 128, s:s + 512], obf[o][:], reads=[B_obf[o]])
                        else:
                            P.op('vector', lambda e, o=o, pm=pm: e.tensor_tensor(
                                out=obf[o][:], in0=psm[pm][:, :], in1=rstd[:, 1:513], op=ALU.mult),
                                reads=[B_psm[pm], B_rstd], writes=[B_obf[o]])
                            fns = [lambda e, o=o, b=b: e.transpose(pst[:, b * 128:(b + 1) * 128],
                                                                   obf[o][:, b * 128:(b + 1) * 128], ident[:])
                                   for b in range(4)]
                            P.op('tensor', fns, reads=[B_obf[o], B_const], writes=[B_pst])
                            o2 = rot('ob', NOB)
                            P.op('scalar', lambda e, o2=o2: e.copy(out=obf[o2][:], in_=pst[:, 0:512]),
                                 reads=[B_pst], writes=[B_obf[o2]])
                            P.dma('sync', vS[s:s + 512, idx * 128:(idx + 1) * 128].rearrange("(b p) d -> p b d", p=128),
                                  obf[o2][:].rearrange("p (b d) -> p b d", b=4), reads=[B_obf[o2]])
                        continue
                    ph = rot('ph', 8)
                    mm_group(psh[:, ph * 4:ph * 4 + 2], B_psh[ph], [(sl, 32, 0)], lambda kc: hg[:, kc, 0:TW:513], [B_hg])
                    z = rot('zt', 2)
                    P.op('vector', [
                        lambda e, z=z, pm=pm: e.tensor_tensor(out=zt[z][:, 1:513], in0=psm[pm][:, :], in1=rstd[:, 1:513],
                                                             op=ALU.mult),
                        lambda e, z=z, ph=ph: e.tensor_tensor(out=zt[z][:, 0:TW:513], in0=psh[:, ph * 4:ph * 4 + 2],
                                                             in1=rstd[:, 0:TW:513], op=ALU.mult)],
                         reads=[B_psm[pm], B_psh[ph], B_rstd], writes=[B_zt[z]])
                    cv = rot('cv', 2)
                    P.op('vector', lambda e, z=z, cv=cv, idx=idx: e.tensor_scalar(
                        out=cva[cv][:], in0=zt[z][:, 0:512], scalar1=hcw_s[:, idx, 0:1], scalar2=hcw_s[:, idx, 3:4],
                        op0=ALU.mult, op1=ALU.add), reads=[B_zt[z], B_const], writes=[B_cva[cv]])
                    P.op('vector', lambda e, z=z, cv=cv, idx=idx: e.scalar_tensor_tensor(
                        out=cvb[cv][:], in0=zt[z][:, 1:513], scalar=hcw_s[:, idx, 1:2], in1=cva[cv][:], op0=ALU.mult,
                        op1=ALU.add), reads=[B_zt[z], B_cva[cv], B_const], writes=[B_cvb[cv]])
                    if kind == 'hv':
                        P.op('vector', lambda e, z=z, cv=cv, idx=idx: e.scalar_tensor_tensor(
                            out=hvc[:], in0=zt[z][:, 2:514], scalar=hcw_s[:, idx, 2:3], in1=cvb[cv][:], op0=ALU.mult,
                            op1=ALU.add), reads=[B_zt[z], B_cvb[cv], B_const], writes=[B_hvc])
                    elif kind == 'hx1':
                        P.op('vector', lambda e, z=z, cv=cv, idx=idx: e.scalar_tensor_tensor(
                            out=cva[cv][:], in0=zt[z][:, 2:514], scalar=hcw_s[:, idx, 2:3], in1=cvb[cv][:], op0=ALU.mult,
                            op1=ALU.add), reads=[B_zt[z], B_cvb[cv], B_const], writes=[B_cva[cv]])
                        P.op('vector', lambda e, cv=cv: e.tensor_tensor(
                            out=cvb[cv][:], in0=cva[cv][:], in1=hvc[:], op=ALU.mult),
                            reads=[B_cva[cv], B_hvc], writes=[B_cvb[cv]])
                        o = rot('of', NOB)
                        P.op('vector', lambda e, cv=cv, o=o: e.tensor_tensor(
                            out=of32[o][:], in0=cvb[cv][:], in1=vmask[:, 1:513], op=ALU.mult),
                            reads=[B_cvb[cv], B_vmask], writes=[B_of32[o]])
                        c = idx - 16
                        P.dma('sync', uT[c * 128:(c + 1) * 128, s:s + 512], of32[o][:], reads=[B_of32[o]])
                    else:
                        o = rot('of', NOB)
                        P.op('vector', lambda e, z=z, cv=cv, idx=idx, o=o: e.scalar_tensor_tensor(
                            out=of32[o][:], in0=zt[z][:, 2:514], scalar=hcw_s[:, idx, 2:3], in1=cvb[cv][:], op0=ALU.mult,
                            op1=ALU.add), reads=[B_zt[z], B_cvb[cv], B_const], writes=[B_of32[o]])
                        c = idx - 32
                        P.dma('sync', hx0T[c * 128:(c + 1) * 128, s:s + 512], of32[o][:], reads=[B_of32[o]])


        att_o = [sb(f"att_o{i}", [128, 512], BF16) for i in range(2)]
        B_att_o = [Buf(f"att_o{i}") for i in range(2)]
        biasT = big[:, 24576:30720].rearrange("p (a t) -> p a t", a=12)
        bkt_s = big[0:33, 30720:32256].bitcast(F32)
        rb_s = sb("rb_s", [33, 16], F32)
        esink = sb("esink", [128, 16], F32)
        kmask_s = sb("kmask_s", [128, 32], F32)
        B_c3 = Buf("c3")
        pb_s = sb("pb_s", [128, 2, 512], BF16)
        B_pb = Buf("pb")
        zero_c = sb("zero_c", [128, 32], F32)
        P.op('vector', lambda e: e.memset(zero_c[:], 0.0), writes=[B_const])
        accs = [(psm[0], B_psm[0]), (psm[1], B_psm[1]), (psm[2], B_psm[2]), (psx, B_psx)]
        cnt['acc'] = 0
        cnt['ao'] = 0

        def acc():
            return accs[rot('acc', 4)]

        def phase2():
            PI = math.pi
            hgf = hg[:].rearrange("p a b -> p (a b)")
            w3_s = hgf[0:64, 0:8192].bitcast(F32)
            w1_s = hgf[0:33, 8192:8320].bitcast(F32)
            w2_s = hgf[0:64, 8320:8448].bitcast(F32)
            fp_s = hgf[0:64, 8448:8464].bitcast(F32)
            nd_s = hgf[:, 8464:8528].bitcast(F32)
            tnb_s = big[:, 0:8192].bitcast(F32)
            zf_s = big[0:33, 8192:16384].bitcast(F32)
            B_f = Buf("filt")
            P.dma('sync', w3_s, fw3, writes=[B_f])
            P.dma('sync', w1_s, fw1, writes=[B_f])
            P.dma('sync', w2_s, fw2, writes=[B_f])
            P.dma('sync', fp_s[:, 0:4], fpar, writes=[B_f])
            P.dma('sync', nd_s, decay, writes=[B_f])
            P.dma('sync', tnb_s, tnb, writes=[B_f])
            P.dma('sync', zf_s, zfT, writes=[B_f])
            P.op('vector', [
                lambda e: e.tensor_tensor(out=fp_s[:, 4:5], in0=fp_s[:, 0:1], in1=fp_s[:, 1:2], op=ALU.mult),
                lambda e: e.tensor_tensor(out=fp_s[:, 5:6], in0=fp_s[:, 2:3], in1=fp_s[:, 3:4], op=ALU.mult),
                lambda e: e.tensor_scalar(out=nd_s, in0=nd_s, scalar1=-1.0, scalar2=None, op0=ALU.mult)],
                 reads=[B_f], writes=[B_f])
            for tt in range(8):
                tsl = slice(tt * 512, (tt + 1) * 512)
                gprev = None
                for layer in range(2):
                    pa, Ba = acc()
                    if layer == 0:
                        P.op('tensor', lambda e, pa=pa, tsl=tsl: e.matmul(pa[0:64, :], lhsT=w1_s, rhs=zf_s[:, tsl],
                                                                         start=True, stop=True), reads=[B_f], writes=[Ba])
                    else:
                        P.op('tensor', lambda e, pa=pa, gp=gprev: e.matmul(pa[0:64, :], lhsT=w2_s, rhs=cva[gp][0:64, :],
                                                                          start=True, stop=True),
                             reads=[B_f, B_cva[gprev]], writes=[Ba])
                    cv = rot('cv', 2)
                    fi, fbi = (1, 4) if layer == 0 else (3, 5)
                    P.op('vector', lambda e, cv=cv, pa=pa, fi=fi, fbi=fbi: e.tensor_scalar(
                        out=cvb[cv][0:64, :], in0=pa[0:64, :], scalar1=fp_s[:, fi:fi + 1], scalar2=fp_s[:, fbi:fbi + 1],
                        op0=ALU.mult, op1=ALU.add), reads=[Ba, B_f], writes=[B_cvb[cv]])
                    z = rot('zt', 2)
                    P.op('vector', [
                        lambda e, cv=cv, z=z: e.tensor_scalar(out=zt[z][0:64, 0:512], in0=cvb[cv][0:64, :], scalar1=PI,
                                                             scalar2=-2 * PI, op0=ALU.is_gt, op1=ALU.mult),
                        lambda e, cv=cv: e.tensor_scalar(out=cva[cv][0:64, :], in0=cvb[cv][0:64, :], scalar1=-PI,
                                                        scalar2=2 * PI, op0=ALU.is_lt, op1=ALU.mult)],
                         reads=[B_cvb[cv]], writes=[B_zt[z], B_cva[cv]])
                    P.op('vector', lambda e, cv=cv, z=z: e.tensor_tensor(out=zt[z][0:64, 0:512], in0=zt[z][0:64, 0:512],
                                                                        in1=cva[cv][0:64, :], op=ALU.add),
                         reads=[B_zt[z], B_cva[cv]], writes=[B_zt[z]])
                    P.op('vector', lambda e, cv=cv, z=z: e.tensor_tensor(out=cva[cv][0:64, :], in0=cvb[cv][0:64, :],
                                                                        in1=zt[z][0:64, 0:512], op=ALU.add),
                         reads=[B_zt[z], B_cvb[cv]], writes=[B_cva[cv]])
                    P.op('scalar', lambda e, cv=cv: e.activation(out=cva[cv][0:64, :], in_=cva[cv][0:64, :], func=AF.Sin),
                         reads=[B_cva[cv]], writes=[B_cva[cv]])
                    gprev = cv
                for k in range(32):
                    pa, Ba = acc()
                    P.op('tensor', lambda e, pa=pa, k=k, gp=gprev: e.matmul(
                        pa[:, :], lhsT=w3_s[:, k * 128:(k + 1) * 128], rhs=cva[gp][0:64, :], start=True, stop=True),
                        reads=[B_f, B_cva[gprev]], writes=[Ba])
                    z = rot('zt', 2)
                    P.op('scalar', lambda e, z=z, k=k, tsl=tsl: e.activation(
                        out=zt[z][:, 0:512], in_=tnb_s[:, tsl], func=AF.Exp, scale=nd_s[:, k:k + 1]),
                        reads=[B_f], writes=[B_zt[z]])
                    o = rot('ob', NOB)
                    fns = [lambda e, o=o, pa=pa, z=z: e.tensor_tensor(out=obf[o][:], in0=pa[:, :], in1=zt[z][:, 0:512],
                                                                      op=ALU.mult)]
                    if k >= 16 and tt == 0:
                        fns.append(lambda e, o=o: e.memset(obf[o][:, 0:1], 0.0))
                    P.op('vector', fns, reads=[Ba, B_zt[z]], writes=[B_obf[o]])
                    P.dma('sync', hfb[k * 128:(k + 1) * 128, tsl], obf[o][:], reads=[B_obf[o]])
            P.barrier()
            NFA = 33
            A_sb = big[:, 0:9504].rearrange("p (r c) -> p r c", c=96)
            slab = [big[0:32, 9504 + i * 4096:9504 + (i + 1) * 4096].rearrange("p (c t) -> p c t", t=128) for i in range(3)]
            B_slab = [Buf(f"slab{i}") for i in range(3)]
            Ya = big[:, 21792:25888].rearrange("p (c r) -> p c r", r=128)
            Yb = big[:, 25888:29984].rearrange("p (c r) -> p c r", r=128)
            y_sb = big[0:32, 29984:38176].bitcast(F32).rearrange("p (c t) -> p c t", t=128)
            D_sb = hgf[:, 0:4096].rearrange("p (t c) -> p t c", c=32)
            G_s = hgf[:, 4096:4096 + 8448].rearrange("p (a s f) -> p a s f", s=2, f=128)
            MI_s = wb[0][:].rearrange("p a b -> p (a b)").rearrange("p (t h) -> p t h", h=32)
            wb1 = wb[1][:].rearrange("p a b -> p (a b)")
            GI_s = wb1[:, 0:256].rearrange("p (s t) -> p s t", s=2)
            FH_s = wb1[0:32, 256:355]
            B_A, B_Y, B_D, B_ysb, B_tab = [Buf("A0"), Buf("A1")], [Buf("Y0"), Buf("Y1")], [Buf("D0"), Buf("D1")], [Buf("ysb0"), Buf("ysb1")], Buf("tab")
            P.dma('sync', G_s, g_t.rearrange("p (a s f) -> p a s f", s=2, f=128), writes=[B_tab])
            P.dma('sync', MI_s, mi_t.rearrange("p (t h) -> p t h", h=32), writes=[B_tab])
            P.dma('sync', GI_s, gi_t.rearrange("p (s t) -> p s t", s=2), writes=[B_tab])
            P.dma('sync', FH_s, fh_t, writes=[B_tab])
            P.op('vector', [lambda e: e.memset(Ya, 0.0), lambda e: e.memset(Yb, 0.0)], writes=B_Y)
            for g in range(64):
                c0 = g * 32
                P.dma('gpsimd', slab[0], uT[c0:c0 + 32, :].rearrange("c (th tl) -> th c tl", tl=128), writes=[B_slab[0]])
                P.dma('sync', slab[1], hfb[c0:c0 + 32, :].rearrange("c (th tl) -> th c tl", tl=128), writes=[B_slab[1]])
                P.dma('sync', slab[2], hfb[2048 + c0:2048 + c0 + 32, :].rearrange("c (th tl) -> th c tl", tl=128),
                      writes=[B_slab[2]])
                for si in range(3):
                    for cb in range(8):
                        pa, Ba = acc()
                        fns = [lambda e, pa=pa, si=si, cb=cb, k=k: e.matmul(
                            pa[:, k * 128:k * 128 + 99], lhsT=slab[si][:, cb * 4 + k, :], rhs=FH_s, start=True, stop=True)
                            for k in range(4)]
                        P.op('tensor', fns, reads=[B_slab[si], B_tab], writes=[Ba])
                        col = si * 32 + cb * 4
                        if cb % 2 == 0:
                            P.op('scalar', lambda e, pa=pa, col=col: e.copy(
                                out=A_sb[:, :, col:col + 4].rearrange("p r k -> p k r"),
                                in_=pa[:, :].rearrange("p (k r) -> p k r", r=128)[:, :, 0:99]),
                                reads=[Ba], writes=[B_A[0]])
                        else:
                            P.op('vector', lambda e, pa=pa, col=col: e.tensor_copy(
                                out=A_sb[:, :, col:col + 4].rearrange("p r k -> p k r"),
                                in_=pa[:, :].rearrange("p (k r) -> p k r", r=128)[:, :, 0:99]),
                                reads=[Ba], writes=[B_A[1]])
                for fa0 in range(0, NFA, 2):
                    nf = min(2, NFA - fa0)
                    pa, Ba = acc()
                    fns = []
                    for fl in range(nf):
                        fa = fa0 + fl
                        b0 = fl * 192
                        fns.append(lambda e, pa=pa, fa=fa, b0=b0: e.matmul(pa[:, b0:b0 + 96], lhsT=G_s[:, fa, 0, :],
                                                                           rhs=A_sb[:, fa, :], start=True, stop=False))
                        fns.append(lambda e, pa=pa, fa=fa, b0=b0: e.matmul(pa[:, b0:b0 + 96], lhsT=G_s[:, fa, 1, :],
                                                                           rhs=A_sb[:, 66 + fa, :], start=False, stop=True))
                        fns.append(lambda e, pa=pa, fa=fa, b0=b0: e.matmul(pa[:, b0 + 96:b0 + 192], lhsT=G_s[:, fa, 1, :],
                                                                           rhs=A_sb[:, fa, :], start=True, stop=False))
                        fns.append(lambda e, pa=pa, fa=fa, b0=b0: e.matmul(pa[:, b0 + 96:b0 + 192], lhsT=G_s[:, fa, 0, :],
                                                                           rhs=A_sb[:, 33 + fa, :], start=False, stop=True))
                    P.op('tensor', fns, reads=B_A + [B_tab], writes=[Ba])
                    z = rot('zt', 2)
                    P.op('scalar', lambda e, z=z, pa=pa, nf=nf: e.copy(out=zt[z][:, 0:nf * 192], in_=pa[:, 0:nf * 192]),
                         reads=[Ba], writes=[B_zt[z]])
                    X = zt[z][:, 0:nf * 192].rearrange("p (f r s c) -> p f r s c", r=2, s=3, c=32)
                    cv = rot('cv', 2)
                    T = cva[cv][:, 0:384].rearrange("p (k f c) -> p k f c", k=6, c=32)[:, :, 0:nf, :]
                    U = cvb[cv][:, 0:128].rearrange("p (k f c) -> p k f c", k=2, c=32)[:, :, 0:nf, :]
                    yav = Ya[:, :, fa0:fa0 + nf].rearrange("p c f -> p f c")
                    yai = Ya[:, :, 64 + fa0:64 + fa0 + nf].rearrange("p c f -> p f c")
                    ybv = Yb[:, :, fa0:fa0 + nf].rearrange("p c f -> p f c")
                    ybi = Yb[:, :, 64 + fa0:64 + fa0 + nf].rearrange("p c f -> p f c")
                    P.op('vector', [
                        lambda e, X=X, U=U: e.tensor_tensor(out=U[:, 0], in0=X[:, :, 0, 1, :], in1=X[:, :, 0, 2, :], op=ALU.add),
                        lambda e, X=X, U=U: e.tensor_tensor(out=U[:, 1], in0=X[:, :, 1, 1, :], in1=X[:, :, 1, 2, :],
                                                           op=ALU.subtract)],
                         reads=[B_zt[z]], writes=[B_cvb[cv]])
                    T2 = of32[cv][:, 0:128].rearrange("p (k f c) -> p k f c", k=2, c=32)[:, :, 0:nf, :]
                    P.op('vector', [
                        lambda e, X=X, U=U, T=T: e.tensor_tensor(out=T[:, 0], in0=X[:, :, 0, 0, :], in1=U[:, 0], op=ALU.mult),
                        lambda e, X=X, U=U, T=T: e.tensor_tensor(out=T[:, 1], in0=X[:, :, 1, 0, :], in1=U[:, 1], op=ALU.mult)],
                         reads=[B_zt[z], B_cvb[cv]], writes=[B_cva[cv]])
                    P.op('gpsimd', [
                        lambda e, X=X, U=U, T2=T2: e.tensor_tensor(out=T2[:, 0], in0=X[:, :, 0, 0, :], in1=U[:, 1], op=ALU.mult),
                        lambda e, X=X, U=U, T2=T2: e.tensor_tensor(out=T2[:, 1], in0=X[:, :, 1, 0, :], in1=U[:, 0], op=ALU.mult)],
                         reads=[B_zt[z], B_cvb[cv]], writes=[B_of32[cv]])
                    P.op('vector', [
                        lambda e, T=T, yav=yav: e.tensor_tensor(out=yav, in0=T[:, 0], in1=T[:, 1], op=ALU.subtract),
                        lambda e, T=T, ybi=ybi: e.tensor_tensor(out=ybi, in0=T[:, 0], in1=T[:, 1], op=ALU.subtract)],
                         reads=[B_cva[cv]], writes=[B_Y[0]])
                    P.op('gpsimd', [
                        lambda e, T2=T2, yai=yai: e.tensor_tensor(out=yai, in0=T2[:, 0], in1=T2[:, 1], op=ALU.add),
                        lambda e, T2=T2, ybv=ybv: e.tensor_tensor(out=ybv, in0=T2[:, 0], in1=T2[:, 1], op=ALU.add)],
                         reads=[B_of32[cv]], writes=[B_Y[1]])
                    P.op('gpsimd', lambda e, ybv=ybv: e.tensor_scalar(out=ybv, in0=ybv, scalar1=-1.0, scalar2=0.0,
                                                                       op0=ALU.mult, op1=ALU.add),
                         reads=[B_Y[1]], writes=[B_Y[1]])
                for cb in range(8):
                    pa, Ba = acc()
                    fns = []
                    for k in range(4):
                        c = cb * 4 + k
                        fns.append(lambda e, pa=pa, c=c, k=k: e.matmul(pa[:, k * 128:(k + 1) * 128], lhsT=Ya[:, c, :],
                                                                       rhs=GI_s[:, 0, :], start=True, stop=False))
                        fns.append(lambda e, pa=pa, c=c, k=k: e.matmul(pa[:, k * 128:(k + 1) * 128], lhsT=Yb[:, c, :],
                                                                       rhs=GI_s[:, 1, :], start=False, stop=True))
                    P.op('tensor', fns, reads=B_Y + [B_tab], writes=[Ba])
                    if cb % 2 == 0:
                        P.op('scalar', lambda e, pa=pa, cb=cb: e.copy(
                            out=D_sb[:, :, cb * 4:cb * 4 + 4].rearrange("p t k -> p k t"),
                            in_=pa[:, :].rearrange("p (k t) -> p k t", t=128)), reads=[Ba], writes=[B_D[0]])
                    else:
                        P.op('vector', lambda e, pa=pa, cb=cb: e.tensor_copy(
                            out=D_sb[:, :, cb * 4:cb * 4 + 4].rearrange("p t k -> p k t"),
                            in_=pa[:, :].rearrange("p (k t) -> p k t", t=128)), reads=[Ba], writes=[B_D[1]])
                for tb in range(8):
                    pa, Ba = acc()
                    fns = [lambda e, pa=pa, tb=tb, tl=tl: e.matmul(
                        pa[0:32, tl * 32:(tl + 1) * 32], lhsT=MI_s[:, tb * 16 + tl, :], rhs=D_sb[:, tb * 16 + tl, :],
                        start=True, stop=True) for tl in range(16)]
                    P.op('tensor', fns, reads=B_D + [B_tab], writes=[Ba])
                    if tb % 2 == 0:
                        P.op('vector', lambda e, pa=pa, tb=tb: e.tensor_copy(
                            out=y_sb[:, :, tb * 16:(tb + 1) * 16].rearrange("p c t -> p t c"),
                            in_=pa[0:32, :].rearrange("p (t c) -> p t c", c=32)), reads=[Ba], writes=[B_ysb[0]])
                    else:
                        P.op('scalar', lambda e, pa=pa, tb=tb: e.copy(
                            out=y_sb[:, :, tb * 16:(tb + 1) * 16].rearrange("p c t -> p t c"),
                            in_=pa[0:32, :].rearrange("p (t c) -> p t c", c=32)), reads=[Ba], writes=[B_ysb[1]])
                P.dma('sync', ycT[c0:c0 + 32, :].rearrange("c (th tl) -> th c tl", tl=128), y_sb, reads=B_ysb)

        def phase3():
            P.dma('sync', bkt_s[:], bkt, writes=[B_c3])
            P.dma('sync', rb_s[:], rbext, writes=[B_c3])
            P.dma('sync', esink[:], sinkr, writes=[B_c3])
            P.dma('sync', kmask_s[:], kmask, writes=[B_c3])
            P.op('scalar', lambda e: e.activation(out=esink[:], in_=esink[:], func=AF.Exp), reads=[B_c3], writes=[B_c3])
            for ri, r in enumerate((-1, 0, 1)):
                for hv in range(4):
                    fns = []
                    for qi in range(128):
                        st = 384 - qi - 128 * r
                        fns.append(lambda e, qi=qi, st=st, hv=hv: e.matmul(
                            psx[:, qi:512:128], lhsT=bkt_s[:, st:st + 128], rhs=rb_s[:, hv * 4:hv * 4 + 4],
                            start=True, stop=True))
                    P.op('tensor', fns, reads=[B_c3], writes=[B_psx])
                    P.op('scalar', lambda e, ri=ri, hv=hv: e.copy(out=biasT[:, ri * 4 + hv, :], in_=psx[:, :]),
                         reads=[B_psx], writes=[B_c3])
            q4 = big[:, 0:16384].rearrange("p (g t) -> p g t", g=4)
            kh = big[:, 16384:20480]
            vh = big[:, 20480:24576].rearrange("p (b d) -> p b d", b=32)
            for hv in range(4):
                P.dma('sync', q4, qT[hv * 512:(hv + 1) * 512, 0:LT].rearrange("(g p) t -> p g t", p=128),
                      reads=[B_qT], writes=[B_big])
                P.dma('sync', kh, kT[hv * 128:(hv + 1) * 128, 0:LT], reads=[B_kT], writes=[B_big])
                P.dma('sync', vh, vS[:, hv * 128:(hv + 1) * 128].rearrange("(b p) d -> p b d", p=128),
                      reads=[B_vS], writes=[B_big])
                for i in range(32):
                    js = [j for j in (i - 1, i, i + 1) if 0 <= j < 32]
                    par = i % 2
                    ps_o, B_o = (pss, B_pss) if par == 0 else (psx, B_psx)
                    ps_d, B_d = (pss2, B_pss2) if par == 0 else (psh, B_psh[0])
                    pts = []
                    for jn, j in enumerate(js):
                        ri = (i - j) + 1
                        pm = rot('pm', 3)
                        P.op('tensor', [
                            lambda e, j=j, i=i, pm=pm: e.matmul(psm[pm][:, :], lhsT=kh[:, j * 128:(j + 1) * 128],
                                                                 rhs=q4[:, :, i * 128:(i + 1) * 128], start=True, stop=False),
                            lambda e, ri=ri, hv=hv, pm=pm: e.matmul(psm[pm][:, :], lhsT=ident[:], rhs=biasT[:, ri * 4 + hv, :],
                                                                    start=False, stop=True)],
                             reads=[B_big, B_c3, B_const], writes=[B_psm[pm]])
                        o = rot('ob', NOB)
                        P.op('scalar', lambda e, o=o, pm=pm, j=j: e.activation(
                            out=obf[o][:], in_=psm[pm][:, :], func=AF.Exp, bias=kmask_s[:, j:j + 1], scale=1.0),
                            reads=[B_psm[pm], B_c3], writes=[B_obf[o]])
                        pts.append((o, j))
                    for jn, (o, j) in enumerate(pts):
                        P.op('tensor', [
                            lambda e, o=o, j=j, jn=jn, ps_o=ps_o: e.matmul(ps_o[:, :], lhsT=vh[:, j, :], rhs=obf[o][:],
                                                                           start=(jn == 0), stop=(jn == len(js) - 1)),
                            lambda e, o=o, jn=jn, ps_d=ps_d: e.matmul(ps_d[:, :], lhsT=ones_b[:], rhs=obf[o][:],
                                                                      start=(jn == 0), stop=(jn == len(js) - 1))],
                             reads=[B_obf[o], B_big, B_const], writes=[B_o, B_d])
                    cv = rot('cv', 2)
                    P.op('vector', [lambda e, g=g, cv=cv, ps_d=ps_d, hv=hv: e.tensor_scalar(
                        out=cva[cv][:, g * 128:(g + 1) * 128], in0=ps_d[:, g * 128:(g + 1) * 128],
                        scalar1=esink[:, hv * 4 + g:hv * 4 + g + 1], scalar2=None, op0=ALU.add) for g in range(4)],
                         reads=[B_d, B_c3], writes=[B_cva[cv]])
                    P.op('vector', lambda e, cv=cv: e.reciprocal(out=cvb[cv][:], in_=cva[cv][:]),
                         reads=[B_cva[cv]], writes=[B_cvb[cv]])
                    ao = rot('ao', 2)
                    P.op('vector', lambda e, cv=cv, ao=ao, ps_o=ps_o: e.tensor_tensor(
                        out=att_o[ao][:], in0=ps_o[:, :], in1=cvb[cv][:], op=ALU.mult),
                        reads=[B_o, B_cvb[cv]], writes=[B_att_o[ao]])
                    P.dma('sync', attnT[hv * 512:(hv + 1) * 512, i * 128:(i + 1) * 128].rearrange("(g d) q -> d g q", d=128),
                          att_o[ao][:].rearrange("d (g q) -> d g q", g=4), reads=[B_att_o[ao]], writes=[B_attnT])

        def phase4(use_yc):
            at_s = big[:, 0:8192].rearrange("p (c t) -> p c t", c=16)
            hy_s = big[:, 8192:16384].rearrange("p (c t) -> p c t", c=16)
            m_s = big[:, 16384:32768].rearrange("p (c t) -> p c t", c=32)
            B_at, B_hy, B_m = Buf("at"), Buf("hy"), Buf("m")
            for half in range(2):
                P.dma('sync', x1T[:, half * (LT + 1):half * (LT + 1) + 1].rearrange("(kc p) o -> p kc o", p=128),
                      zero_c[:].rearrange("p (k o) -> p k o", o=1), reads=[B_const], writes=[B_x1T], allow_slow_non_contiguous=True)
            for ti in range(NT):
                s = ti * 512
                prep(xT, s, 0)
                P.dma('sync', vmask[:], valid[:, s:s + TW], writes=[B_vmask])
                P.dma('sync', at_s, attnT[:, s:s + 512].rearrange("(c p) t -> p c t", p=128), reads=[B_attnT], writes=[B_at])
                for c in range(16):
                    z = rot('zt', 2)
                    cv = rot('cv', 2)
                    P.dma('sync', cva[cv][:], uT[c * 128:(c + 1) * 128, s:s + 512], reads=[B_uT], writes=[B_cva[cv]])
                    P.dma('sync', cvb[cv][:], hx0T[c * 128:(c + 1) * 128, s:s + 512], reads=[B_hx0T], writes=[B_cvb[cv]])
                    if use_yc:
                        P.dma('sync', zt[z][:, 0:512], ycT[c * 128:(c + 1) * 128, s:s + 512], reads=[B_ycT], writes=[B_zt[z]])
                        P.op('vector', lambda e, z=z, cv=cv, c=c: e.scalar_tensor_tensor(
                            out=zt[z][:, 0:512], in0=cva[cv][:], scalar=hskip_s[:, c:c + 1], in1=zt[z][:, 0:512],
                            op0=ALU.mult, op1=ALU.add), reads=[B_cva[cv], B_zt[z], B_const], writes=[B_zt[z]])
                    else:
                        P.op('vector', lambda e, z=z, cv=cv, c=c: e.tensor_scalar(
                            out=zt[z][:, 0:512], in0=cva[cv][:], scalar1=hskip_s[:, c:c + 1], scalar2=None,
                            op0=ALU.mult), reads=[B_cva[cv], B_const], writes=[B_zt[z]])
                    P.op('vector', lambda e, z=z, cv=cv, c=c: e.tensor_tensor(
                        out=hy_s[:, c, :], in0=zt[z][:, 0:512], in1=cvb[cv][:], op=ALU.mult),
                        reads=[B_zt[z], B_cvb[cv]], writes=[B_hy])
                for j in range(32):
                    c0 = j * 128
                    s_a = wload(w_ao[:, c0:c0 + 128], 16, ('ao', j))
                    s_h = wload(w_ho[:, c0:c0 + 128], 16, ('ho', j))
                    s_ga = wload(w_in[:, 9216 + c0:9216 + c0 + 128], 32, ('ga', j))
                    s_gh = wload(w_in[:, 13312 + c0:13312 + c0 + 128], 32, ('gh', j))
                    pa, Ba = acc()
                    mm_group(pa[:, :], Ba, [(s_a, 16, 0)], lambda kc: at_s[:, kc, :], [B_at])
                    ph_, Bh = acc()
                    mm_group(ph_[:, :], Bh, [(s_h, 16, 0)], lambda kc: hy_s[:, kc, :], [B_hy])
                    pga, Bga = acc()
                    mm_group(pga[:, :], Bga, [(s_ga, 32, 0)], lambda kc: hg[:, kc, 1:513], [B_hg])
                    pgh, Bgh = acc()
                    mm_group(pgh[:, :], Bgh, [(s_gh, 32, 0)], lambda kc: hg[:, kc, 1:513], [B_hg])
                    res = []
                    for (pg, Bg, pp, Bp) in ((pga, Bga, pa, Ba), (pgh, Bgh, ph_, Bh)):
                        z = rot('zt', 2)
                        P.op('vector', lambda e, z=z, pg=pg: e.tensor_tensor(
                            out=zt[z][:, 0:512], in0=pg[:, :], in1=rstd[:, 1:513], op=ALU.mult),
                            reads=[Bg, B_rstd], writes=[B_zt[z]])
                        P.op('scalar', lambda e, z=z: e.activation(out=zt[z][:, 0:512], in_=zt[z][:, 0:512], func=AF.Sigmoid),
                             reads=[B_zt[z]], writes=[B_zt[z]])
                        cv = rot('cv', 2)
                        P.op('vector', lambda e, z=z, cv=cv, pp=pp: e.tensor_tensor(
                            out=cva[cv][:], in0=zt[z][:, 0:512], in1=pp[:, :], op=ALU.mult),
                            reads=[B_zt[z], Bp], writes=[B_cva[cv]])
                        res.append(cv)
                    P.op('vector', lambda e, j=j, a=res[0], b=res[1]: e.tensor_tensor(
                        out=m_s[:, j, :], in0=cva[a][:], in1=cva[b][:], op=ALU.add),
                        reads=[B_cva[res[0]], B_cva[res[1]]], writes=[B_m])
                for j in range(32):
                    c0 = j * 128
                    s_o = wload(w_out[:, c0:c0 + 128], 32, ('wo', j))
                    po, Bo = acc()
                    mm_group(po[:, :], Bo, [(s_o, 32, 0)], lambda kc: m_s[:, kc, :], [B_m])
                    cv = rot('cv', 2)
                    P.dma('sync', cvb[cv][:], xT[c0:c0 + 128, 1 + s:1 + s + 512], writes=[B_cvb[cv]])
                    P.op('vector', lambda e, cv=cv, po=po: e.tensor_tensor(
                        out=cva[cv][:], in0=po[:, :], in1=cvb[cv][:], op=ALU.add),
                        reads=[Bo, B_cvb[cv]], writes=[B_cva[cv]])
                    o = rot('of', NOB)
                    P.op('vector', lambda e, cv=cv, o=o: e.tensor_tensor(
                        out=of32[o][:], in0=cva[cv][:], in1=vmask[:, 1:513], op=ALU.mult),
                        reads=[B_cva[cv], B_vmask], writes=[B_of32[o]])
                    P.dma('sync', x1T[c0:c0 + 128, 1 + s:1 + s + 512], of32[o][:], reads=[B_of32[o]], writes=[B_x1T])

        def phase5():
            act_s = big[:, :].rearrange("p (c t) -> p c t", c=NFF)
            B_act = Buf("act")
            for ti in range(NT):
                s = ti * 512
                prep(x1T, s, 1)
                for i in range(NFF):
                    c0 = i * 128
                    s_g = wload(w_up[:, c0:c0 + 128], 32, ('ug', i))
                    s_v = wload(w_up[:, DFF + c0:DFF + c0 + 128], 32, ('uv', i))
                    pg, Bg = acc()
                    mm_group(pg[:, :], Bg, [(s_g, 32, 0)], lambda kc: hg[:, kc, 1:513], [B_hg])
                    ph = rot('ph', 8)
                    mm_group(psh[:, ph * 4:ph * 4 + 2], B_psh[ph], [(s_g, 32, 0)], lambda kc: hg[:, kc, 0:TW:513], [B_hg])
                    pv, Bv = acc()
                    mm_group(pv[:, :], Bv, [(s_v, 32, 0)], lambda kc: hg[:, kc, 1:513], [B_hg])
                    z = rot('zt', 2)
                    P.op('vector', [
                        lambda e, z=z, pg=pg: e.tensor_tensor(out=zt[z][:, 1:513], in0=pg[:, :], in1=rstd[:, 1:513], op=ALU.mult),
                        lambda e, z=z, ph=ph: e.tensor_tensor(out=zt[z][:, 0:TW:513], in0=psh[:, ph * 4:ph * 4 + 2],
                                                             in1=rstd[:, 0:TW:513], op=ALU.mult)],
                         reads=[Bg, B_psh[ph], B_rstd], writes=[B_zt[z]])
                    cv = rot('cv', 2)
                    P.op('vector', lambda e, z=z, cv=cv, i=i: e.tensor_scalar(
                        out=cva[cv][:], in0=zt[z][:, 0:512], scalar1=fcw_s[:, i, 0:1], scalar2=fcw_s[:, i, 3:4],
                        op0=ALU.mult, op1=ALU.add), reads=[B_zt[z], B_const], writes=[B_cva[cv]])
                    P.op('vector', lambda e, z=z, cv=cv, i=i: e.scalar_tensor_tensor(
                        out=cvb[cv][:], in0=zt[z][:, 1:513], scalar=fcw_s[:, i, 1:2], in1=cva[cv][:], op0=ALU.mult,
                        op1=ALU.add), reads=[B_zt[z], B_cva[cv], B_const], writes=[B_cvb[cv]])
                    P.op('vector', lambda e, z=z, cv=cv, i=i: e.scalar_tensor_tensor(
                        out=cva[cv][:], in0=zt[z][:, 2:514], scalar=fcw_s[:, i, 2:3], in1=cvb[cv][:], op0=ALU.mult,
                        op1=ALU.add), reads=[B_zt[z], B_cvb[cv], B_const], writes=[B_cva[cv]])
                    P.op('scalar', lambda e, cv=cv: e.activation(out=cvb[cv][:], in_=cva[cv][:], func=AF.Gelu),
                         reads=[B_cva[cv]], writes=[B_cvb[cv]])
                    o = rot('of', NOB)
                    P.op('vector', lambda e, o=o, pv=pv: e.tensor_tensor(
                        out=of32[o][:], in0=pv[:, :], in1=rstd[:, 1:513], op=ALU.mult),
                        reads=[Bv, B_rstd], writes=[B_of32[o]])
                    P.op('vector', lambda e, o=o, cv=cv, i=i: e.tensor_tensor(
                        out=act_s[:, i, :], in0=of32[o][:], in1=cvb[cv][:], op=ALU.mult),
                        reads=[B_of32[o], B_cvb[cv]], writes=[B_act])
                for j in range(32):
                    c0 = j * 128
                    pcs = [(wload(w_down[kb * 128:(kb + n) * 128, c0:c0 + 128], n, ('dn', j, kb)), n, kb)
                           for (kb, n) in ((0, 32), (32, 32), (64, 22))]
                    pd, Bd = acc()
                    mm_group(pd[:, :], Bd, pcs, lambda kc: act_s[:, kc, :], [B_act])
                    cv = rot('cv', 2)
                    P.dma('sync', cvb[cv][:], x1T[c0:c0 + 128, 1 + s:1 + s + 512], reads=[B_x1T], writes=[B_cvb[cv]])
                    o = rot('of', NOB)
                    P.op('vector', lambda e, cv=cv, pd=pd, o=o: e.tensor_tensor(
                        out=of32[o][:], in0=pd[:, :], in1=cvb[cv][:], op=ALU.add),
                        reads=[Bd, B_cvb[cv]], writes=[B_of32[o]])
                    P.dma('sync', x2T[c0:c0 + 128, 1 + s:1 + s + 512], of32[o][:], reads=[B_of32[o]], writes=[B_x2T])

        def phase6():
            x3_s = big[:, 0:32768].bitcast(F32).rearrange("p (c t) -> p c t", c=32)
            B_x3 = Buf("x3")
            for ti in range(NT):
                s = ti * 512
                prep(x2T, s, 2)
                pend_ss = []

                def emit_ss():
                    o_, j_ = pend_ss.pop(0)
                    P.op('tensor', lambda e, o_=o_, j_=j_: e.matmul(pss[:, :], lhsT=ones_b[:], rhs=obf[o_][:],
                                                                    start=(j_ == 0), stop=(j_ == 31)),
                         reads=[B_obf[o_], B_const], writes=[B_pss])
                for kc in range(2):
                    cv = rot('cv', 2)
                    P.dma('sync', cva[cv][:], pT[kc * 128:(kc + 1) * 128, s:s + 512], writes=[B_cva[cv]])
                    P.op('scalar', lambda e, cv=cv, kc=kc: e.copy(out=pb_s[:, kc, :], in_=cva[cv][:]),
                         reads=[B_cva[cv]], writes=[B_pb])
                for j in range(32):
                    c0 = j * 128
                    s_g = wload(w_pg[:, c0:c0 + 128], 32, ('pg', j))
                    s_p = wload(w_ple[:, c0:c0 + 128], 2, ('pl', j))
                    pg, Bg = acc()
                    mm_group(pg[:, :], Bg, [(s_g, 32, 0)], lambda kc: hg[:, kc, 1:513], [B_hg])
                    pp, Bp = acc()
                    mm_group(pp[:, :], Bp, [(s_p, 2, 0)], lambda kc: pb_s[:, kc, :], [B_pb])
                    if len(pend_ss) >= 2:
                        emit_ss()
                    z = rot('zt', 2)
                    P.op('vector', lambda e, z=z, pg=pg: e.tensor_tensor(
                        out=zt[z][:, 0:512], in0=pg[:, :], in1=rstd[:, 1:513], op=ALU.mult),
                        reads=[Bg, B_rstd], writes=[B_zt[z]])
                    P.op('scalar', lambda e, z=z: e.activation(out=zt[z][:, 0:512], in_=zt[z][:, 0:512], func=AF.Sigmoid),
                         reads=[B_zt[z]], writes=[B_zt[z]])
                    cv = rot('cv', 2)
                    P.op('vector', lambda e, z=z, cv=cv, pp=pp: e.tensor_tensor(
                        out=cva[cv][:], in0=zt[z][:, 0:512], in1=pp[:, :], op=ALU.mult),
                        reads=[B_zt[z], Bp], writes=[B_cva[cv]])
                    P.dma('sync', cvb[cv][:], x2T[c0:c0 + 128, 1 + s:1 + s + 512], reads=[B_x2T], writes=[B_cvb[cv]])
                    P.op('vector', lambda e, cv=cv, j=j: e.tensor_tensor(
                        out=x3_s[:, j, :], in0=cva[cv][:], in1=cvb[cv][:], op=ALU.add),
                        reads=[B_cva[cv], B_cvb[cv]], writes=[B_x3])
                    o = rot('ob', NOB)
                    P.op('scalar', lambda e, o=o, j=j: e.activation(out=obf[o][:], in_=x3_s[:, j, :], func=AF.Square),
                         reads=[B_x3], writes=[B_obf[o]])
                    pend_ss.append((o, j))
                while pend_ss:
                    emit_ss()
                P.op('scalar', lambda e: e.activation(out=rt[:, 1:513], in_=pss[:, :], func=AF.Sqrt, bias=eps_t[:, 0:1],
                                                      scale=1.0 / D), reads=[B_pss, B_const], writes=[B_rt])
                P.op('vector', lambda e: e.reciprocal(out=rstd[:, 1:513], in_=rt[:, 1:513]), reads=[B_rt], writes=[B_rstd])
                for j in range(32):
                    o = rot('of', NOB)
                    P.op('vector', lambda e, o=o, j=j: e.scalar_tensor_tensor(
                        out=of32[o][:], in0=x3_s[:, j, :], scalar=gains_s[:, 3, j:j + 1], in1=rstd[:, 1:513],
                        op0=ALU.mult, op1=ALU.mult), reads=[B_x3, B_rstd, B_const], writes=[B_of32[o]])
                    P.dma('sync', yT[j * 128:(j + 1) * 128, s:s + 512], of32[o][:], reads=[B_of32[o]])

        for ph_name in ('p1', 'p2', 'p3', 'p4', 'p5', 'p6'):
            if ph_name not in phases:
                continue
            if ph_name == 'p1':
                phase1()
            elif ph_name == 'p2':
                phase2()
            elif ph_name == 'p3':
                phase3()
            elif ph_name == 'p4':
                phase4('p2' in phases)
            elif ph_name == 'p5':
                phase5()
            elif ph_name == 'p6':
                phase6()
            P.barrier()

        P.finish()

        @block.sync
        def _(e):
            for f in P.ops['sync']:
                f(e)

        @block.scalar
        def _(e):
            for f in P.ops['scalar']:
                f(e)

        @block.vector
        def _(e):
            for f in P.ops['vector']:
                f(e)

        @block.gpsimd
        def _(e):
            for f in P.ops['gpsimd']:
                f(e)

        @block.tensor
        def _(e):
            for f in P.ops['tensor']:
                f(e)
    return nc


def _t5_bucket(rel):
    half = 16
    max_exact = 8
    ret = np.where(rel > 0, half, 0)
    n = np.abs(rel)
    nf = np.maximum(n, 1).astype(np.float32)
    large = max_exact + (np.log(nf / max_exact) / math.log(128 / max_exact) * (half - max_exact)).astype(np.int32)
    large = np.minimum(large, half - 1)
    return ret + np.where(n < max_exact, n, large)


def _chunked(v, n):
    return np.ascontiguousarray(np.asarray(v, np.float32).reshape(n, 128).T)


def make_inputs(inp, core):
    f32 = np.float32
    if core < 4:
        x = np.asarray(inp['x_prompt'][core]); p = np.asarray(inp['p_prompt'][0, core]); L = 4096
    else:
        x = np.asarray(inp['x_sample'][core - 4]); p = np.asarray(inp['p_sample'][0, core - 4]); L = 2048
    xTp = np.zeros((D, LT + 2), f32)
    xTp[:, 1:L + 1] = x.T
    pTp = np.zeros((256, LT), f32)
    pTp[:, :L] = p.T
    valid = np.zeros((128, LT + 2), f32)
    valid[:, 1:L + 1] = 1.0
    kmask = np.zeros((128, 32), f32)
    tok = np.arange(32)[None, :] * 128 + np.arange(128)[:, None]
    kmask[tok >= L] = NEG
    pos = np.arange(LT, dtype=f32)
    t = (pos / f32(L - 1)).astype(f32)
    bands = np.linspace(1e-4, 15, 16, dtype=f32)
    ang = (f32(2.0 * math.pi / L) * pos[:, None] * bands[None, :]).astype(f32)
    zf = np.concatenate([t[:, None], np.cos(ang), -np.sin(ang)], axis=-1).astype(f32)
    m = {'xT': xTp, 'pT': pTp, 'valid': valid, 'kmask': kmask,
         'zfT': np.ascontiguousarray(zf.T), 'tnb': np.ascontiguousarray(np.broadcast_to(t[None, :], (128, LT)))}
    return m


def shared_inputs(inp):
    f32 = np.float32
    m = {}
    m['w_in'] = np.asarray(inp['w_in'][0]); m['w_attn_o'] = np.asarray(inp['w_attn_o'][0])
    m['w_hyena_o'] = np.asarray(inp['w_hyena_o'][0]); m['w_out'] = np.asarray(inp['w_out'][0])
    m['w_up'] = np.asarray(inp['w_up'][0]); m['w_down'] = np.asarray(inp['w_down'][0])
    m['w_ple_gate'] = np.asarray(inp['w_ple_gate'][0]); m['w_ple'] = np.asarray(inp['w_ple'][0])
    g = np.stack([_chunked(inp['g_mix'][0], 32), _chunked(inp['g_ffn'][0], 32), _chunked(inp['g_ple'][0], 32),
                  _chunked(inp['g_final'], 32)], axis=1)
    m['gains'] = np.ascontiguousarray(g)
    hw = np.asarray(inp['hy_short_w'][0]); hb = np.asarray(inp['hy_short_b'][0])
    m['hcw'] = np.ascontiguousarray(np.stack([_chunked(hw[0], 48), _chunked(hw[1], 48), _chunked(hw[2], 48),
                                              _chunked(hb, 48)], axis=2))
    fw = np.asarray(inp['ffn_conv_w'][0]); fb = np.asarray(inp['ffn_conv_b'][0])
    m['fcw'] = np.ascontiguousarray(np.stack([_chunked(fw[0], NFF), _chunked(fw[1], NFF), _chunked(fw[2], NFF),
                                              _chunked(fb, NFF)], axis=2))
    m['sinkr'] = np.ascontiguousarray(np.broadcast_to(np.asarray(inp['attn_sink'][0], f32)[None, :], (128, 16)))
    m['rbext'] = np.concatenate([np.asarray(inp['rel_bias'], f32), np.full((1, 16), NEG, f32)], axis=0)
    d = np.arange(768) - 384
    bk = _t5_bucket(d)
    T = np.zeros((33, 768), f32)
    T[bk, np.arange(768)] = 1.0
    T[32, :] = (np.abs(d) > 128).astype(f32)
    m['bkt'] = T
    m['identb'] = np.eye(128, dtype=f32).astype(ml_dtypes.bfloat16)
    m['hskip'] = _chunked(inp['hy_skip'][0], 16)
    m['fw1'] = np.asarray(inp['hy_filt_w1'][0], f32); m['fw2'] = np.asarray(inp['hy_filt_w2'][0], f32)
    m['fw3'] = np.asarray(inp['hy_filt_w3'][0], f32)
    m['fpar'] = np.ascontiguousarray(np.stack([inp['hy_filt_b1'][0], inp['hy_filt_f1'][0], inp['hy_filt_b2'][0],
                                               inp['hy_filt_f2'][0]], axis=1).astype(f32))
    m['decay'] = _chunked(np.asarray(inp['hy_decay'][0]).reshape(-1), 32)
    m.update(_fft_tables())
    return m


def _fft_tables():
    bf = ml_dtypes.bfloat16
    N = 8192
    th = np.arange(32)[:, None]; fa = np.arange(33)[None, :]
    a = 2 * np.pi * th * fa / 64.0
    fh = np.concatenate([np.cos(a), -np.sin(a), np.sin(a)], axis=1)
    tl = np.arange(128)[:, None, None]; fa3 = np.arange(33)[None, :, None]; fb = np.arange(128)[None, None, :]
    ph = 2 * np.pi * ((tl * (fa3 + 64 * fb)) % N) / N
    g = np.stack([np.cos(ph), -np.sin(ph)], axis=2)
    fbp = np.arange(128)[:, None]; tlo = np.arange(128)[None, :]
    p2 = 2 * np.pi * ((fbp * tlo) % 128) / 128.0
    gi = np.stack([np.cos(p2), np.sin(p2)], axis=1)
    fa_ = np.arange(64)[:, None, None]; tl_ = np.arange(128)[None, :, None]; th_ = np.arange(32)[None, None, :]
    th3 = 2 * np.pi * ((fa_ * (128 * th_ + tl_)) % N) / N
    w = np.zeros(64); w[0] = 1; w[32] = 1; w[1:32] = 2
    mi = np.concatenate([w[:, None, None] * np.cos(th3) / N, -w[:, None, None] * np.sin(th3) / N], axis=0)
    return {'fh_t': fh.astype(np.float32).astype(bf), 'g_t': g.reshape(128, -1).astype(np.float32).astype(bf),
            'gi_t': gi.reshape(128, -1).astype(np.float32).astype(bf), 'mi_t': mi.reshape(128, -1).astype(np.float32).astype(bf)}


_NC_CACHE = {}


def kernel(**inputs):
    key = 'full'
    if key not in _NC_CACHE:
        _NC_CACHE[key] = build()
    nc = _NC_CACHE[key]
    sh = shared_inputs(inputs)
    in_maps = []
    for c in range(8):
        m = dict(sh)
        m.update(make_inputs(inputs, c))
        in_maps.append(m)
    names = set()
    for alloc in nc.allocations:
        if isinstance(alloc, mybir.MemoryLocationSet) and alloc.kind == "ExternalInput":
            names.add(alloc.memorylocations[0].name)
    in_maps = [{k: v for k, v in m.items() if k in names} for m in in_maps]
    res = run_bass_kernel_spmd(nc, in_maps, core_ids=list(range(8)))
    yp = np.stack([res.results[c]['yT'].T for c in range(4)], axis=0).astype(np.float32)
    ys = np.stack([res.results[c]['yT'][:, :2048].T for c in range(4, 8)], axis=0).astype(np.float32)
    return (np.ascontiguousarray(yp), np.ascontiguousarray(ys))
```

```python
import math
from contextlib import ExitStack
import numpy as np
import ml_dtypes
import concourse.bass as bass
import concourse.mybir as mybir
from concourse.bass_utils import run_bass_kernel_spmd

F32 = mybir.dt.float32
BF16 = mybir.dt.bfloat16
AF = mybir.ActivationFunctionType
ALU = mybir.AluOpType

D = 4096
LT = 4096
NT = 8
TW = 514
DFF = 11008
NFF = 86
IN_COLS = 17408
EPS = 1e-6
NEG = -1e30
ENGS = ['sync', 'scalar', 'vector', 'gpsimd', 'tensor']


class Buf:
    __slots__ = ('name', 'w', 'r', 'track')

    def __init__(self, name, track=True):
        self.name = name
        self.w = {}
        self.r = {}
        self.track = track


class Prog:
    def __init__(self, nc, es):
        self.nc = nc
        self.es = es
        self.ops = {e: [] for e in ENGS}
        self.waited = {e: {} for e in ENGS}
        self.esem = {}
        self.ecnt = {}
        for e in ['scalar', 'vector', 'tensor', 'gpsimd']:
            self.esem[e] = es.enter_context(nc.semaphore('es_' + e))
            self.ecnt[e] = 0
        self.dpool = {}
        self.dnext = {}
        for q, n in [('sync', 14), ('gpsimd', 8)]:
            self.dpool[q] = [[es.enter_context(nc.semaphore(f'd_{q}{i}')), 0] for i in range(n)]
            self.dnext[q] = 0
        self.sid = {}

    def _sid(self, h):
        return id(h)

    def _need(self, eng, reads, writes):
        need = {}
        own = self._sid(self.esem[eng]) if eng in self.esem else None

        def add(d, skip_own):
            for sid, (h, v) in d.items():
                if skip_own and sid == own:
                    continue
                if sid not in need or need[sid][1] < v:
                    need[sid] = (h, v)
        for b in reads:
            if b.track:
                add(b.w, False)
        for b in writes:
            if b.track:
                add(b.r, True)
                add(b.w, True)
        wd = self.waited[eng]
        for sid, (h, v) in need.items():
            if wd.get(sid, 0) < v:
                wd[sid] = v
                self.ops[eng].append(lambda e, h=h, v=v: e.wait_ge(h, v))

    def _mark(self, reads, writes, h, v):
        sid = self._sid(h)
        for b in reads:
            if b.track:
                b.r[sid] = (h, v)
        for b in writes:
            if b.track:
                b.w = {sid: (h, v)}
                b.r = {}

    def op(self, eng, fns, reads=(), writes=()):
        if not isinstance(fns, (list, tuple)):
            fns = [fns]
        self._need(eng, reads, writes)
        self.ecnt[eng] += 1
        v = self.ecnt[eng]
        h = self.esem[eng]
        for f in fns[:-1]:
            self.ops[eng].append(f)
        last = fns[-1]
        self.ops[eng].append(lambda e, last=last, h=h: last(e).then_inc(h, 1))
        self._mark(reads, writes, h, v)

    def dma(self, q, out, in_, reads=(), writes=(), **kw):
        self._need(q, reads, writes)
        pool = self.dpool[q]
        i = self.dnext[q]
        self.dnext[q] = (i + 1) % len(pool)
        h, cnt = pool[i]
        wd = self.waited[q]
        sid = self._sid(h)
        if cnt > 0 and wd.get(sid, 0) < cnt:
            wd[sid] = cnt
            self.ops[q].append(lambda e, h=h, v=cnt: e.wait_ge(h, v))
        cnt += 16
        pool[i][1] = cnt
        self.ops[q].append(lambda e, out=out, in_=in_, h=h, kw=kw: e.dma_start(out=out, in_=in_, **kw).then_inc(h, 16))
        self._mark(reads, writes, h, cnt)

    def barrier(self):
        deps = []
        for e2, h in self.esem.items():
            if self.ecnt[e2] > 0:
                deps.append((h, self.ecnt[e2]))
        for q in self.dpool:
            for h, cnt in self.dpool[q]:
                if cnt > 0:
                    deps.append((h, cnt))
        for eng in ENGS:
            wd = self.waited[eng]
            for h, v in deps:
                sid = self._sid(h)
                if wd.get(sid, 0) < v:
                    wd[sid] = v
                    self.ops[eng].append(lambda e, h=h, v=v: e.wait_ge(h, v))

    def finish(self):
        for q in self.dpool:
            for h, cnt in self.dpool[q]:
                if cnt > 0:
                    self.ops[q].append(lambda e, h=h, v=cnt: e.wait_ge(h, v))
        for q in ['sync']:
            for e2 in self.esem:
                if self.ecnt[e2] > 0:
                    self.ops[q].append(lambda e, h=self.esem[e2], v=self.ecnt[e2]: e.wait_ge(h, v))
            for h, cnt in self.dpool['gpsimd']:
                if cnt > 0:
                    self.ops[q].append(lambda e, h=h, v=cnt: e.wait_ge(h, v))


def build(phases=('p1', 'p2', 'p3', 'p4', 'p5', 'p6'), dbg=()):
    nc = bass.Bass("TRN2", target_bir_lowering=False)

    def din(name, shape, dt=F32):
        return nc.dram_tensor(name, list(shape), dt, kind="ExternalInput").ap()

    def dscr(name, shape, dt=F32):
        kind = "ExternalOutput" if name in dbg else "Internal"
        return nc.dram_tensor(name, list(shape), dt, kind=kind).ap()

    need = set(phases)
    xT = din("xT", [D, LT + 2])
    valid = din("valid", [128, LT + 2])
    gains = din("gains", [128, 4, 32])
    identb = din("identb", [128, 128], BF16)
    w_in = din("w_in", [D, IN_COLS]) if need & {'p1', 'p4'} else None
    hcw = din("hcw", [128, 48, 4]) if 'p1' in need else None
    kmask = din("kmask", [128, 32]) if 'p3' in need else None
    sinkr = din("sinkr", [128, 16]) if 'p3' in need else None
    rbext = din("rbext", [33, 16]) if 'p3' in need else None
    bkt = din("bkt", [33, 768]) if 'p3' in need else None
    w_ao = din("w_attn_o", [2048, D]) if 'p4' in need else None
    w_ho = din("w_hyena_o", [2048, D]) if 'p4' in need else None
    w_out = din("w_out", [D, D]) if 'p4' in need else None
    hskip = din("hskip", [128, 16]) if 'p4' in need else None
    if 'p2' in need:
        zfT = din("zfT", [33, LT]); tnb = din("tnb", [128, LT])
        fw1 = din("fw1", [33, 64]); fw2 = din("fw2", [64, 64]); fw3 = din("fw3", [64, 4096])
        fpar = din("fpar", [64, 4]); decay = din("decay", [128, 32])
        fh_t = din("fh_t", [32, 99], BF16); g_t = din("g_t", [128, 33 * 256], BF16)
        gi_t = din("gi_t", [128, 256], BF16); mi_t = din("mi_t", [128, 4096], BF16)
    w_up = din("w_up", [D, 2 * DFF]) if 'p5' in need else None
    w_down = din("w_down", [DFF, D]) if 'p5' in need else None
    fcw = din("fcw", [128, NFF, 4]) if 'p5' in need else None
    w_pg = din("w_ple_gate", [D, D]) if 'p6' in need else None
    w_ple = din("w_ple", [256, D]) if 'p6' in need else None
    pT = din("pT", [256, LT]) if 'p6' in need else None
    yT = nc.dram_tensor("yT", [D, LT], F32, kind="ExternalOutput").ap()

    qT = dscr("qT", [2048, LT], BF16)
    kT = dscr("kT", [512, LT], BF16)
    vS = dscr("vS", [LT, 512], BF16)
    uT = dscr("uT", [2048, LT])
    hx0T = dscr("hx0T", [2048, LT])
    ycT = dscr("ycT", [2048, LT])
    attnT = dscr("attnT", [2048, LT], BF16)
    x1T = dscr("x1T", [D, LT + 2])
    x2T = dscr("x2T", [D, LT + 2])
    hfb = dscr("hfb", [4096, LT], BF16)
    NPIECE = 72 + 160 + 268 + 64
    wcaches = [dscr(f"wcache{i}", [128, 128, 4096], BF16) for i in range((NPIECE + 127) // 128)]

    B_qT, B_kT, B_vS, B_uT, B_hx0T, B_ycT, B_attnT, B_x1T, B_x2T = [Buf(n, track=False) for n in ('qT','kT','vS','uT','hx0T','ycT','attnT','x1T','x2T')]
    with ExitStack() as es:
        E = es.enter_context
        P = Prog(nc, es)

        def sb(name, shape, dt):
            return E(nc.sbuf_tensor(name, list(shape), dt))

        xin = [sb(f"xin{i}", [128, 2, TW], F32) for i in range(2)]
        B_xin = [Buf(f"xin{i}") for i in range(2)]
        sq = [sb(f"sq{i}", [128, 2, TW], BF16) for i in range(2)]
        B_sq = [Buf(f"sq{i}") for i in range(2)]
        hg = sb("hg", [128, 32, TW], BF16)
        B_hg = Buf("hg")
        rt = sb("rt", [128, TW], F32)
        B_rt = Buf("rt")
        rstd = sb("rstd", [128, TW], F32)
        B_rstd = Buf("rstd")
        NWB = 4
        wb = [sb(f"wb{i}", [128, 32, 128], BF16) for i in range(NWB)]
        B_wb = [Buf(f"wb{i}") for i in range(NWB)]
        NOB = 3
        obf = [sb(f"obf{i}", [128, 512], BF16) for i in range(NOB)]
        B_obf = [Buf(f"obf{i}") for i in range(NOB)]
        of32 = [sb(f"of{i}", [128, 512], F32) for i in range(NOB)]
        B_of32 = [Buf(f"of{i}") for i in range(NOB)]
        zt = [sb(f"zt{i}", [128, TW], F32) for i in range(2)]
        B_zt = [Buf(f"zt{i}") for i in range(2)]
        cva = [sb(f"cva{i}", [128, 512], F32) for i in range(2)]
        B_cva = [Buf(f"cva{i}") for i in range(2)]
        cvb = [sb(f"cvb{i}", [128, 512], F32) for i in range(2)]
        B_cvb = [Buf(f"cvb{i}") for i in range(2)]
        hvc = sb("hvc", [128, 512], F32)
        B_hvc = Buf("hvc")
        vmask = sb("vmask", [128, TW], F32)
        B_vmask = Buf("vmask")
        big = sb("big", [128, NFF * 512], BF16)
        B_big = Buf("big")
        ones_b = sb("ones_b", [128, 128], BF16)
        ident = sb("ident", [128, 128], BF16)
        eps_t = sb("eps_t", [128, 1], F32)
        gains_s = sb("gains_s", [128, 4, 32], F32)
        hcw_s = sb("hcw_s", [128, 48, 4], F32)
        fcw_s = sb("fcw_s", [128, NFF, 4], F32)
        hskip_s = sb("hskip_s", [128, 16], F32)
        B_const = Buf("const")

        psm = [E(nc.psum_tensor(f"psm{i}", [128, 512], F32)) for i in range(3)]
        B_psm = [Buf(f"psm{i}") for i in range(3)]
        psh = E(nc.psum_tensor("psh", [128, 512], F32))
        B_psh = [Buf(f"psh{i}") for i in range(8)]
        pss = E(nc.psum_tensor("pss", [128, 512], F32))
        B_pss = Buf("pss")
        pss2 = E(nc.psum_tensor("pss2", [128, 512], F32))
        B_pss2 = Buf("pss2")
        psx = E(nc.psum_tensor("psx", [128, 512], F32))
        B_psx = Buf("psx")
        pst = E(nc.psum_tensor("pst", [128, 1024], BF16))
        B_pst = Buf("pst")

        block = E(nc.Block())

        P.op('vector', lambda e: e.memset(ones_b[:], 1.0), writes=[B_const])
        P.op('vector', lambda e: e.memset(eps_t[:], EPS), writes=[B_const])
        P.dma('sync', ident[:], identb, writes=[B_const])
        P.dma('sync', gains_s[:], gains, writes=[B_const])
        if hcw is not None:
            P.dma('sync', hcw_s[:], hcw, writes=[B_const])
        if fcw is not None:
            P.dma('sync', fcw_s[:], fcw, writes=[B_const])
        if hskip is not None:
            P.dma('sync', hskip_s[:], hskip, writes=[B_const])

        cnt = {'pm': 0, 'ph': 0, 'ob': 0, 'of': 0, 'zt': 0, 'cv': 0, 'wb': 0}

        def rot(key, n):
            v = cnt[key] % n
            cnt[key] += 1
            return v

        def prep(src, s, gi):
            for qd in range(16):
                sl = qd % 2
                P.dma('sync', xin[sl][:], src[qd * 256:(qd + 1) * 256, s:s + TW].rearrange("(kc p) t -> p kc t", p=128),
                      writes=[B_xin[sl]])
                P.op('scalar', lambda e, sl=sl: e.activation(out=sq[sl][:], in_=xin[sl][:], func=AF.Square),
                     reads=[B_xin[sl]], writes=[B_sq[sl]])
                fns = []
                for j in range(2):
                    kc = qd * 2 + j
                    fns.append(lambda e, sl=sl, j=j, kc=kc: e.tensor_scalar(
                        out=hg[:, kc, :], in0=xin[sl][:, j, :], scalar1=gains_s[:, gi, kc:kc + 1], scalar2=None,
                        op0=ALU.mult))
                P.op('vector', fns, reads=[B_xin[sl], B_const], writes=[B_hg])
                fns = []
                for j in range(2):
                    kc = qd * 2 + j
                    fns.append(lambda e, sl=sl, j=j, kc=kc: e.matmul(
                        pss[:, :], lhsT=ones_b[:], rhs=sq[sl][:, j, 1:513], start=(kc == 0), stop=(kc == 31)))
                    fns.append(lambda e, sl=sl, j=j, kc=kc: e.matmul(
                        pss2[:, 0:2], lhsT=ones_b[:], rhs=sq[sl][:, j, 0:TW:513], start=(kc == 0), stop=(kc == 31)))
                P.op('tensor', fns, reads=[B_sq[sl], B_const], writes=[B_pss, B_pss2])
            P.op('scalar', [
                lambda e: e.activation(out=rt[:, 1:513], in_=pss[:, :], func=AF.Sqrt, bias=eps_t[:, 0:1], scale=1.0 / D),
                lambda e: e.activation(out=rt[:, 0:TW:513], in_=pss2[:, 0:2], func=AF.Sqrt, bias=eps_t[:, 0:1],
                                       scale=1.0 / D)],
                 reads=[B_pss, B_pss2, B_const], writes=[B_rt])
            P.op('vector', lambda e: e.reciprocal(out=rstd[:], in_=rt[:]), reads=[B_rt], writes=[B_rstd])

        pieces = {}
        wb_x = [big[:, 32768 + i * 4096:32768 + (i + 1) * 4096].rearrange("p (k c) -> p k c", c=128) for i in range(2)]
        B_wbx = [Buf("wbx0"), Buf("wbx1")]
        wbl = list(wb)
        Bwbl = list(B_wb)

        def set_wb(extra):
            wbl[:] = list(wb) + (wb_x if extra else [])
            Bwbl[:] = list(B_wb) + (B_wbx if extra else [])
            cnt['wb'] = 0

        def wload(src_rows, nkc, key=None):
            sl = rot('wb', len(wbl))
            if key is not None and key in pieces:
                idx, Bp = pieces[key]
                P.dma('gpsimd', wbl[sl][:, 0:nkc, :], wcaches[idx // 128][idx % 128, :, 0:nkc * 128].rearrange("p (k c) -> p k c", c=128),
                      reads=[Bp], writes=[Bwbl[sl]])
                return sl
            P.dma('gpsimd', wbl[sl][:, 0:nkc, :], src_rows.rearrange("(kc p) c -> p kc c", p=128), writes=[Bwbl[sl]])
            if key is not None:
                idx = len(pieces)
                Bp = Buf(f"piece{idx}")
                pieces[key] = (idx, Bp)
                P.dma('sync', wcaches[idx // 128][idx % 128, :, 0:nkc * 128].rearrange("p (k c) -> p k c", c=128), wbl[sl][:, 0:nkc, :],
                      reads=[Bwbl[sl]], writes=[Bp])
            return sl

        def mm_group(ps_ap, B_ps, pieces, rhs_fn, extra_reads, n_total=None):
            tot = sum(p[1] for p in pieces)
            i = 0
            for (sl, nkc, kb) in pieces:
                fns = []
                for k in range(nkc):
                    fns.append(lambda e, sl=sl, k=k, kb=kb, i=i: e.matmul(
                        ps_ap, lhsT=wbl[sl][:, k, :], rhs=rhs_fn(kb + k), start=(i == 0), stop=(i == tot - 1)))
                    i += 1
                P.op('tensor', fns, reads=list(extra_reads) + [Bwbl[sl]], writes=[B_ps])

        def phase1():
            chunks = [('q', h * 128, h) for h in range(16)]
            chunks += [('k', 2048 + h * 128, h) for h in range(4)]
            chunks += [('v', 2560 + h * 128, h) for h in range(4)]
            for c in range(16):
                chunks += [('hv', 3072 + c * 128, c), ('hx1', 5120 + c * 128, 16 + c), ('hx0', 7168 + c * 128, 32 + c)]
            SC = 128.0 ** -0.5
            for ti in range(NT):
                s = ti * 512
                prep(xT, s, 0)
                P.dma('sync', vmask[:], valid[:, s:s + TW], writes=[B_vmask])
                pend = [wload(w_in[:, chunks[i][1]:chunks[i][1] + 128], 32, ('p1', i)) for i in range(2)]
                for n, (kind, col0, idx) in enumerate(chunks):
                    if n + 2 < len(chunks):
                        c2 = chunks[n + 2][1]
                        pend.append(wload(w_in[:, c2:c2 + 128], 32, ('p1', n + 2)))
                    sl = pend.pop(0)
                    pm = rot('pm', 3)
                    mm_group(psm[pm][:, :], B_psm[pm], [(sl, 32, 0)], lambda kc: hg[:, kc, 1:513], [B_hg])
                    if kind in ('q', 'k', 'v'):
                        o = rot('ob', NOB)
                        if kind == 'q':
                            P.op('vector', lambda e, o=o, pm=pm: e.scalar_tensor_tensor(
                                out=obf[o][:], in0=psm[pm][:, :], scalar=SC, in1=rstd[:, 1:513], op0=ALU.mult,
                                op1=ALU.mult), reads=[B_psm[pm], B_rstd], writes=[B_obf[o]])
                            P.dma('sync', qT[idx * 128:(idx + 1) * 128, s:s + 512], obf[o][:], reads=[B_obf[o]])
                        elif kind == 'k':
                            P.op('vector', lambda e, o=o, pm=pm: e.tensor_tensor(
                                out=obf[o][:], in0=psm[pm][:, :], in1=rstd[:, 1:513], op=ALU.mult),
                                reads=[B_psm[pm], B_rstd], writes=[B_obf[o]])
                            P.dma('sync', kT[idx * 128:(idx + 1) * 128, s:s + 512], obf[o][:], reads=[B_obf[o]])
                        else:
                            P.op('vector', lambda e, o=o, pm=pm: e.tensor_tensor(
                                out=obf[o][:], in0=psm[pm][:, :], in1=rstd[:, 1:513], op=ALU.mult),
                                reads=[B_psm[pm], B_rstd], writes=[B_obf[o]])
                            fns = [lambda e, o=o, b=b: e.transpose(pst[:, b * 128:(b + 1) * 128],
                                                                   obf[o][:, b * 128:(b + 1) * 128], ident[:])
                                   for b in range(4)]
                            P.op('tensor', fns, reads=[B_obf[o], B_const], writes=[B_pst])
                            o2 = rot('ob', NOB)
                            P.op('scalar', lambda e, o2=o2: e.copy(out=obf[o2][:], in_=pst[:, 0:512]),
                                 reads=[B_pst], writes=[B_obf[o2]])
                            P.dma('sync', vS[s:s + 512, idx * 128:(idx + 1) * 128].rearrange("(b p) d -> p b d", p=128),
                                  obf[o2][:].rearrange("p (b d) -> p b d", b=4), reads=[B_obf[o2]])
                        continue
                    ph = rot('ph', 8)
                    mm_group(psh[:, ph * 4:ph * 4 + 2], B_psh[ph], [(sl, 32, 0)], lambda kc: hg[:, kc, 0:TW:513], [B_hg])
                    z = rot('zt', 2)
                    P.op('vector', [
                        lambda e, z=z, pm=pm: e.tensor_tensor(out=zt[z][:, 1:513], in0=psm[pm][:, :], in1=rstd[:, 1:513],
                                                             op=ALU.mult),
                        lambda e, z=z, ph=ph: e.tensor_tensor(out=zt[z][:, 0:TW:513], in0=psh[:, ph * 4:ph * 4 + 2],
                                                             in1=rstd[:, 0:TW:513], op=ALU.mult)],
                         reads=[B_psm[pm], B_psh[ph], B_rstd], writes=[B_zt[z]])
                    cv = rot('cv', 2)
                    P.op('vector', lambda e, z=z, cv=cv, idx=idx: e.tensor_scalar(
                        out=cva[cv][:], in0=zt[z][:, 0:512], scalar1=hcw_s[:, idx, 0:1], scalar2=hcw_s[:, idx, 3:4],
                        op0=ALU.mult, op1=ALU.add), reads=[B_zt[z], B_const], writes=[B_cva[cv]])
                    P.op('vector', lambda e, z=z, cv=cv, idx=idx: e.scalar_tensor_tensor(
                        out=cvb[cv][:], in0=zt[z][:, 1:513], scalar=hcw_s[:, idx, 1:2], in1=cva[cv][:], op0=ALU.mult,
                        op1=ALU.add), reads=[B_zt[z], B_cva[cv], B_const], writes=[B_cvb[cv]])
                    if kind == 'hv':
                        P.op('vector', lambda e, z=z, cv=cv, idx=idx: e.scalar_tensor_tensor(
                            out=hvc[:], in0=zt[z][:, 2:514], scalar=hcw_s[:, idx, 2:3], in1=cvb[cv][:], op0=ALU.mult,
                            op1=ALU.add), reads=[B_zt[z], B_cvb[cv], B_const], writes=[B_hvc])
                    elif kind == 'hx1':
                        P.op('vector', lambda e, z=z, cv=cv, idx=idx: e.scalar_tensor_tensor(
                            out=cva[cv][:], in0=zt[z][:, 2:514], scalar=hcw_s[:, idx, 2:3], in1=cvb[cv][:], op0=ALU.mult,
                            op1=ALU.add), reads=[B_zt[z], B_cvb[cv], B_const], writes=[B_cva[cv]])
                        P.op('vector', lambda e, cv=cv: e.tensor_tensor(
                            out=cvb[cv][:], in0=cva[cv][:], in1=hvc[:], op=ALU.mult),
                            reads=[B_cva[cv], B_hvc], writes=[B_cvb[cv]])
                        o = rot('of', NOB)
                        P.op('vector', lambda e, cv=cv, o=o: e.tensor_tensor(
                            out=of32[o][:], in0=cvb[cv][:], in1=vmask[:, 1:513], op=ALU.mult),
                            reads=[B_cvb[cv], B_vmask], writes=[B_of32[o]])
                        c = idx - 16
                        P.dma('sync', uT[c * 128:(c + 1) * 128, s:s + 512], of32[o][:], reads=[B_of32[o]])
                    else:
                        o = rot('of', NOB)
                        P.op('vector', lambda e, z=z, cv=cv, idx=idx, o=o: e.scalar_tensor_tensor(
                            out=of32[o][:], in0=zt[z][:, 2:514], scalar=hcw_s[:, idx, 2:3], in1=cvb[cv][:], op0=ALU.mult,
                            op1=ALU.add), reads=[B_zt[z], B_cvb[cv], B_const], writes=[B_of32[o]])
                        c = idx - 32
                        P.dma('sync', hx0T[c * 128:(c + 1) * 128, s:s + 512], of32[o][:], reads=[B_of32[o]])


        att_o = [sb(f"att_o{i}", [128, 512], BF16) for i in range(2)]
        B_att_o = [Buf(f"att_o{i}") for i in range(2)]
        biasT = big[:, 24576:30720].rearrange("p (a t) -> p a t", a=12)
        bkt_s = big[0:33, 30720:32256].bitcast(F32)
        rb_s = sb("rb_s", [33, 16], F32)
        esink = sb("esink", [128, 16], F32)
        kmask_s = sb("kmask_s", [128, 32], F32)
        B_c3 = Buf("c3")
        pb_s = sb("pb_s", [128, 2, 512], BF16)
        B_pb = Buf("pb")
        zero_c = sb("zero_c", [128, 32], F32)
        P.op('vector', lambda e: e.memset(zero_c[:], 0.0), writes=[B_const])
        accs = [(psm[0], B_psm[0]), (psm[1], B_psm[1]), (psm[2], B_psm[2]), (psx, B_psx)]
        cnt['acc'] = 0
        cnt['ao'] = 0

        def acc():
            return accs[rot('acc', len(accs))]

        def set_accs(lst):
            accs[:] = lst
            cnt['acc'] = 0
        base_accs = list(accs)

        def phase2():
            PI = math.pi
            hgf = hg[:].rearrange("p a b -> p (a b)")
            w3_s = hgf[0:64, 0:8192].bitcast(F32)
            w1_s = hgf[0:33, 8192:8320].bitcast(F32)
            w2_s = hgf[0:64, 8320:8448].bitcast(F32)
            fp_s = hgf[0:64, 8448:8464].bitcast(F32)
            nd_s = hgf[:, 8464:8528].bitcast(F32)
            tnb_s = big[:, 0:8192].bitcast(F32)
            zf_s = big[0:33, 8192:16384].bitcast(F32)
            B_f = Buf("filt")
            P.dma('sync', w3_s, fw3, writes=[B_f])
            P.dma('sync', w1_s, fw1, writes=[B_f])
            P.dma('sync', w2_s, fw2, writes=[B_f])
            P.dma('sync', fp_s[:, 0:4], fpar, writes=[B_f])
            P.dma('sync', nd_s, decay, writes=[B_f])
            P.dma('sync', tnb_s, tnb, writes=[B_f])
            P.dma('sync', zf_s, zfT, writes=[B_f])
            P.op('vector', [
                lambda e: e.tensor_tensor(out=fp_s[:, 4:5], in0=fp_s[:, 0:1], in1=fp_s[:, 1:2], op=ALU.mult),
                lambda e: e.tensor_tensor(out=fp_s[:, 5:6], in0=fp_s[:, 2:3], in1=fp_s[:, 3:4], op=ALU.mult),
                lambda e: e.tensor_scalar(out=nd_s, in0=nd_s, scalar1=-1.0, scalar2=None, op0=ALU.mult)],
                 reads=[B_f], writes=[B_f])
            for tt in range(8):
                tsl = slice(tt * 512, (tt + 1) * 512)
                gprev = None
                for layer in range(2):
                    pa, Ba = acc()
                    if layer == 0:
                        P.op('tensor', lambda e, pa=pa, tsl=tsl: e.matmul(pa[0:64, :], lhsT=w1_s, rhs=zf_s[:, tsl],
                                                                         start=True, stop=True), reads=[B_f], writes=[Ba])
                    else:
                        P.op('tensor', lambda e, pa=pa, gp=gprev: e.matmul(pa[0:64, :], lhsT=w2_s, rhs=cva[gp][0:64, :],
                                                                          start=True, stop=True),
                             reads=[B_f, B_cva[gprev]], writes=[Ba])
                    cv = rot('cv', 2)
                    fi, fbi = (1, 4) if layer == 0 else (3, 5)
                    P.op('vector', lambda e, cv=cv, pa=pa, fi=fi, fbi=fbi: e.tensor_scalar(
                        out=cvb[cv][0:64, :], in0=pa[0:64, :], scalar1=fp_s[:, fi:fi + 1], scalar2=fp_s[:, fbi:fbi + 1],
                        op0=ALU.mult, op1=ALU.add), reads=[Ba, B_f], writes=[B_cvb[cv]])
                    z = rot('zt', 2)
                    P.op('vector', [
                        lambda e, cv=cv, z=z: e.tensor_scalar(out=zt[z][0:64, 0:512], in0=cvb[cv][0:64, :], scalar1=PI,
                                                             scalar2=-2 * PI, op0=ALU.is_gt, op1=ALU.mult),
                        lambda e, cv=cv: e.tensor_scalar(out=cva[cv][0:64, :], in0=cvb[cv][0:64, :], scalar1=-PI,
                                                        scalar2=2 * PI, op0=ALU.is_lt, op1=ALU.mult)],
                         reads=[B_cvb[cv]], writes=[B_zt[z], B_cva[cv]])
                    P.op('vector', lambda e, cv=cv, z=z: e.tensor_tensor(out=zt[z][0:64, 0:512], in0=zt[z][0:64, 0:512],
                                                                        in1=cva[cv][0:64, :], op=ALU.add),
                         reads=[B_zt[z], B_cva[cv]], writes=[B_zt[z]])
                    P.op('vector', lambda e, cv=cv, z=z: e.tensor_tensor(out=cva[cv][0:64, :], in0=cvb[cv][0:64, :],
                                                                        in1=zt[z][0:64, 0:512], op=ALU.add),
                         reads=[B_zt[z], B_cvb[cv]], writes=[B_cva[cv]])
                    P.op('scalar', lambda e, cv=cv: e.activation(out=cva[cv][0:64, :], in_=cva[cv][0:64, :], func=AF.Sin),
                         reads=[B_cva[cv]], writes=[B_cva[cv]])
                    gprev = cv
                for k in range(32):
                    pa, Ba = acc()
                    P.op('tensor', lambda e, pa=pa, k=k, gp=gprev: e.matmul(
                        pa[:, :], lhsT=w3_s[:, k * 128:(k + 1) * 128], rhs=cva[gp][0:64, :], start=True, stop=True),
                        reads=[B_f, B_cva[gprev]], writes=[Ba])
                    z = rot('zt', 2)
                    P.op('scalar', lambda e, z=z, k=k, tsl=tsl: e.activation(
                        out=zt[z][:, 0:512], in_=tnb_s[:, tsl], func=AF.Exp, scale=nd_s[:, k:k + 1]),
                        reads=[B_f], writes=[B_zt[z]])
                    o = rot('ob', NOB)
                    fns = [lambda e, o=o, pa=pa, z=z: e.tensor_tensor(out=obf[o][:], in0=pa[:, :], in1=zt[z][:, 0:512],
                                                                      op=ALU.mult)]
                    if k >= 16 and tt == 0:
                        fns.append(lambda e, o=o: e.memset(obf[o][:, 0:1], 0.0))
                    P.op('vector', fns, reads=[Ba, B_zt[z]], writes=[B_obf[o]])
                    P.dma('sync', hfb[k * 128:(k + 1) * 128, tsl], obf[o][:], reads=[B_obf[o]])
            P.barrier()
            NFA = 33
            A_sb = big[:, 0:9504].rearrange("p (r c) -> p r c", c=96)
            slab = [big[0:32, 9504 + i * 4096:9504 + (i + 1) * 4096].rearrange("p (c t) -> p c t", t=128) for i in range(3)]
            B_slab = [Buf(f"slab{i}") for i in range(3)]
            Ya = big[:, 21792:25888].rearrange("p (c r) -> p c r", r=128)
            Yb = big[:, 25888:29984].rearrange("p (c r) -> p c r", r=128)
            y_sb = big[0:32, 29984:38176].bitcast(F32).rearrange("p (c t) -> p c t", t=128)
            D_sb = hgf[:, 0:4096].rearrange("p (t c) -> p t c", c=32)
            G_s = hgf[:, 4096:4096 + 8448].rearrange("p (a s f) -> p a s f", s=2, f=128)
            MI_s = wb[0][:].rearrange("p a b -> p (a b)").rearrange("p (t h) -> p t h", h=32)
            wb1 = wb[1][:].rearrange("p a b -> p (a b)")
            GI_s = wb1[:, 0:256].rearrange("p (s t) -> p s t", s=2)
            FH_s = wb1[0:32, 256:355]
            B_A, B_Y, B_D, B_ysb, B_tab = [Buf("A0"), Buf("A1")], [Buf("Y0"), Buf("Y1")], [Buf("D0"), Buf("D1")], [Buf("ysb0"), Buf("ysb1")], Buf("tab")
            P.dma('sync', G_s, g_t.rearrange("p (a s f) -> p a s f", s=2, f=128), writes=[B_tab])
            P.dma('sync', MI_s, mi_t.rearrange("p (t h) -> p t h", h=32), writes=[B_tab])
            P.dma('sync', GI_s, gi_t.rearrange("p (s t) -> p s t", s=2), writes=[B_tab])
            P.dma('sync', FH_s, fh_t, writes=[B_tab])
            P.op('vector', [lambda e: e.memset(Ya, 0.0), lambda e: e.memset(Yb, 0.0)], writes=B_Y)
            for g in range(64):
                c0 = g * 32
                P.dma('gpsimd', slab[0], uT[c0:c0 + 32, :].rearrange("c (th tl) -> th c tl", tl=128), writes=[B_slab[0]])
                P.dma('sync', slab[1], hfb[c0:c0 + 32, :].rearrange("c (th tl) -> th c tl", tl=128), writes=[B_slab[1]])
                P.dma('sync', slab[2], hfb[2048 + c0:2048 + c0 + 32, :].rearrange("c (th tl) -> th c tl", tl=128),
                      writes=[B_slab[2]])
                for si in range(3):
                    for cb in range(8):
                        pa, Ba = acc()
                        fns = [lambda e, pa=pa, si=si, cb=cb, k=k: e.matmul(
                            pa[:, k * 128:k * 128 + 99], lhsT=slab[si][:, cb * 4 + k, :], rhs=FH_s, start=True, stop=True)
                            for k in range(4)]
                        P.op('tensor', fns, reads=[B_slab[si], B_tab], writes=[Ba])
                        col = si * 32 + cb * 4
                        if cb % 2 == 0:
                            P.op('scalar', lambda e, pa=pa, col=col: e.copy(
                                out=A_sb[:, :, col:col + 4].rearrange("p r k -> p k r"),
                                in_=pa[:, :].rearrange("p (k r) -> p k r", r=128)[:, :, 0:99]),
                                reads=[Ba], writes=[B_A[0]])
                        else:
                            P.op('vector', lambda e, pa=pa, col=col: e.tensor_copy(
                                out=A_sb[:, :, col:col + 4].rearrange("p r k -> p k r"),
                                in_=pa[:, :].rearrange("p (k r) -> p k r", r=128)[:, :, 0:99]),
                                reads=[Ba], writes=[B_A[1]])
                for fa0 in range(0, NFA, 2):
                    nf = min(2, NFA - fa0)
                    pa, Ba = acc()
                    fns = []
                    for fl in range(nf):
                        fa = fa0 + fl
                        b0 = fl * 192
                        fns.append(lambda e, pa=pa, fa=fa, b0=b0: e.matmul(pa[:, b0:b0 + 96], lhsT=G_s[:, fa, 0, :],
                                                                           rhs=A_sb[:, fa, :], start=True, stop=False))
                        fns.append(lambda e, pa=pa, fa=fa, b0=b0: e.matmul(pa[:, b0:b0 + 96], lhsT=G_s[:, fa, 1, :],
                                                                           rhs=A_sb[:, 66 + fa, :], start=False, stop=True))
                        fns.append(lambda e, pa=pa, fa=fa, b0=b0: e.matmul(pa[:, b0 + 96:b0 + 192], lhsT=G_s[:, fa, 1, :],
                                                                           rhs=A_sb[:, fa, :], start=True, stop=False))
                        fns.append(lambda e, pa=pa, fa=fa, b0=b0: e.matmul(pa[:, b0 + 96:b0 + 192], lhsT=G_s[:, fa, 0, :],
                                                                           rhs=A_sb[:, 33 + fa, :], start=False, stop=True))
                    P.op('tensor', fns, reads=B_A + [B_tab], writes=[Ba])
                    z = rot('zt', 2)
                    P.op('scalar', lambda e, z=z, pa=pa, nf=nf: e.copy(out=zt[z][:, 0:nf * 192], in_=pa[:, 0:nf * 192]),
                         reads=[Ba], writes=[B_zt[z]])
                    X = zt[z][:, 0:nf * 192].rearrange("p (f r s c) -> p f r s c", r=2, s=3, c=32)
                    cv = rot('cv', 2)
                    T = cva[cv][:, 0:384].rearrange("p (k f c) -> p k f c", k=6, c=32)[:, :, 0:nf, :]
                    U = cvb[cv][:, 0:128].rearrange("p (k f c) -> p k f c", k=2, c=32)[:, :, 0:nf, :]
                    yav = Ya[:, :, fa0:fa0 + nf].rearrange("p c f -> p f c")
                    yai = Ya[:, :, 64 + fa0:64 + fa0 + nf].rearrange("p c f -> p f c")
                    ybv = Yb[:, :, fa0:fa0 + nf].rearrange("p c f -> p f c")
                    ybi = Yb[:, :, 64 + fa0:64 + fa0 + nf].rearrange("p c f -> p f c")
                    P.op('vector', [
                        lambda e, X=X, U=U: e.tensor_tensor(out=U[:, 0], in0=X[:, :, 0, 1, :], in1=X[:, :, 0, 2, :], op=ALU.add),
                        lambda e, X=X, U=U: e.tensor_tensor(out=U[:, 1], in0=X[:, :, 1, 1, :], in1=X[:, :, 1, 2, :],
                                                           op=ALU.subtract)],
                         reads=[B_zt[z]], writes=[B_cvb[cv]])
                    T2 = of32[cv][:, 0:128].rearrange("p (k f c) -> p k f c", k=2, c=32)[:, :, 0:nf, :]
                    P.op('vector', [
                        lambda e, X=X, U=U, T=T: e.tensor_tensor(out=T[:, 0], in0=X[:, :, 0, 0, :], in1=U[:, 0], op=ALU.mult),
                        lambda e, X=X, U=U, T=T: e.tensor_tensor(out=T[:, 1], in0=X[:, :, 1, 0, :], in1=U[:, 1], op=ALU.mult)],
                         reads=[B_zt[z], B_cvb[cv]], writes=[B_cva[cv]])
                    P.op('gpsimd', [
                        lambda e, X=X, U=U, T2=T2: e.tensor_tensor(out=T2[:, 0], in0=X[:, :, 0, 0, :], in1=U[:, 1], op=ALU.mult),
                        lambda e, X=X, U=U, T2=T2: e.tensor_tensor(out=T2[:, 1], in0=X[:, :, 1, 0, :], in1=U[:, 0], op=ALU.mult)],
                         reads=[B_zt[z], B_cvb[cv]], writes=[B_of32[cv]])
                    P.op('vector', [
                        lambda e, T=T, yav=yav: e.tensor_tensor(out=yav, in0=T[:, 0], in1=T[:, 1], op=ALU.subtract),
                        lambda e, T=T, ybi=ybi: e.tensor_tensor(out=ybi, in0=T[:, 0], in1=T[:, 1], op=ALU.subtract)],
                         reads=[B_cva[cv]], writes=[B_Y[0]])
                    P.op('gpsimd', [
                        lambda e, T2=T2, yai=yai: e.tensor_tensor(out=yai, in0=T2[:, 0], in1=T2[:, 1], op=ALU.add),
                        lambda e, T2=T2, ybv=ybv: e.tensor_tensor(out=ybv, in0=T2[:, 0], in1=T2[:, 1], op=ALU.add)],
                         reads=[B_of32[cv]], writes=[B_Y[1]])
                    P.op('gpsimd', lambda e, ybv=ybv: e.tensor_scalar(out=ybv, in0=ybv, scalar1=-1.0, scalar2=0.0,
                                                                       op0=ALU.mult, op1=ALU.add),
                         reads=[B_Y[1]], writes=[B_Y[1]])
                for cb in range(8):
                    pa, Ba = acc()
                    fns = []
                    for k in range(4):
                        c = cb * 4 + k
                        fns.append(lambda e, pa=pa, c=c, k=k: e.matmul(pa[:, k * 128:(k + 1) * 128], lhsT=Ya[:, c, :],
                                                                       rhs=GI_s[:, 0, :], start=True, stop=False))
                        fns.append(lambda e, pa=pa, c=c, k=k: e.matmul(pa[:, k * 128:(k + 1) * 128], lhsT=Yb[:, c, :],
                                                                       rhs=GI_s[:, 1, :], start=False, stop=True))
                    P.op('tensor', fns, reads=B_Y + [B_tab], writes=[Ba])
                    if cb % 2 == 0:
                        P.op('scalar', lambda e, pa=pa, cb=cb: e.copy(
                            out=D_sb[:, :, cb * 4:cb * 4 + 4].rearrange("p t k -> p k t"),
                            in_=pa[:, :].rearrange("p (k t) -> p k t", t=128)), reads=[Ba], writes=[B_D[0]])
                    else:
                        P.op('vector', lambda e, pa=pa, cb=cb: e.tensor_copy(
                            out=D_sb[:, :, cb * 4:cb * 4 + 4].rearrange("p t k -> p k t"),
                            in_=pa[:, :].rearrange("p (k t) -> p k t", t=128)), reads=[Ba], writes=[B_D[1]])
                for tb in range(8):
                    pa, Ba = acc()
                    fns = [lambda e, pa=pa, tb=tb, tl=tl: e.matmul(
                        pa[0:32, tl * 32:(tl + 1) * 32], lhsT=MI_s[:, tb * 16 + tl, :], rhs=D_sb[:, tb * 16 + tl, :],
                        start=True, stop=True) for tl in range(16)]
                    P.op('tensor', fns, reads=B_D + [B_tab], writes=[Ba])
                    if tb % 2 == 0:
                        P.op('vector', lambda e, pa=pa, tb=tb: e.tensor_copy(
                            out=y_sb[:, :, tb * 16:(tb + 1) * 16].rearrange("p c t -> p t c"),
                            in_=pa[0:32, :].rearrange("p (t c) -> p t c", c=32)), reads=[Ba], writes=[B_ysb[0]])
                    else:
                        P.op('scalar', lambda e, pa=pa, tb=tb: e.copy(
                            out=y_sb[:, :, tb * 16:(tb + 1) * 16].rearrange("p c t -> p t c"),
                            in_=pa[0:32, :].rearrange("p (t c) -> p t c", c=32)), reads=[Ba], writes=[B_ysb[1]])
                P.dma('sync', ycT[c0:c0 + 32, :].rearrange("c (th tl) -> th c tl", tl=128), y_sb, reads=B_ysb)

        def phase3():
            P.dma('sync', bkt_s[:], bkt, writes=[B_c3])
            P.dma('sync', rb_s[:], rbext, writes=[B_c3])
            P.dma('sync', esink[:], sinkr, writes=[B_c3])
            P.dma('sync', kmask_s[:], kmask, writes=[B_c3])
            P.op('scalar', lambda e: e.activation(out=esink[:], in_=esink[:], func=AF.Exp), reads=[B_c3], writes=[B_c3])
            for ri, r in enumerate((-1, 0, 1)):
                for hv in range(4):
                    fns = []
                    for qi in range(128):
                        st = 384 - qi - 128 * r
                        fns.append(lambda e, qi=qi, st=st, hv=hv: e.matmul(
                            psx[:, qi:512:128], lhsT=bkt_s[:, st:st + 128], rhs=rb_s[:, hv * 4:hv * 4 + 4],
                            start=True, stop=True))
                    P.op('tensor', fns, reads=[B_c3], writes=[B_psx])
                    P.op('scalar', lambda e, ri=ri, hv=hv: e.copy(out=biasT[:, ri * 4 + hv, :], in_=psx[:, :]),
                         reads=[B_psx], writes=[B_c3])
            q4 = big[:, 0:16384].rearrange("p (g t) -> p g t", g=4)
            kh = big[:, 16384:20480]
            vh = big[:, 20480:24576].rearrange("p (b d) -> p b d", b=32)
            for hv in range(4):
                P.dma('sync', q4, qT[hv * 512:(hv + 1) * 512, 0:LT].rearrange("(g p) t -> p g t", p=128),
                      reads=[B_qT], writes=[B_big])
                P.dma('sync', kh, kT[hv * 128:(hv + 1) * 128, 0:LT], reads=[B_kT], writes=[B_big])
                P.dma('sync', vh, vS[:, hv * 128:(hv + 1) * 128].rearrange("(b p) d -> p b d", p=128),
                      reads=[B_vS], writes=[B_big])
                for i in range(32):
                    js = [j for j in (i - 1, i, i + 1) if 0 <= j < 32]
                    par = i % 2
                    ps_o, B_o = (pss, B_pss) if par == 0 else (psx, B_psx)
                    ps_d, B_d = (pss2, B_pss2) if par == 0 else (psh, B_psh[0])
                    pts = []
                    for jn, j in enumerate(js):
                        ri = (i - j) + 1
                        pm = rot('pm', 3)
                        P.op('tensor', [
                            lambda e, j=j, i=i, pm=pm: e.matmul(psm[pm][:, :], lhsT=kh[:, j * 128:(j + 1) * 128],
                                                                 rhs=q4[:, :, i * 128:(i + 1) * 128], start=True, stop=False),
                            lambda e, ri=ri, hv=hv, pm=pm: e.matmul(psm[pm][:, :], lhsT=ident[:], rhs=biasT[:, ri * 4 + hv, :],
                                                                    start=False, stop=True)],
                             reads=[B_big, B_c3, B_const], writes=[B_psm[pm]])
                        o = rot('ob', NOB)
                        P.op('scalar', lambda e, o=o, pm=pm, j=j: e.activation(
                            out=obf[o][:], in_=psm[pm][:, :], func=AF.Exp, bias=kmask_s[:, j:j + 1], scale=1.0),
                            reads=[B_psm[pm], B_c3], writes=[B_obf[o]])
                        pts.append((o, j))
                    for jn, (o, j) in enumerate(pts):
                        P.op('tensor', [
                            lambda e, o=o, j=j, jn=jn, ps_o=ps_o: e.matmul(ps_o[:, :], lhsT=vh[:, j, :], rhs=obf[o][:],
                                                                           start=(jn == 0), stop=(jn == len(js) - 1)),
                            lambda e, o=o, jn=jn, ps_d=ps_d: e.matmul(ps_d[:, :], lhsT=ones_b[:], rhs=obf[o][:],
                                                                      start=(jn == 0), stop=(jn == len(js) - 1))],
                             reads=[B_obf[o], B_big, B_const], writes=[B_o, B_d])
                    cv = rot('cv', 2)
                    P.op('vector', [lambda e, g=g, cv=cv, ps_d=ps_d, hv=hv: e.tensor_scalar(
                        out=cva[cv][:, g * 128:(g + 1) * 128], in0=ps_d[:, g * 128:(g + 1) * 128],
                        scalar1=esink[:, hv * 4 + g:hv * 4 + g + 1], scalar2=None, op0=ALU.add) for g in range(4)],
                         reads=[B_d, B_c3], writes=[B_cva[cv]])
                    P.op('vector', lambda e, cv=cv: e.reciprocal(out=cvb[cv][:], in_=cva[cv][:]),
                         reads=[B_cva[cv]], writes=[B_cvb[cv]])
                    ao = rot('ao', 2)
                    P.op('vector', lambda e, cv=cv, ao=ao, ps_o=ps_o: e.tensor_tensor(
                        out=att_o[ao][:], in0=ps_o[:, :], in1=cvb[cv][:], op=ALU.mult),
                        reads=[B_o, B_cvb[cv]], writes=[B_att_o[ao]])
                    P.dma('sync', attnT[hv * 512:(hv + 1) * 512, i * 128:(i + 1) * 128].rearrange("(g d) q -> d g q", d=128),
                          att_o[ao][:].rearrange("d (g q) -> d g q", g=4), reads=[B_att_o[ao]], writes=[B_attnT])

        def phase4(use_yc):
            at_s = big[:, 0:8192].rearrange("p (c t) -> p c t", c=16)
            hy_s = big[:, 8192:16384].rearrange("p (c t) -> p c t", c=16)
            m_s = big[:, 16384:32768].rearrange("p (c t) -> p c t", c=32)
            B_at, B_hy, B_m = Buf("at"), Buf("hy"), Buf("m")
            for half in range(2):
                P.dma('sync', x1T[:, half * (LT + 1):half * (LT + 1) + 1].rearrange("(kc p) o -> p kc o", p=128),
                      zero_c[:].rearrange("p (k o) -> p k o", o=1), reads=[B_const], writes=[B_x1T], allow_slow_non_contiguous=True)
            for ti in range(NT):
                s = ti * 512
                prep(xT, s, 0)
                P.dma('sync', vmask[:], valid[:, s:s + TW], writes=[B_vmask])
                P.dma('sync', at_s, attnT[:, s:s + 512].rearrange("(c p) t -> p c t", p=128), reads=[B_attnT], writes=[B_at])
                for c in range(16):
                    z = rot('zt', 2)
                    cv = rot('cv', 2)
                    P.dma('sync', cva[cv][:], uT[c * 128:(c + 1) * 128, s:s + 512], reads=[B_uT], writes=[B_cva[cv]])
                    P.dma('sync', cvb[cv][:], hx0T[c * 128:(c + 1) * 128, s:s + 512], reads=[B_hx0T], writes=[B_cvb[cv]])
                    if use_yc:
                        P.dma('sync', zt[z][:, 0:512], ycT[c * 128:(c + 1) * 128, s:s + 512], reads=[B_ycT], writes=[B_zt[z]])
                        P.op('vector', lambda e, z=z, cv=cv, c=c: e.scalar_tensor_tensor(
                            out=zt[z][:, 0:512], in0=cva[cv][:], scalar=hskip_s[:, c:c + 1], in1=zt[z][:, 0:512],
                            op0=ALU.mult, op1=ALU.add), reads=[B_cva[cv], B_zt[z], B_const], writes=[B_zt[z]])
                    else:
                        P.op('vector', lambda e, z=z, cv=cv, c=c: e.tensor_scalar(
                            out=zt[z][:, 0:512], in0=cva[cv][:], scalar1=hskip_s[:, c:c + 1], scalar2=None,
                            op0=ALU.mult), reads=[B_cva[cv], B_const], writes=[B_zt[z]])
                    P.op('vector', lambda e, z=z, cv=cv, c=c: e.tensor_tensor(
                        out=hy_s[:, c, :], in0=zt[z][:, 0:512], in1=cvb[cv][:], op=ALU.mult),
                        reads=[B_zt[z], B_cvb[cv]], writes=[B_hy])
                for j in range(32):
                    c0 = j * 128
                    s_a = wload(w_ao[:, c0:c0 + 128], 16, ('ao', j))
                    s_h = wload(w_ho[:, c0:c0 + 128], 16, ('ho', j))
                    s_ga = wload(w_in[:, 9216 + c0:9216 + c0 + 128], 32, ('ga', j))
                    s_gh = wload(w_in[:, 13312 + c0:13312 + c0 + 128], 32, ('gh', j))
                    pa, Ba = acc()
                    mm_group(pa[:, :], Ba, [(s_a, 16, 0)], lambda kc: at_s[:, kc, :], [B_at])
                    ph_, Bh = acc()
                    mm_group(ph_[:, :], Bh, [(s_h, 16, 0)], lambda kc: hy_s[:, kc, :], [B_hy])
                    pga, Bga = acc()
                    mm_group(pga[:, :], Bga, [(s_ga, 32, 0)], lambda kc: hg[:, kc, 1:513], [B_hg])
                    pgh, Bgh = acc()
                    mm_group(pgh[:, :], Bgh, [(s_gh, 32, 0)], lambda kc: hg[:, kc, 1:513], [B_hg])
                    res = []
                    for (pg, Bg, pp, Bp) in ((pga, Bga, pa, Ba), (pgh, Bgh, ph_, Bh)):
                        z = rot('zt', 2)
                        P.op('vector', lambda e, z=z, pg=pg: e.tensor_tensor(
                            out=zt[z][:, 0:512], in0=pg[:, :], in1=rstd[:, 1:513], op=ALU.mult),
                            reads=[Bg, B_rstd], writes=[B_zt[z]])
                        P.op('scalar', lambda e, z=z: e.activation(out=zt[z][:, 0:512], in_=zt[z][:, 0:512], func=AF.Sigmoid),
                             reads=[B_zt[z]], writes=[B_zt[z]])
                        cv = rot('cv', 2)
                        P.op('vector', lambda e, z=z, cv=cv, pp=pp: e.tensor_tensor(
                            out=cva[cv][:], in0=zt[z][:, 0:512], in1=pp[:, :], op=ALU.mult),
                            reads=[B_zt[z], Bp], writes=[B_cva[cv]])
                        res.append(cv)
                    P.op('vector', lambda e, j=j, a=res[0], b=res[1]: e.tensor_tensor(
                        out=m_s[:, j, :], in0=cva[a][:], in1=cva[b][:], op=ALU.add),
                        reads=[B_cva[res[0]], B_cva[res[1]]], writes=[B_m])
                for j in range(32):
                    c0 = j * 128
                    s_o = wload(w_out[:, c0:c0 + 128], 32, ('wo', j))
                    po, Bo = acc()
                    mm_group(po[:, :], Bo, [(s_o, 32, 0)], lambda kc: m_s[:, kc, :], [B_m])
                    cv = rot('cv', 2)
                    P.dma('sync', cvb[cv][:], xT[c0:c0 + 128, 1 + s:1 + s + 512], writes=[B_cvb[cv]])
                    P.op('vector', lambda e, cv=cv, po=po: e.tensor_tensor(
                        out=cva[cv][:], in0=po[:, :], in1=cvb[cv][:], op=ALU.add),
                        reads=[Bo, B_cvb[cv]], writes=[B_cva[cv]])
                    o = rot('of', NOB)
                    P.op('vector', lambda e, cv=cv, o=o: e.tensor_tensor(
                        out=of32[o][:], in0=cva[cv][:], in1=vmask[:, 1:513], op=ALU.mult),
                        reads=[B_cva[cv], B_vmask], writes=[B_of32[o]])
                    P.dma('sync', x1T[c0:c0 + 128, 1 + s:1 + s + 512], of32[o][:], reads=[B_of32[o]], writes=[B_x1T])

        def phase5():
            act_s = big[:, :].rearrange("p (c t) -> p c t", c=NFF)
            B_act = Buf("act")
            for ti in range(NT):
                s = ti * 512
                prep(x1T, s, 1)
                for i in range(NFF):
                    c0 = i * 128
                    s_g = wload(w_up[:, c0:c0 + 128], 32, ('ug', i))
                    s_v = wload(w_up[:, DFF + c0:DFF + c0 + 128], 32, ('uv', i))
                    pg, Bg = acc()
                    mm_group(pg[:, :], Bg, [(s_g, 32, 0)], lambda kc: hg[:, kc, 1:513], [B_hg])
                    ph = rot('ph', 8)
                    mm_group(psh[:, ph * 4:ph * 4 + 2], B_psh[ph], [(s_g, 32, 0)], lambda kc: hg[:, kc, 0:TW:513], [B_hg])
                    pv, Bv = acc()
                    mm_group(pv[:, :], Bv, [(s_v, 32, 0)], lambda kc: hg[:, kc, 1:513], [B_hg])
                    z = rot('zt', 2)
                    P.op('vector', [
                        lambda e, z=z, pg=pg: e.tensor_tensor(out=zt[z][:, 1:513], in0=pg[:, :], in1=rstd[:, 1:513], op=ALU.mult),
                        lambda e, z=z, ph=ph: e.tensor_tensor(out=zt[z][:, 0:TW:513], in0=psh[:, ph * 4:ph * 4 + 2],
                                                             in1=rstd[:, 0:TW:513], op=ALU.mult)],
                         reads=[Bg, B_psh[ph], B_rstd], writes=[B_zt[z]])
                    cv = rot('cv', 2)
                    P.op('vector', lambda e, z=z, cv=cv, i=i: e.tensor_scalar(
                        out=cva[cv][:], in0=zt[z][:, 0:512], scalar1=fcw_s[:, i, 0:1], scalar2=fcw_s[:, i, 3:4],
                        op0=ALU.mult, op1=ALU.add), reads=[B_zt[z], B_const], writes=[B_cva[cv]])
                    P.op('vector', lambda e, z=z, cv=cv, i=i: e.scalar_tensor_tensor(
                        out=cvb[cv][:], in0=zt[z][:, 1:513], scalar=fcw_s[:, i, 1:2], in1=cva[cv][:], op0=ALU.mult,
                        op1=ALU.add), reads=[B_zt[z], B_cva[cv], B_const], writes=[B_cvb[cv]])
                    P.op('vector', lambda e, z=z, cv=cv, i=i: e.scalar_tensor_tensor(
                        out=cva[cv][:], in0=zt[z][:, 2:514], scalar=fcw_s[:, i, 2:3], in1=cvb[cv][:], op0=ALU.mult,
                        op1=ALU.add), reads=[B_zt[z], B_cvb[cv], B_const], writes=[B_cva[cv]])
                    P.op('scalar', lambda e, cv=cv: e.activation(out=cvb[cv][:], in_=cva[cv][:], func=AF.Gelu),
                         reads=[B_cva[cv]], writes=[B_cvb[cv]])
                    o = rot('of', NOB)
                    P.op('vector', lambda e, o=o, pv=pv: e.tensor_tensor(
                        out=of32[o][:], in0=pv[:, :], in1=rstd[:, 1:513], op=ALU.mult),
                        reads=[Bv, B_rstd], writes=[B_of32[o]])
                    P.op('vector', lambda e, o=o, cv=cv, i=i: e.tensor_tensor(
                        out=act_s[:, i, :], in0=of32[o][:], in1=cvb[cv][:], op=ALU.mult),
                        reads=[B_of32[o], B_cvb[cv]], writes=[B_act])
                for j in range(32):
                    c0 = j * 128
                    pcs = [(wload(w_down[kb * 128:(kb + n) * 128, c0:c0 + 128], n, ('dn', j, kb)), n, kb)
                           for (kb, n) in ((0, 32), (32, 32), (64, 22))]
                    pd, Bd = acc()
                    mm_group(pd[:, :], Bd, pcs, lambda kc: act_s[:, kc, :], [B_act])
                    cv = rot('cv', 2)
                    P.dma('sync', cvb[cv][:], x1T[c0:c0 + 128, 1 + s:1 + s + 512], reads=[B_x1T], writes=[B_cvb[cv]])
                    o = rot('of', NOB)
                    P.op('vector', lambda e, cv=cv, pd=pd, o=o: e.tensor_tensor(
                        out=of32[o][:], in0=pd[:, :], in1=cvb[cv][:], op=ALU.add),
                        reads=[Bd, B_cvb[cv]], writes=[B_of32[o]])
                    P.dma('sync', x2T[c0:c0 + 128, 1 + s:1 + s + 512], of32[o][:], reads=[B_of32[o]], writes=[B_x2T])

        def phase6():
            x3_s = big[:, 0:32768].bitcast(F32).rearrange("p (c t) -> p c t", c=32)
            B_x3 = Buf("x3")
            for ti in range(NT):
                s = ti * 512
                prep(x2T, s, 2)
                pend_ss = []

                def emit_ss():
                    o_, j_ = pend_ss.pop(0)
                    P.op('tensor', lambda e, o_=o_, j_=j_: e.matmul(pss[:, :], lhsT=ones_b[:], rhs=obf[o_][:],
                                                                    start=(j_ == 0), stop=(j_ == 31)),
                         reads=[B_obf[o_], B_const], writes=[B_pss])
                for kc in range(2):
                    cv = rot('cv', 2)
                    P.dma('sync', cva[cv][:], pT[kc * 128:(kc + 1) * 128, s:s + 512], writes=[B_cva[cv]])
                    P.op('scalar', lambda e, cv=cv, kc=kc: e.copy(out=pb_s[:, kc, :], in_=cva[cv][:]),
                         reads=[B_cva[cv]], writes=[B_pb])
                for j in range(32):
                    c0 = j * 128
                    s_g = wload(w_pg[:, c0:c0 + 128], 32, ('pg', j))
                    s_p = wload(w_ple[:, c0:c0 + 128], 2, ('pl', j))
                    pg, Bg = acc()
                    mm_group(pg[:, :], Bg, [(s_g, 32, 0)], lambda kc: hg[:, kc, 1:513], [B_hg])
                    pp, Bp = acc()
                    mm_group(pp[:, :], Bp, [(s_p, 2, 0)], lambda kc: pb_s[:, kc, :], [B_pb])
                    if len(pend_ss) >= 2:
                        emit_ss()
                    z = rot('zt', 2)
                    P.op('vector', lambda e, z=z, pg=pg: e.tensor_tensor(
                        out=zt[z][:, 0:512], in0=pg[:, :], in1=rstd[:, 1:513], op=ALU.mult),
                        reads=[Bg, B_rstd], writes=[B_zt[z]])
                    P.op('scalar', lambda e, z=z: e.activation(out=zt[z][:, 0:512], in_=zt[z][:, 0:512], func=AF.Sigmoid),
                         reads=[B_zt[z]], writes=[B_zt[z]])
                    cv = rot('cv', 2)
                    P.op('vector', lambda e, z=z, cv=cv, pp=pp: e.tensor_tensor(
                        out=cva[cv][:], in0=zt[z][:, 0:512], in1=pp[:, :], op=ALU.mult),
                        reads=[B_zt[z], Bp], writes=[B_cva[cv]])
                    P.dma('sync', cvb[cv][:], x2T[c0:c0 + 128, 1 + s:1 + s + 512], reads=[B_x2T], writes=[B_cvb[cv]])
                    P.op('vector', lambda e, cv=cv, j=j: e.tensor_tensor(
                        out=x3_s[:, j, :], in0=cva[cv][:], in1=cvb[cv][:], op=ALU.add),
                        reads=[B_cva[cv], B_cvb[cv]], writes=[B_x3])
                    o = rot('ob', NOB)
                    P.op('scalar', lambda e, o=o, j=j: e.activation(out=obf[o][:], in_=x3_s[:, j, :], func=AF.Square),
                         reads=[B_x3], writes=[B_obf[o]])
                    pend_ss.append((o, j))
                while pend_ss:
                    emit_ss()
                P.op('scalar', lambda e: e.activation(out=rt[:, 1:513], in_=pss[:, :], func=AF.Sqrt, bias=eps_t[:, 0:1],
                                                      scale=1.0 / D), reads=[B_pss, B_const], writes=[B_rt])
                P.op('vector', lambda e: e.reciprocal(out=rstd[:, 1:513], in_=rt[:, 1:513]), reads=[B_rt], writes=[B_rstd])
                for j in range(32):
                    o = rot('of', NOB)
                    P.op('vector', lambda e, o=o, j=j: e.scalar_tensor_tensor(
                        out=of32[o][:], in0=x3_s[:, j, :], scalar=gains_s[:, 3, j:j + 1], in1=rstd[:, 1:513],
                        op0=ALU.mult, op1=ALU.mult), reads=[B_x3, B_rstd, B_const], writes=[B_of32[o]])
                    P.dma('sync', yT[j * 128:(j + 1) * 128, s:s + 512], of32[o][:], reads=[B_of32[o]])

        for ph_name in ('p1', 'p2', 'p3', 'p4', 'p5', 'p6'):
            if ph_name not in phases:
                continue
            if ph_name == 'p1':
                phase1()
            elif ph_name == 'p2':
                phase2()
            elif ph_name == 'p3':
                phase3()
            elif ph_name == 'p4':
                set_accs(base_accs + [(psh, B_psh[0]), (pss, B_pss), (pss2, B_pss2)])
                set_wb(True)
                phase4('p2' in phases)
            elif ph_name == 'p5':
                set_accs(base_accs + [(pss, B_pss), (pss2, B_pss2)])
                set_wb(False)
                phase5()
            elif ph_name == 'p6':
                set_accs(base_accs + [(psh, B_psh[0]), (pss2, B_pss2)])
                set_wb(True)
                phase6()
            P.barrier()

        P.finish()

        @block.sync
        def _(e):
            for f in P.ops['sync']:
                f(e)

        @block.scalar
        def _(e):
            for f in P.ops['scalar']:
                f(e)

        @block.vector
        def _(e):
            for f in P.ops['vector']:
                f(e)

        @block.gpsimd
        def _(e):
            for f in P.ops['gpsimd']:
                f(e)

        @block.tensor
        def _(e):
            for f in P.ops['tensor']:
                f(e)
    return nc


def _t5_bucket(rel):
    half = 16
    max_exact = 8
    ret = np.where(rel > 0, half, 0)
    n = np.abs(rel)
    nf = np.maximum(n, 1).astype(np.float32)
    large = max_exact + (np.log(nf / max_exact) / math.log(128 / max_exact) * (half - max_exact)).astype(np.int32)
    large = np.minimum(large, half - 1)
    return ret + np.where(n < max_exact, n, large)


def _chunked(v, n):
    return np.ascontiguousarray(np.asarray(v, np.float32).reshape(n, 128).T)


def make_inputs(inp, core):
    f32 = np.float32
    if core < 4:
        x = np.asarray(inp['x_prompt'][core]); p = np.asarray(inp['p_prompt'][0, core]); L = 4096
    else:
        x = np.asarray(inp['x_sample'][core - 4]); p = np.asarray(inp['p_sample'][0, core - 4]); L = 2048
    xTp = np.zeros((D, LT + 2), f32)
    xTp[:, 1:L + 1] = x.T
    pTp = np.zeros((256, LT), f32)
    pTp[:, :L] = p.T
    valid = np.zeros((128, LT + 2), f32)
    valid[:, 1:L + 1] = 1.0
    kmask = np.zeros((128, 32), f32)
    tok = np.arange(32)[None, :] * 128 + np.arange(128)[:, None]
    kmask[tok >= L] = NEG
    pos = np.arange(LT, dtype=f32)
    t = (pos / f32(L - 1)).astype(f32)
    bands = np.linspace(1e-4, 15, 16, dtype=f32)
    ang = (f32(2.0 * math.pi / L) * pos[:, None] * bands[None, :]).astype(f32)
    zf = np.concatenate([t[:, None], np.cos(ang), -np.sin(ang)], axis=-1).astype(f32)
    m = {'xT': xTp, 'pT': pTp, 'valid': valid, 'kmask': kmask,
         'zfT': np.ascontiguousarray(zf.T), 'tnb': np.ascontiguousarray(np.broadcast_to(t[None, :], (128, LT)))}
    return m


def shared_inputs(inp):
    f32 = np.float32
    m = {}
    m['w_in'] = np.asarray(inp['w_in'][0]); m['w_attn_o'] = np.asarray(inp['w_attn_o'][0])
    m['w_hyena_o'] = np.asarray(inp['w_hyena_o'][0]); m['w_out'] = np.asarray(inp['w_out'][0])
    m['w_up'] = np.asarray(inp['w_up'][0]); m['w_down'] = np.asarray(inp['w_down'][0])
    m['w_ple_gate'] = np.asarray(inp['w_ple_gate'][0]); m['w_ple'] = np.asarray(inp['w_ple'][0])
    g = np.stack([_chunked(inp['g_mix'][0], 32), _chunked(inp['g_ffn'][0], 32), _chunked(inp['g_ple'][0], 32),
                  _chunked(inp['g_final'], 32)], axis=1)
    m['gains'] = np.ascontiguousarray(g)
    hw = np.asarray(inp['hy_short_w'][0]); hb = np.asarray(inp['hy_short_b'][0])
    m['hcw'] = np.ascontiguousarray(np.stack([_chunked(hw[0], 48), _chunked(hw[1], 48), _chunked(hw[2], 48),
                                              _chunked(hb, 48)], axis=2))
    fw = np.asarray(inp['ffn_conv_w'][0]); fb = np.asarray(inp['ffn_conv_b'][0])
    m['fcw'] = np.ascontiguousarray(np.stack([_chunked(fw[0], NFF), _chunked(fw[1], NFF), _chunked(fw[2], NFF),
                                              _chunked(fb, NFF)], axis=2))
    m['sinkr'] = np.ascontiguousarray(np.broadcast_to(np.asarray(inp['attn_sink'][0], f32)[None, :], (128, 16)))
    m['rbext'] = np.concatenate([np.asarray(inp['rel_bias'], f32), np.full((1, 16), NEG, f32)], axis=0)
    d = np.arange(768) - 384
    bk = _t5_bucket(d)
    T = np.zeros((33, 768), f32)
    T[bk, np.arange(768)] = 1.0
    T[32, :] = (np.abs(d) > 128).astype(f32)
    m['bkt'] = T
    m['identb'] = np.eye(128, dtype=f32).astype(ml_dtypes.bfloat16)
    m['hskip'] = _chunked(inp['hy_skip'][0], 16)
    m['fw1'] = np.asarray(inp['hy_filt_w1'][0], f32); m['fw2'] = np.asarray(inp['hy_filt_w2'][0], f32)
    m['fw3'] = np.asarray(inp['hy_filt_w3'][0], f32)
    m['fpar'] = np.ascontiguousarray(np.stack([inp['hy_filt_b1'][0], inp['hy_filt_f1'][0], inp['hy_filt_b2'][0],
                                               inp['hy_filt_f2'][0]], axis=1).astype(f32))
    m['decay'] = _chunked(np.asarray(inp['hy_decay'][0]).reshape(-1), 32)
    m.update(_fft_tables())
    return m


def _fft_tables():
    bf = ml_dtypes.bfloat16
    N = 8192
    th = np.arange(32)[:, None]; fa = np.arange(33)[None, :]
    a = 2 * np.pi * th * fa / 64.0
    fh = np.concatenate([np.cos(a), -np.sin(a), np.sin(a)], axis=1)
    tl = np.arange(128)[:, None, None]; fa3 = np.arange(33)[None, :, None]; fb = np.arange(128)[None, None, :]
    ph = 2 * np.pi * ((tl * (fa3 + 64 * fb)) % N) / N
    g = np.stack([np.cos(ph), -np.sin(ph)], axis=2)
    fbp = np.arange(128)[:, None]; tlo = np.arange(128)[None, :]
    p2 = 2 * np.pi * ((fbp * tlo) % 128) / 128.0
    gi = np.stack([np.cos(p2), np.sin(p2)], axis=1)
    fa_ = np.arange(64)[:, None, None]; tl_ = np.arange(128)[None, :, None]; th_ = np.arange(32)[None, None, :]
    th3 = 2 * np.pi * ((fa_ * (128 * th_ + tl_)) % N) / N
    w = np.zeros(64); w[0] = 1; w[32] = 1; w[1:32] = 2
    mi = np.concatenate([w[:, None, None] * np.cos(th3) / N, -w[:, None, None] * np.sin(th3) / N], axis=0)
    return {'fh_t': fh.astype(np.float32).astype(bf), 'g_t': g.reshape(128, -1).astype(np.float32).astype(bf),
            'gi_t': gi.reshape(128, -1).astype(np.float32).astype(bf), 'mi_t': mi.reshape(128, -1).astype(np.float32).astype(bf)}


_NC_CACHE = {}


def kernel(**inputs):
    key = 'full'
    if key not in _NC_CACHE:
        _NC_CACHE[key] = build()
    nc = _NC_CACHE[key]
    sh = shared_inputs(inputs)
    in_maps = []
    for c in range(8):
        m = dict(sh)
        m.update(make_inputs(inputs, c))
        in_maps.append(m)
    names = set()
    for alloc in nc.allocations:
        if isinstance(alloc, mybir.MemoryLocationSet) and alloc.kind == "ExternalInput":
            names.add(alloc.memorylocations[0].name)
    in_maps = [{k: v for k, v in m.items() if k in names} for m in in_maps]
    res = run_bass_kernel_spmd(nc, in_maps, core_ids=list(range(8)))
    yp = np.stack([res.results[c]['yT'].T for c in range(4)], axis=0).astype(np.float32)
    ys = np.stack([res.results[c]['yT'][:, :2048].T for c in range(4, 8)], axis=0).astype(np.float32)
    return (np.ascontiguousarray(yp), np.ascontiguousarray(ys))
```
